# Optimizing a Trainium2 kernel written in Bass

```python
import math
import jax, jax.numpy as jnp
from jax import lax
import numpy as np

D_MODEL = 1024
BATCH = 4
SEQ = 8192
DEPTH = 2

D_MIX = D_MODEL
D_ATTN = D_MIX // 2
D_SSM = D_MIX - D_ATTN
HEAD_DIM = 64
N_ATTN_HEADS = D_ATTN // (2 * HEAD_DIM)
SSM_GROUP = 16
N_SSM_GROUPS = D_SSM // SSM_GROUP
SSM_STATE = 64
D_IN = 3 * D_ATTN + D_SSM
D_FF = ((8 * D_MODEL // 3 + 255) // 256) * 256
CONV_W = 3
ROPE_THETA = 10000.0
Q_BLOCK = 128
EPS = 1e-6
EIG_CLIP = -1e-4

kernel_name = "hybrid_diffattn_s5_convffn_adaln"


def _rmsnorm(x, g):
    xf = x.astype(jnp.float32)
    y = xf * lax.rsqrt(jnp.mean(xf * xf, axis=-1, keepdims=True) + EPS)
    return (y * g.astype(jnp.float32)).astype(x.dtype)


def _rope_tables(positions):
    inv_freq = ROPE_THETA ** (-jnp.arange(0, HEAD_DIM, 2, dtype=jnp.float32) / HEAD_DIM)
    ang = positions.astype(jnp.float32)[..., None] * inv_freq
    ang = jnp.concatenate([ang, ang], axis=-1)
    return jnp.cos(ang), jnp.sin(ang)


def _apply_rope(x, cos, sin):
    half = HEAD_DIM // 2
    rot = jnp.concatenate([-x[..., half:], x[..., :half]], axis=-1)
    cos = cos[:, :, None, None, :]
    sin = sin[:, :, None, None, :]
    return (x.astype(jnp.float32) * cos + rot.astype(jnp.float32) * sin).astype(x.dtype)


def _diff_attention(q, k, v, cos, sin, q_g, k_g, lq1, lk1, lq2, lk2, sub_g, lam_init):
    b, s, _ = q.shape
    q = q.reshape(b, s, N_ATTN_HEADS, 2, HEAD_DIM)
    k = k.reshape(b, s, N_ATTN_HEADS, 2, HEAD_DIM)
    v = v.reshape(b, s, N_ATTN_HEADS, 2 * HEAD_DIM)
    q = _apply_rope(_rmsnorm(q, q_g), cos, sin)
    k = _apply_rope(_rmsnorm(k, k_g), cos, sin)
    f32 = jnp.float32
    lam = (jnp.exp(jnp.sum(lq1.astype(f32) * lk1.astype(f32)))
           - jnp.exp(jnp.sum(lq2.astype(f32) * lk2.astype(f32))) + lam_init)
    n_blk = s // Q_BLOCK
    q_blocks = q.reshape(b, n_blk, Q_BLOCK, N_ATTN_HEADS, 2, HEAD_DIM).transpose(1, 0, 2, 3, 4, 5)
    scale = HEAD_DIM ** -0.5

    def one_block(qb):
        sc = jnp.einsum('bqhmd,bkhmd->bhmqk', qb, k, preferred_element_type=f32) * scale
        p = jax.nn.softmax(sc, axis=-1)
        w = p[:, :, 0] - lam * p[:, :, 1]
        return jnp.einsum('bhqk,bkhe->bqhe', w.astype(v.dtype), v)

    o = lax.map(one_block, q_blocks)
    o = o.transpose(1, 0, 2, 3, 4).reshape(b, s, N_ATTN_HEADS, 2 * HEAD_DIM)
    o = _rmsnorm(o, sub_g) * (1.0 - lam_init)
    return o.reshape(b, s, D_ATTN)


def _complex_linear_combine(left, right):
    a1r, a1i, b1r, b1i = left
    a2r, a2i, b2r, b2i = right
    return (a2r * a1r - a2i * a1i,
            a2r * a1i + a2i * a1r,
            a2r * b1r - a2i * b1i + b2r,
            a2r * b1i + a2i * b1r + b2i)


def _s5_direction(u, a_re, a_im, log_dt, b_re, b_im, c_re, c_im, reverse):
    f32 = jnp.float32
    lam_re = jnp.minimum(a_re.astype(f32), EIG_CLIP)
    lam_im = a_im.astype(f32)
    dt = jnp.exp(log_dt.astype(f32))[:, None]
    mag = jnp.exp(lam_re * dt)
    ab_re = mag * jnp.cos(lam_im * dt)
    ab_im = mag * jnp.sin(lam_im * dt)
    den = lam_re * lam_re + lam_im * lam_im
    num_re = ab_re - 1.0
    num_im = ab_im
    f_re = (num_re * lam_re + num_im * lam_im) / den
    f_im = (num_im * lam_re - num_re * lam_im) / den
    br = b_re.astype(f32)
    bi = b_im.astype(f32)
    bb_re = f_re[..., None] * br - f_im[..., None] * bi
    bb_im = f_re[..., None] * bi + f_im[..., None] * br
    bu_re = jnp.einsum('bsgp,gnp->bsgn', u, bb_re)
    bu_im = jnp.einsum('bsgp,gnp->bsgn', u, bb_im)
    s = u.shape[1]
    a_seq_re = jnp.broadcast_to(ab_re, (1, s) + ab_re.shape)
    a_seq_im = jnp.broadcast_to(ab_im, (1, s) + ab_im.shape)
    _, _, h_re, h_im = lax.associative_scan(
        _complex_linear_combine, (a_seq_re, a_seq_im, bu_re, bu_im), reverse=reverse, axis=1)
    return (jnp.einsum('bsgn,gpn->bsgp', h_re, c_re.astype(f32))
            - jnp.einsum('bsgn,gpn->bsgp', h_im, c_im.astype(f32)))


def _s5_branch(u, a_re, a_im, log_dt, b_re, b_im, c_re, c_im, d, glu_w, glu_b, out_g):
    bsz, s, _ = u.shape
    uf = u.astype(jnp.float32)
    ug = uf.reshape(bsz, s, N_SSM_GROUPS, SSM_GROUP)
    y = (_s5_direction(ug, a_re[0], a_im[0], log_dt[0], b_re[0], b_im[0], c_re[0], c_im[0], False)
         + _s5_direction(ug, a_re[1], a_im[1], log_dt[1], b_re[1], b_im[1], c_re[1], c_im[1], True))
    y = y.reshape(bsz, s, D_SSM) + d.astype(jnp.float32) * uf
    y = jax.nn.gelu(y).astype(u.dtype)
    y = y * jax.nn.sigmoid(y @ glu_w + glu_b)
    return _rmsnorm(y, out_g)


def _conv_ffn(h, w_up, conv_w, conv_b, w_down):
    up = h @ w_up
    up = lax.conv_general_dilated(
        up, conv_w[:, None, :], window_strides=(1,),
        padding=((CONV_W // 2, CONV_W // 2),),
        dimension_numbers=('NWC', 'WIO', 'NWC'),
        feature_group_count=2 * D_FF) + conv_b
    a, g = jnp.split(up, 2, axis=-1)
    return (jax.nn.silu(g) * a) @ w_down


def setup_inputs(seed: int = 0) -> dict:
    key = jax.random.key(seed)
    ks = iter(jax.random.split(key, 40))
    nrm = lambda shape, s: jax.random.normal(next(ks), shape, jnp.float32) * s
    gain = lambda shape: 1.0 + 0.02 * jax.random.normal(next(ks), shape, jnp.float32)
    G, N, P = N_SSM_GROUPS, SSM_STATE, SSM_GROUP
    x = jax.random.normal(next(ks), (BATCH, SEQ, D_MODEL), jnp.float32)
    c = jax.random.normal(next(ks), (BATCH, D_MODEL), jnp.float32)
    offs = jax.random.randint(next(ks), (BATCH, 1), 0, SEQ, dtype=jnp.int32)
    positions = jnp.arange(SEQ, dtype=jnp.int32)[None, :] + offs
    ada_w = nrm((DEPTH, D_MODEL, 6 * D_MODEL), 0.5 * D_MODEL ** -0.5)
    ada_b = nrm((DEPTH, 6 * D_MODEL), 0.01)
    norm1_g = gain((DEPTH, D_MODEL))
    w_in = nrm((DEPTH, D_MODEL, D_IN), D_MODEL ** -0.5)
    q_norm_g = gain((DEPTH, HEAD_DIM))
    k_norm_g = gain((DEPTH, HEAD_DIM))
    lam_q1 = nrm((DEPTH, HEAD_DIM), 0.1)
    lam_k1 = nrm((DEPTH, HEAD_DIM), 0.1)
    lam_q2 = nrm((DEPTH, HEAD_DIM), 0.1)
    lam_k2 = nrm((DEPTH, HEAD_DIM), 0.1)
    subln_g = gain((DEPTH, 2 * HEAD_DIM))
    ssm_a_re = -0.5 * jnp.exp(nrm((DEPTH, 2, G, N), 0.02))
    ssm_a_im = (math.pi * jnp.arange(N, dtype=jnp.float32))[None, None, None, :] + nrm((DEPTH, 2, G, N), 0.01)
    ssm_log_dt = jax.random.uniform(next(ks), (DEPTH, 2, G), jnp.float32,
                                    math.log(0.001), math.log(0.1))
    ssm_b_re = nrm((DEPTH, 2, G, N, P), (2.0 * P) ** -0.5)
    ssm_b_im = nrm((DEPTH, 2, G, N, P), (2.0 * P) ** -0.5)
    ssm_c_re = nrm((DEPTH, 2, G, P, N), N ** -0.5)
    ssm_c_im = nrm((DEPTH, 2, G, P, N), N ** -0.5)
    ssm_d = nrm((DEPTH, D_SSM), 1.0)
    glu_w = nrm((DEPTH, D_SSM, D_SSM), D_SSM ** -0.5)
    glu_b = nrm((DEPTH, D_SSM), 0.01)
    ssm_norm_g = gain((DEPTH, D_SSM))
    w_out = nrm((DEPTH, D_MIX, D_MODEL), D_MIX ** -0.5)
    norm2_g = gain((DEPTH, D_MODEL))
    w_up = nrm((DEPTH, D_MODEL, 2 * D_FF), D_MODEL ** -0.5)
    conv_w = nrm((DEPTH, CONV_W, 2 * D_FF), CONV_W ** -0.5)
    conv_b = nrm((DEPTH, 2 * D_FF), 0.01)
    w_down = nrm((DEPTH, D_FF, D_MODEL), D_FF ** -0.5)
    return {"x": x, "c": c, "positions": positions, "ada_w": ada_w, "ada_b": ada_b,
            "norm1_g": norm1_g, "w_in": w_in, "q_norm_g": q_norm_g, "k_norm_g": k_norm_g,
            "lam_q1": lam_q1, "lam_k1": lam_k1, "lam_q2": lam_q2, "lam_k2": lam_k2,
            "subln_g": subln_g, "ssm_a_re": ssm_a_re, "ssm_a_im": ssm_a_im,
            "ssm_log_dt": ssm_log_dt, "ssm_b_re": ssm_b_re, "ssm_b_im": ssm_b_im,
            "ssm_c_re": ssm_c_re, "ssm_c_im": ssm_c_im, "ssm_d": ssm_d, "glu_w": glu_w,
            "glu_b": glu_b, "ssm_norm_g": ssm_norm_g, "w_out": w_out, "norm2_g": norm2_g,
            "w_up": w_up, "conv_w": conv_w, "conv_b": conv_b, "w_down": w_down}


def reference(x, c, positions, ada_w, ada_b, norm1_g, w_in, q_norm_g, k_norm_g,
              lam_q1, lam_k1, lam_q2, lam_k2, subln_g, ssm_a_re, ssm_a_im, ssm_log_dt,
              ssm_b_re, ssm_b_im, ssm_c_re, ssm_c_im, ssm_d, glu_w, glu_b, ssm_norm_g,
              w_out, norm2_g, w_up, conv_w, conv_b, w_down):
    cos, sin = _rope_tables(positions)
    cond = jax.nn.silu(c)
    for l in range(DEPTH):
        lam_init = 0.8 - 0.6 * math.exp(-0.3 * l)
        mod = cond @ ada_w[l] + ada_b[l]
        sh1, sc1, g1, sh2, sc2, g2 = jnp.split(mod, 6, axis=-1)
        h = _rmsnorm(x, norm1_g[l]) * (1.0 + sc1[:, None, :]) + sh1[:, None, :]
        proj = h @ w_in[l]
        q, k, v, u = jnp.split(proj, [D_ATTN, 2 * D_ATTN, 3 * D_ATTN], axis=-1)
        y_attn = _diff_attention(q, k, v, cos, sin, q_norm_g[l], k_norm_g[l],
                                 lam_q1[l], lam_k1[l], lam_q2[l], lam_k2[l], subln_g[l], lam_init)
        y_ssm = _s5_branch(u, ssm_a_re[l], ssm_a_im[l], ssm_log_dt[l], ssm_b_re[l], ssm_b_im[l],
                           ssm_c_re[l], ssm_c_im[l], ssm_d[l], glu_w[l], glu_b[l], ssm_norm_g[l])
        mix = jnp.concatenate([y_attn, y_ssm], axis=-1) @ w_out[l]
        x = x + g1[:, None, :] * mix
        h = _rmsnorm(x, norm2_g[l]) * (1.0 + sc2[:, None, :]) + sh2[:, None, :]
        x = x + g2[:, None, :] * _conv_ffn(h, w_up[l], conv_w[l], conv_b[l], w_down[l])
    return x
```

```python
import math
from contextlib import ExitStack
import numpy as np
import ml_dtypes
import concourse.bass as bass
import concourse.mybir as mybir
from concourse.bass_utils import run_bass_kernel_spmd

F32 = mybir.dt.float32
BF16 = mybir.dt.bfloat16
I32 = mybir.dt.int32
AF = mybir.ActivationFunctionType
ALU = mybir.AluOpType
AX = mybir.AxisListType

D = 1024
DEPTH = 2
DFF = 2816
NFF = DFF // 128
EPS = 1e-6
TWO_PI = 2.0 * math.pi
EPOCH = 30000
LCH = 512


class Prog:
    ENGS = ("pe", "act", "dve", "pool", "sp")

    def __init__(self, nc, es):
        self.nc = nc
        self.es = es
        self.nsem = 0
        self.ops = {e: [] for e in self.ENGS}
        self.cnt = {e: 0 for e in self.ENGS}
        self.sem = {e: self._newsem() for e in self.ENGS}
        self.pesems = {id(self.sem["pe"])}
        self.known = {e: {} for e in self.ENGS}
        self.lastw = {}
        self.readers = {}
        self.pend = {e: [] for e in self.ENGS}
        self.dpool = {e: [self._newsem() for _ in range(6)] for e in ("sp", "pool", "act")}
        self.dval = {e: [0] * 6 for e in self.dpool}
        self.drr = {e: 0 for e in self.dpool}
        self.semobj = {}
        self.ninstr = 0
        self.banklast = {}

    def _newsem(self):
        self.nsem += 1
        return self.es.enter_context(self.nc.semaphore(f"s{self.nsem}"))

    def _banks(self, *aps):
        ks = []
        for a in aps:
            if a is None or isinstance(a, (int, float)):
                continue
            if type(a.tensor).__name__ == "PSumTensorHandle":
                ks.append(a.name)
        return ks

    def op(self, eng, fn, r=(), w=(), dma=False, x=()):
        deps = list(self.pend[eng])
        self.pend[eng] = []
        for k in x:
            t = self.banklast.get(k)
            if t is not None and t[0] != eng:
                deps.append(t[1])
        for k in list(r) + list(w):
            t = self.lastw.get(k)
            if t is not None:
                deps.append(t)
        for k in w:
            deps.extend(self.readers.get(k, ()))
        if dma:
            i = self.drr[eng]
            self.drr[eng] = (i + 1) % len(self.dpool[eng])
            s = self.dpool[eng][i]
            if self.dval[eng][i] > 0:
                deps.append((s, self.dval[eng][i]))
            self.dval[eng][i] += 16
            tok = (s, self.dval[eng][i])
            amt = 16
        else:
            if self.cnt[eng] >= EPOCH:
                self.sem[eng] = self._newsem()
                self.cnt[eng] = 0
                if eng == "pe":
                    self.pesems.add(id(self.sem[eng]))
            self.cnt[eng] += 1
            tok = (self.sem[eng], self.cnt[eng])
            amt = 1
        waits = {}
        kn = self.known[eng]
        for (s, v) in deps:
            if eng == "pe" and id(s) in self.pesems:
                continue
            if v <= kn.get(id(s), 0):
                continue
            if id(s) not in waits or waits[id(s)][1] < v:
                waits[id(s)] = (s, v)
        for i_, (s, v) in waits.items():
            kn[i_] = v
        self.ops[eng].append((list(waits.values()), fn, tok[0], amt))
        self.ninstr += 1
        for k in x:
            self.banklast[k] = (eng, tok)
        for k in w:
            self.lastw[k] = tok
            self.readers[k] = []
        for k in r:
            self.readers.setdefault(k, []).append(tok)
        return tok

    def barrier(self):
        toks = []
        for e in self.ENGS:
            if self.cnt[e] > 0:
                toks.append((self.sem[e], self.cnt[e]))
        for e in self.dpool:
            for s, v in zip(self.dpool[e], self.dval[e]):
                if v > 0:
                    toks.append((s, v))
        for e in self.ENGS:
            self.pend[e].extend(toks)
        self.lastw = {}
        self.readers = {}
        self.banklast = {}

    def emit(self, final=False):
        nc = self.nc
        with nc.Block() as block:
            decos = {"pe": block.tensor, "act": block.scalar, "dve": block.vector,
                     "pool": block.gpsimd, "sp": block.sync}
            for eng in self.ENGS:
                ops = self.ops[eng]
                tail = []
                if final:
                    kn = self.known[eng]
                    for (s, v) in self.pend[eng]:
                        if eng == "pe" and id(s) in self.pesems:
                            continue
                        if v > kn.get(id(s), 0):
                            kn[id(s)] = v
                            tail.append((s, v))
                    self.pend[eng] = []

                def body(e, ops=ops, tail=tail):
                    for waits, fn, s, amt in ops:
                        for (ws, wv) in waits:
                            e.wait_ge(ws, wv)
                        fn(e).then_inc(s, amt)
                    for (ws, wv) in tail:
                        e.wait_ge(ws, wv)
                decos[eng](body)
        self.ops = {e: [] for e in self.ENGS}

    def mm(self, out, lhsT, rhs, start, stop, r, w, sgc=False):
        if sgc:
            return self.op("pe", lambda e: e.matmul(out, lhsT=lhsT, rhs=rhs, start=start, stop=stop, skip_group_check=True), r, w, x=self._banks(out))
        return self.op("pe", lambda e: e.matmul(out, lhsT=lhsT, rhs=rhs, start=start, stop=stop), r, w, x=self._banks(out))

    def tr(self, out, in_, ident, r, w):
        return self.op("pe", lambda e: e.transpose(out, in_, ident), r, w, x=self._banks(out))

    def act(self, out, in_, func, r, w, bias=None, scale=None, accum=None):
        kw = {}
        if bias is not None:
            kw["bias"] = bias
        if scale is not None:
            kw["scale"] = scale
        if accum is not None:
            kw["accum_out"] = accum
        return self.op("act", lambda e: e.activation(out=out, in_=in_, func=func, **kw), r, w, x=self._banks(out, in_, bias, scale))

    def ts(self, eng, out, in0, s1, s2, op0, op1, r, w):
        if op1 is None:
            return self.op(eng, lambda e: e.tensor_scalar(out=out, in0=in0, scalar1=s1, scalar2=None, op0=op0), r, w, x=self._banks(out, in0, s1))
        return self.op(eng, lambda e: e.tensor_scalar(out=out, in0=in0, scalar1=s1, scalar2=s2, op0=op0, op1=op1), r, w, x=self._banks(out, in0, s1, s2))

    def stt(self, out, in0, scalar, in1, op0, op1, r, w):
        return self.op("dve", lambda e: e.scalar_tensor_tensor(out=out, in0=in0, scalar=scalar, in1=in1, op0=op0, op1=op1), r, w, x=self._banks(out, in0, scalar, in1))

    def tt(self, eng, out, in0, in1, op, r, w):
        return self.op(eng, lambda e: e.tensor_tensor(out=out, in0=in0, in1=in1, op=op), r, w, x=self._banks(out, in0, in1))

    def cp(self, eng, out, in_, r, w):
        if eng == "act":
            return self.op("act", lambda e: e.copy(out=out, in_=in_), r, w, x=self._banks(out, in_))
        return self.op(eng, lambda e: e.tensor_copy(out=out, in_=in_), r, w, x=self._banks(out, in_))

    def recip(self, out, in_, r, w):
        return self.op("dve", lambda e: e.reciprocal(out=out, in_=in_), r, w, x=self._banks(out, in_))

    def memset(self, eng, ap, val, w):
        return self.op(eng, lambda e: e.memset(ap, val), (), w)

    def dma(self, q, out, in_, r, w, slow=False):
        if slow:
            return self.op(q, lambda e: e.dma_start(out=out, in_=in_, allow_slow_non_contiguous=True), r, w, dma=True)
        return self.op(q, lambda e: e.dma_start(out=out, in_=in_), r, w, dma=True)


def _rr(lst, i):
    return lst[i % len(lst)]


def make_consts():
    c = {}
    c["ident_bf"] = np.eye(128, dtype=np.float32).astype(ml_dtypes.bfloat16)
    c["ident_f"] = np.eye(128, dtype=np.float32)
    bo = np.zeros((128, 128), np.float32)
    bo[:64, :64] = 1.0
    bo[64:, 64:] = 1.0
    c["blockones"] = bo
    c["ones_f"] = np.ones((128, 128), np.float32)
    js = np.zeros((128, 128), np.float32)
    for k in range(128):
        js[k, (k + 64) % 128] = 1.0
    c["jswap"] = js
    inv = (np.float32(10000.0) ** (-(np.arange(0, 64, 2, dtype=np.float32)) / np.float32(64))).astype(np.float32)
    invf = np.zeros((128, 1), np.float32)
    for p in range(128):
        invf[p, 0] = inv[(p % 64) % 32]
    c["invf"] = invf
    gm = np.zeros((128, 8), np.float32)
    for p in range(128):
        gm[p, p // 16] = 1.0
    c["gmask"] = gm
    c["jrow"] = np.tile(np.arange(LCH, dtype=np.float32)[None, :], (128, 1))
    rm = np.zeros((128, 128), np.float32)
    for m_ in range(128):
        if (m_ % 64) < 32:
            rm[m_ + 32, m_] = -1.0
        else:
            rm[m_ - 32, m_] = 1.0
    c["rotmat"] = rm
    sb = np.zeros((128, 8, 240), np.float32)
    for x_ in range(8):
        for p in range(16):
            sb[16 * x_ + p, x_, 112 + p] = 1.0
    c["selbig"] = sb.astype(ml_dtypes.bfloat16)
    nm = np.zeros((128, 2, 128), np.float32)
    for r_ in range(128):
        for c_ in range(128):
            jp, jj = r_ // 16, c_ // 16
            if jp > jj:
                nm[r_, 0, c_] = -1.0
            if jp < jj:
                nm[r_, 1, c_] = -1.0
    c["nmask"] = nm
    return c


CONST_SHAPES = {"ident_bf": ([128, 128], BF16), "ident_f": ([128, 128], F32), "blockones": ([128, 128], F32),
                "ones_f": ([128, 128], F32), "jswap": ([128, 128], F32), "invf": ([128, 1], F32),
                "gmask": ([128, 8], F32), "jrow": ([128, LCH], F32),
                "selbig": ([128, 8, 240], BF16), "nmask": ([128, 2, 128], F32),
                "rotmat": ([128, 128], F32)}

IN_SHAPES = {
    "c": [1024], "ada_w": [2, 1024, 6144], "ada_b": [2, 6144], "norm1_g": [2, 1024],
    "w_in": [2, 1024, 2048], "q_norm_g": [2, 64], "k_norm_g": [2, 64], "lam_q1": [2, 64], "lam_k1": [2, 64],
    "lam_q2": [2, 64], "lam_k2": [2, 64], "subln_g": [2, 128], "ssm_a_re": [2, 2, 32, 64],
    "ssm_a_im": [2, 2, 32, 64], "ssm_log_dt": [2, 2, 32], "ssm_b_re": [2, 2, 32, 64, 16],
    "ssm_b_im": [2, 2, 32, 64, 16], "ssm_c_re": [2, 2, 32, 16, 64], "ssm_c_im": [2, 2, 32, 16, 64],
    "ssm_d": [2, 512], "glu_w": [2, 512, 512], "glu_b": [2, 512], "ssm_norm_g": [2, 512],
    "w_out": [2, 1024, 1024], "norm2_g": [2, 1024], "w_up": [2, 1024, 5632], "conv_w": [2, 3, 5632],
    "conv_b": [2, 5632], "w_down": [2, 2816, 1024],
}


class Ctx:
    pass


_UNIQ = [0]


def sbt(es, nc, name, shape, dt):
    _UNIQ[0] += 1
    return es.enter_context(nc.sbuf_tensor(f"{name}_{_UNIQ[0]}", list(shape), dt))


def pst(es, nc, name, shape, dt=F32):
    _UNIQ[0] += 1
    return es.enter_context(nc.psum_tensor(f"{name}_{_UNIQ[0]}", list(shape), dt))


def colvec(ap1d, n):
    return ap1d.rearrange("(c p) -> p c", p=128)


def phase0(X):
    nc, P, S = X.nc, X.P, X.S
    I = X.ins
    with ExitStack() as es:
        ct = sbt(es, nc, "p0_ct", [128, 8], F32)
        cond = sbt(es, nc, "p0_cond", [128, 8], F32)
        aw = [sbt(es, nc, f"p0_aw{i}", [128, 8, 512], F32) for i in range(2)]
        abr = sbt(es, nc, "p0_abr", [1, 6144], F32)
        mrow = sbt(es, nc, "p0_mrow", [1, 6144], F32)
        psr = [pst(es, nc, f"p0_ps{i}", [1, 512]) for i in range(2)]
        P.dma("sp", ct[:], colvec(I["c"], 8), [], ["ct"], slow=True)
        P.act(cond[:], ct[:], AF.Silu, ["ct"], ["cond"])
        for l in range(DEPTH):
            P.dma("pool", abr[:], I["ada_b"][l:l + 1, :], [], ["abr"])
            for n in range(12):
                b = n % 2
                P.dma("sp", aw[b][:], I["ada_w"][l, :, n * 512:(n + 1) * 512].rearrange("(k p) n -> p k n", p=128),
                      [], [("aw", b)])
                for k in range(8):
                    P.mm(psr[b][:], cond[:, k:k + 1], aw[b][:, k, :], k == 0, k == 7, ["cond", ("aw", b)], [("psr", b)])
                P.tt("dve", mrow[:, n * 512:(n + 1) * 512], psr[b][:], abr[:, n * 512:(n + 1) * 512], ALU.add,
                     [("psr", b), "abr"], [("mrow", n)])
            P.dma("sp", X.modrow[l:l + 1, :], mrow[:], [("mrow", n) for n in range(12)], [("modrow", l)])
            for c0 in range(0, 48, 16):
                P.dma("sp", X.modT[l][:, c0:c0 + 16], X.modrow[l, :].rearrange("(j p) -> p j", p=128)[:, c0:c0 + 16], [("modrow", l)], [("modT", l)], slow=True)
            P.dma("pool", X.g1bc[l][:], X.modrow[l, 2048:3072].partition_broadcast(128), [("modrow", l)], [("g1bc", l)])
            P.dma("pool", X.g2bc[l][:], X.modrow[l, 5120:6144].partition_broadcast(128), [("modrow", l)], [("g2bc", l)])
            n1 = sbt(es, nc, f"p0_n1_{l}", [128, 8], F32)
            n2 = sbt(es, nc, f"p0_n2_{l}", [128, 8], F32)
            P.dma("sp", n1[:], colvec(I["norm1_g"][l, :], 8), [], [("n1", l)], slow=True)
            P.dma("sp", n2[:], colvec(I["norm2_g"][l, :], 8), [], [("n2", l)], slow=True)
            P.stt(X.gm1[l][:], X.modT[l][:, 8:16], 1.0, n1[:], ALU.add, ALU.mult, [("modT", l), ("n1", l)], [("gm1", l)])
            P.stt(X.gm2[l][:], X.modT[l][:, 32:40], 1.0, n2[:], ALU.add, ALU.mult, [("modT", l), ("n2", l)], [("gm2", l)])
        CW = min(S, 2048)
        posi = sbt(es, nc, "p0_posi", [128, CW], I32)
        posf = sbt(es, nc, "p0_posf", [128, CW], F32)
        ang = sbt(es, nc, "p0_ang", [128, CW], F32)
        kf = sbt(es, nc, "p0_kf", [128, CW], F32)
        ki = sbt(es, nc, "p0_ki", [128, CW], I32)
        rr = sbt(es, nc, "p0_r", [128, CW], F32)
        tmp = sbt(es, nc, "p0_tmp", [128, CW], F32)
        r2 = sbt(es, nc, "p0_r2", [128, CW], F32)
        outs = sbt(es, nc, "p0_outs", [128, CW], F32)
        outc = sbt(es, nc, "p0_outc", [128, CW], F32)
        invf = X.cst["invf"]
        C1 = 6.28125
        C2 = TWO_PI - 6.28125
        for ci in range(S // CW):
            sl = slice(ci * CW, (ci + 1) * CW)
            P.dma("sp", posi[:], I["pos"][sl].partition_broadcast(128), [], ["posi"])
            P.cp("dve", posf[:], posi[:], ["posi"], ["posf"])
            P.ts("dve", ang[:], posf[:], invf[:, 0:1], None, ALU.mult, None, ["posf", "invf"], ["ang"])
            P.ts("dve", kf[:], ang[:], 1.0 / TWO_PI, None, ALU.mult, None, ["ang"], ["kf"])
            P.cp("dve", ki[:], kf[:], ["kf"], ["ki"])
            P.cp("dve", kf[:], ki[:], ["ki"], ["kf"])
            P.stt(rr[:], kf[:], -C1, ang[:], ALU.mult, ALU.add, ["kf", "ang"], ["rr"])
            P.stt(rr[:], kf[:], -C2, rr[:], ALU.mult, ALU.add, ["kf", "rr"], ["rr"])
            P.ts("dve", tmp[:], rr[:], math.pi, -TWO_PI, ALU.is_gt, ALU.mult, ["rr"], ["tmp"])
            P.tt("dve", rr[:], rr[:], tmp[:], ALU.add, ["rr", "tmp"], ["rr"])
            P.ts("dve", r2[:], rr[:], math.pi / 2, None, ALU.add, None, ["rr"], ["r2"])
            P.ts("dve", tmp[:], r2[:], math.pi, -TWO_PI, ALU.is_gt, ALU.mult, ["r2"], ["tmp"])
            P.tt("dve", r2[:], r2[:], tmp[:], ALU.add, ["r2", "tmp"], ["r2"])
            P.ts("dve", rr[:], rr[:], -math.pi, math.pi, ALU.max, ALU.min, ["rr"], ["rr"])
            P.ts("dve", r2[:], r2[:], -math.pi, math.pi, ALU.max, ALU.min, ["r2"], ["r2"])
            P.act(outs[:], rr[:], AF.Sin, ["rr"], ["outs"])
            P.act(outc[:], r2[:], AF.Sin, ["r2"], ["outc"])
            P.dma("sp", X.sinT[:, sl], outs[:], ["outs"], ["sinT"])
            P.dma("sp", X.cosT[:, sl], outc[:], ["outc"], ["cosT"])
        P.barrier()
        P.emit()


def load_weight_bf16(X, es, name, dram_ap, K, N, tag, stage):
    nc, P = X.nc, X.P
    wt = sbt(es, nc, name, [128, K, N], BF16)
    for k in range(K):
        for n0 in range(0, N, 2048):
            n1 = min(N, n0 + 2048)
            i = X.stage_i
            X.stage_i += 1
            st = stage[i % len(stage)]
            P.dma("sp" if i % 2 == 0 else "pool", st[:, 0:n1 - n0], dram_ap[k * 128:(k + 1) * 128, n0:n1], [], [("stage", i % len(stage))])
            eng = ("dve", "pool", "act")[i % 3]
            P.cp(eng, wt[:, k, n0:n1], st[:, 0:n1 - n0], [("stage", i % len(stage))], [(tag, k)])
    return wt


def phaseA(X, l, xsrc):
    nc, P, S = X.nc, X.P, X.S
    I = X.ins
    NB = S // 512
    with ExitStack() as es:
        stage = [sbt(es, nc, f"pa_stage{i}", [128, 2048], F32) for i in range(3)]
        win = load_weight_bf16(X, es, "pa_win", I["w_in"][l], 8, 2048, "win", stage)
        wkeys = [("win", k) for k in range(8)]
        gq = sbt(es, nc, "pa_gq", [128, 4], F32)
        for j, (nm, sc) in enumerate((("q_norm_g", 0.125), ("k_norm_g", 1.0))):
            g = I[nm][l, :]
            for m in range(2):
                P.dma("sp", gq[m * 64:(m + 1) * 64, 2 * j:2 * j + 1], g.rearrange("(d o) -> d o", o=1), [], ["gq"], slow=True)
                P.dma("sp", gq[m * 64:m * 64 + 32, 2 * j + 1:2 * j + 2], g[32:64].rearrange("(d o) -> d o", o=1), [], ["gq"], slow=True)
                P.dma("sp", gq[m * 64 + 32:m * 64 + 64, 2 * j + 1:2 * j + 2], g[0:32].rearrange("(d o) -> d o", o=1), [], ["gq"], slow=True)
        P.ts("dve", gq[:, 0:2], gq[:, 0:2], 0.125, None, ALU.mult, None, ["gq"], ["gq"])
        if getattr(X, "cut", 0) == 1:
            P.barrier(); P.emit(); return
        xt = [sbt(es, nc, f"pa_xt{i}", [128, 1024], F32) for i in range(3)]
        junk = [sbt(es, nc, f"pa_junk{i}", [128, 1024], BF16) for i in range(2)]
        xn = [sbt(es, nc, f"pa_xn{i}", [128, 1024], BF16) for i in range(2)]
        ss = sbt(es, nc, "pa_ss", [128, 8], F32)
        rs = sbt(es, nc, "pa_rs", [128, 8], F32)
        hT = [sbt(es, nc, f"pa_hT{i}", [128, 8, 512], BF16) for i in range(2)]
        cs = [sbt(es, nc, f"pa_cs{i}", [128, 2, 512], F32) for i in range(2)]
        sq = [sbt(es, nc, f"pa_sq{i}", [128, 512], F32) for i in range(2)]
        rawsb = [sbt(es, nc, f"pa_rawsb{i}", [128, 512], F32) for i in range(2)]
        rotmat = X.cst["rotmat"]
        rsb = [sbt(es, nc, f"pa_rsb{i}", [128, 512], F32) for i in range(2)]
        t1 = [sbt(es, nc, f"pa_t1{i}", [128, 512], F32) for i in range(2)]
        t2 = [sbt(es, nc, f"pa_t2{i}", [128, 512], F32) for i in range(2)]
        qo = [sbt(es, nc, f"pa_qo{i}", [128, 512], BF16) for i in range(3)]
        uo = [sbt(es, nc, f"pa_uo{i}", [128, 512], BF16) for i in range(3)]
        pT = [pst(es, nc, f"pa_pT{i}", [128, 8, 128], BF16) for i in range(2)]
        praw = [pst(es, nc, f"pa_raw{i}", [128, 512]) for i in range(2)]
        prot = [pst(es, nc, f"pa_rot{i}", [128, 512]) for i in range(2)]
        pss = pst(es, nc, "pa_pss", [128, 512])
        puv = pst(es, nc, "pa_puv", [128, 512])
        ident = X.cst["ident_bf"]
        bones = X.cst["blockones"]
        gm, mT = X.gm1[l], X.modT[l]
        tcount = 0
        mcount = 0
        ucount = 0
        for nb in range(NB):
            hb = nb % 2
            P.dma("pool", cs[hb][:, 0, :], X.cosT[:, nb * 512:(nb + 1) * 512], [], [("cs", hb)])
            P.dma("pool", cs[hb][:, 1, :], X.sinT[:, nb * 512:(nb + 1) * 512], [], [("cs", hb)])
            for tl in range(4):
                tt_ = nb * 4 + tl
                xb, jb, nbuf, pb, sc = tcount % 3, tcount % 2, tcount % 2, tcount % 2, tcount % 8
                tcount += 1
                P.dma("sp", xt[xb][:], xsrc[tt_ * 128:(tt_ + 1) * 128, :], [], [("xt", xb)])
                P.act(junk[jb][:], xt[xb][:], AF.Square, [("xt", xb)], [("junk", jb), ("ss", sc)], accum=ss[:, sc:sc + 1])
                if X.cut == 21:
                    P.barrier(); P.emit(); return
                P.act(rs[:, sc:sc + 1], ss[:, sc:sc + 1], AF.Sqrt, [("ss", sc)], [("rs", sc)], bias=X.epsc[:, 0:1], scale=1.0 / D)
                P.recip(rs[:, sc:sc + 1], rs[:, sc:sc + 1], [("rs", sc)], [("rs", sc)])
                if X.cut == 22:
                    P.barrier(); P.emit(); return
                P.ts("dve", xn[nbuf][:], xt[xb][:], rs[:, sc:sc + 1], None, ALU.mult, None, [("xt", xb), ("rs", sc)], [("xn", nbuf)])
                if X.cut == 23:
                    P.barrier(); P.emit(); return
                for c in range(8):
                    P.tr(pT[pb][:, c, :], xn[nbuf][:, c * 128:(c + 1) * 128], ident[:], [("xn", nbuf), "ident"], [("pT", pb)])
                if X.cut == 24:
                    P.barrier(); P.emit(); return
                for c in range(8):
                    dst = hT[hb][:, c, tl * 128:(tl + 1) * 128]
                    if (tt_ % 2 == 0 and X.evac != 'dve') or X.evac == 'act':
                        P.act(dst, pT[pb][:, c, :], AF.Identity, [("pT", pb), ("gm1", l), ("modT", l)], [("hT", hb, c, tl)],
                              bias=mT[:, c:c + 1], scale=gm[:, c:c + 1])
                    else:
                        P.ts("dve", dst, pT[pb][:, c, :], gm[:, c:c + 1], mT[:, c:c + 1], ALU.mult, ALU.add,
                             [("pT", pb), ("gm1", l), ("modT", l)], [("hT", hb, c, tl)])
            hk = lambda k: [("hT", hb, k, tl) for tl in range(4)]
            if getattr(X, "cut", 0) == 2:
                P.barrier(); P.emit(); return
            for mi in range(8):
                rb = mcount % 2
                mcount += 1
                isq = mi < 4
                for k in range(8):
                    P.mm(praw[rb][:], win[:, k, mi * 128:(mi + 1) * 128], hT[hb][:, k, :], k == 0, k == 7, hk(k) + [("win", k)], [("raw", rb)])
                P.cp("act", rawsb[rb][:], praw[rb][:], [("raw", rb)], [("rawsb", rb)])
                P.mm(prot[rb][:], rotmat[:], rawsb[rb][:], True, True, [("rawsb", rb), "rotmat"], [("rot", rb)])
                P.act(sq[rb][:], praw[rb][:], AF.Square, [("raw", rb)], [("sq", rb)])
                P.mm(pss[:], bones[:], sq[rb][:], True, True, [("sq", rb), "bones"], ["pss"])
                P.act(rsb[rb][:], pss[:], AF.Sqrt, ["pss"], [("rsb", rb)], bias=X.epsc[:, 0:1], scale=1.0 / 64)
                P.recip(rsb[rb][:], rsb[rb][:], [("rsb", rb)], [("rsb", rb)])
                gc = 0 if isq else 2
                P.stt(t1[rb][:], praw[rb][:], gq[:, gc:gc + 1], cs[hb][:, 0, :], ALU.mult, ALU.mult, [("raw", rb), "gq", ("cs", hb)], [("t1", rb)])
                P.stt(t2[rb][:], prot[rb][:], gq[:, gc + 1:gc + 2], cs[hb][:, 1, :], ALU.mult, ALU.mult, [("rot", rb), "gq", ("cs", hb)], [("t2", rb)])
                P.tt("pool", t1[rb][:], t1[rb][:], t2[rb][:], ALU.add, [("t1", rb), ("t2", rb)], [("t1", rb)])
                ob = ucount % 3
                ucount += 1
                P.tt("pool", qo[ob][:], t1[rb][:], rsb[rb][:], ALU.mult, [("t1", rb), ("rsb", rb)], [("qo", ob)])
                dst = (X.qT if isq else X.kT)[mi % 4, :, nb * 512:(nb + 1) * 512]
                P.dma("sp", dst, qo[ob][:], [("qo", ob)], [("qkT", mi, nb)])
            if getattr(X, "cut", 0) == 3:
                P.barrier(); P.emit(); return
            for ui in range(4):
                for k in range(8):
                    P.mm(puv[:], win[:, k, 1536 + ui * 128:1536 + (ui + 1) * 128], hT[hb][:, k, :], k == 0, k == 7, hk(k) + [("win", k)], ["puv"])
                ob = ucount % 3
                ucount += 1
                P.cp("act", uo[ob][:], puv[:], ["puv"], [("uo", ob)])
                P.dma("pool", X.uT[ui, :, nb * 512:(nb + 1) * 512], uo[ob][:], [("uo", ob)], [("uT", ui, nb)])
            for tl in range(4):
                for k in range(8):
                    P.mm(puv[:], hT[hb][:, k, tl * 128:(tl + 1) * 128], win[:, k, 1024:1536], k == 0, k == 7, [("hT", hb, k, tl), ("win", k)], ["puv"])
                ob = ucount % 3
                ucount += 1
                P.cp("act", uo[ob][:], puv[:], ["puv"], [("uo", ob)])
                P.dma("pool", X.vv[(nb * 4 + tl) * 128:(nb * 4 + tl + 1) * 128, :], uo[ob][:], [("uo", ob)], [("vv", nb, tl)])
        P.barrier()
        P.emit()


def phaseB(X, l):
    nc, P, S = X.nc, X.P, X.S
    I = X.ins
    NQ = S // 512
    NK = S // 128
    lam_init = 0.8 - 0.6 * math.exp(-0.3 * l)
    with ExitStack() as es:
        L4 = sbt(es, nc, "pb_L4", [128, 4], F32)
        pr = sbt(es, nc, "pb_pr", [128, 2], F32)
        ee = sbt(es, nc, "pb_ee", [128, 2], F32)
        nlam = sbt(es, nc, "pb_nlam", [128, 1], F32)
        gsub = sbt(es, nc, "pb_gsub", [128, 128], F32)
        kt = [[sbt(es, nc, f"pb_kt{i}_{m}", [128, S], BF16) for m in range(2)] for i in range(2)]
        vx = [sbt(es, nc, f"pb_vx{i}", [128, NK, 129], BF16) for i in range(2)]
        qt = [sbt(es, nc, f"pb_qt{i}", [128, 512], BF16) for i in range(2)]
        pb = [[sbt(es, nc, f"pb_p{i}_{m}", [128, 512], BF16) for m in range(2)] for i in range(3)]
        tbuf = [sbt(es, nc, f"pb_t{i}", [128, 128], F32) for i in range(4)]
        obuf = [sbt(es, nc, f"pb_o{i}", [128, 128], F32) for i in range(4)]
        jk = [sbt(es, nc, f"pb_jk{i}", [128, 128], F32) for i in range(4)]
        sm = sbt(es, nc, "pb_sm", [128, 8, 4], F32)
        ybf = [sbt(es, nc, f"pb_ybf{i}", [128, 128], BF16) for i in range(4)]
        yT = [sbt(es, nc, f"pb_yT{i}", [128, 512], BF16) for i in range(2)]
        pS = [[pst(es, nc, f"pb_pS{i}_{m}", [128, 512]) for m in range(2)] for i in range(2)]
        pO = [pst(es, nc, f"pb_pO{i}", [128, 512]) for i in range(3)]
        pTr = pst(es, nc, "pb_pTr", [128, 4, 128], BF16)
        ident = X.cst["ident_bf"]
        P.memset("dve", L4[:], 0.0, ["L4"])
        for j, nm in enumerate(("lam_q1", "lam_k1", "lam_q2", "lam_k2")):
            P.dma("sp", L4[0:64, j:j + 1], I[nm][l, :].rearrange("(d o) -> d o", o=1), [], ["L4"], slow=True)
        P.tt("dve", pr[:, 0:1], L4[:, 0:1], L4[:, 1:2], ALU.mult, ["L4"], ["pr"])
        P.tt("dve", pr[:, 1:2], L4[:, 2:3], L4[:, 3:4], ALU.mult, ["L4"], ["pr"])
        P.mm(pO[0][:, 0:2], X.cst["ones_f"][:], pr[:], True, True, ["pr", "ones_f"], ["pO0"])
        P.act(ee[:], pO[0][:, 0:2], AF.Exp, ["pO0"], ["ee"])
        P.tt("dve", nlam[:], ee[:, 1:2], ee[:, 0:1], ALU.subtract, ["ee"], ["nlam"])
        P.ts("dve", nlam[:], nlam[:], -lam_init, None, ALU.add, None, ["nlam"], ["nlam"])
        P.dma("sp", gsub[:], I["subln_g"][l, :].partition_broadcast(128), [], ["gsub"])
        P.ts("dve", gsub[:], gsub[:], 1.0 - lam_init, None, ALU.mult, None, ["gsub"], ["gsub"])
        for i in range(2):
            P.memset("pool", vx[i][:, :, 128:129], 1.0, [("vx", i)])
            P.memset("pool", kt[i][0][64:128, :], 0.0, [("kt", i)])
            P.memset("pool", kt[i][1][0:64, :], 0.0, [("kt", i)])
        reg = {}
        idx = 0
        for m in range(2):
            for j in range(4):
                reg[(m, j)] = (idx // 3, (idx % 3) * 129)
                idx += 1
        pcount = 0
        qcount = 0
        ecount = 0
        for h in range(4):
            hb = h % 2
            P.dma("sp", kt[hb][0][0:64, :], X.kT[h, 0:64, :], [], [("kt", hb)])
            P.dma("sp", kt[hb][1][64:128, :], X.kT[h, 64:128, :], [], [("kt", hb)])
            vsrc = X.vv[:, h * 128:(h + 1) * 128].rearrange("(kb p) e -> p kb e", p=128)
            KS = max(1, NK // 8)
            for k0 in range(0, NK, KS):
                P.dma("pool", vx[hb][:, k0:k0 + KS, 0:128], vsrc[:, k0:k0 + KS, :], [], [("vx", hb)])
            for qb in range(NQ):
                qi = qcount % 2
                qcount += 1
                P.dma("sp", qt[qi][:], X.qT[h, :, qb * 512:(qb + 1) * 512], [], [("qt", qi)])
                started = set()
                pis = {}
                for kb in range(NK + 1):
                    if kb < NK:
                        sb = kb % 2
                        pi = pcount % 3
                        pcount += 1
                        pis[kb] = pi
                        for m in range(2):
                            if X.rowtile:
                                P.mm(pS[sb][m][:], kt[hb][m][m * 64:(m + 1) * 64, kb * 128:(kb + 1) * 128], qt[qi][m * 64:(m + 1) * 64, :],
                                     True, True, [("kt", hb), ("qt", qi)], [("pS", sb, m)])
                            else:
                                P.mm(pS[sb][m][:], kt[hb][m][:, kb * 128:(kb + 1) * 128], qt[qi][:],
                                     True, True, [("kt", hb), ("qt", qi)], [("pS", sb, m)])
                        for m in range(2):
                            P.act(pb[pi][m][:], pS[sb][m][:], AF.Exp, [("pS", sb, m)], [("p", pi, m)])
                    if kb >= 1:
                        kp = kb - 1
                        pi = pis[kp]
                        for m in range(2):
                            for j in range(4):
                                bk, off = reg[(m, j)]
                                st = bk not in started
                                started.add(bk)
                                P.mm(pO[bk][:, off:off + 129], pb[pi][m][:, j * 128:(j + 1) * 128], vx[hb][:, kp, :],
                                     st, kp == NK - 1, [("p", pi, m), ("vx", hb)], [("O", m, j)], sgc=True)
                yi = qcount % 2
                ej = []
                for j in range(4):
                    e8 = ecount % 8
                    ecount += 1
                    b1, o1 = reg[(0, j)]
                    b2, o2 = reg[(1, j)]
                    ej.append((j, e8, pO[b1][:, o1:o1 + 129], pO[b2][:, o2:o2 + 129], ("sm", e8)))
                for (j, e8, O1, O2, smk) in ej:
                    P.recip(sm[:, e8, 0:1], O1[:, 128:129], [("O", 0, j)], [smk])
                    P.recip(sm[:, e8, 1:2], O2[:, 128:129], [("O", 1, j)], [smk])
                for (j, e8, O1, O2, smk) in ej:
                    P.tt("dve", sm[:, e8, 1:2], sm[:, e8, 1:2], nlam[:], ALU.mult, [smk, "nlam"], [smk])
                for (j, e8, O1, O2, smk) in ej:
                    P.ts("dve", tbuf[j][:], O2[:, 0:128], sm[:, e8, 1:2], None, ALU.mult, None, [("O", 1, j), smk], [("tbuf", j)])
                for (j, e8, O1, O2, smk) in ej:
                    P.stt(obuf[j][:], O1[:, 0:128], sm[:, e8, 0:1], tbuf[j][:], ALU.mult, ALU.add, [("O", 0, j), smk, ("tbuf", j)], [("obuf", j)])
                for (j, e8, O1, O2, smk) in ej:
                    P.act(jk[j][:], obuf[j][:], AF.Square, [("obuf", j)], [("jk", j), smk], accum=sm[:, e8, 2:3])
                for (j, e8, O1, O2, smk) in ej:
                    P.act(sm[:, e8, 3:4], sm[:, e8, 2:3], AF.Sqrt, [smk], [smk], bias=X.epsc[:, 0:1], scale=1.0 / 128)
                for (j, e8, O1, O2, smk) in ej:
                    P.recip(sm[:, e8, 3:4], sm[:, e8, 3:4], [smk], [smk])
                for (j, e8, O1, O2, smk) in ej:
                    P.stt(ybf[j][:], obuf[j][:], sm[:, e8, 3:4], gsub[:], ALU.mult, ALU.mult, [("obuf", j), smk, "gsub"], [("ybf", j)])
                for (j, e8, O1, O2, smk) in ej:
                    P.tr(pTr[:, j, :], ybf[j][:], ident[:], [("ybf", j), "ident"], [("pTr", j)])
                for (j, e8, O1, O2, smk) in ej:
                    P.cp("dve", yT[yi][:, j * 128:(j + 1) * 128], pTr[:, j, :], [("pTr", j)], [("yT", yi, j)])
                if X.cut in (31, 32, 33, 34, 35, 36):
                    continue
                P.dma("sp", X.yaT[h, :, qb * 512:(qb + 1) * 512], yT[yi][:], [("yT", yi, j) for j in range(4)], [("yaT", h, qb)])
        P.barrier()
        P.emit()


class SinCos:
    def __init__(self, X, es, W, nm, nsets=2):
        nc = X.nc
        self.W = W
        self.n = 0
        self.sets = []
        for i in range(nsets):
            self.sets.append(dict(kf=sbt(es, nc, f"{nm}_kf{i}", [128, W], F32), ki=sbt(es, nc, f"{nm}_ki{i}", [128, W], I32),
                                  rr=sbt(es, nc, f"{nm}_rr{i}", [128, W], F32), tmp=sbt(es, nc, f"{nm}_tmp{i}", [128, W], F32),
                                  r2=sbt(es, nc, f"{nm}_r2{i}", [128, W], F32), key=(nm, i)))

    def run(self, P, ang, kang, osin, ocos, kout, w=None):
        sc = self.sets[self.n % len(self.sets)]
        self.n += 1
        w = w or self.W
        kf, ki, rr, tmp, r2, k = sc["kf"][:, 0:w], sc["ki"][:, 0:w], sc["rr"][:, 0:w], sc["tmp"][:, 0:w], sc["r2"][:, 0:w], sc["key"]
        C1 = 6.28125
        C2 = TWO_PI - 6.28125
        P.ts("dve", kf, ang, 1.0 / TWO_PI, None, ALU.mult, None, [kang], [k])
        P.cp("dve", ki, kf, [k], [k])
        P.cp("dve", kf, ki, [k], [k])
        P.stt(rr, kf, -C1, ang, ALU.mult, ALU.add, [k, kang], [k])
        P.stt(rr, kf, -C2, rr, ALU.mult, ALU.add, [k], [k])
        P.ts("dve", tmp, rr, math.pi, -TWO_PI, ALU.is_gt, ALU.mult, [k], [k])
        P.tt("dve", rr, rr, tmp, ALU.add, [k], [k])
        P.ts("dve", tmp, rr, -math.pi, TWO_PI, ALU.is_lt, ALU.mult, [k], [k])
        P.tt("dve", rr, rr, tmp, ALU.add, [k], [k])
        P.ts("dve", r2, rr, math.pi / 2, None, ALU.add, None, [k], [k])
        P.ts("dve", tmp, r2, math.pi, -TWO_PI, ALU.is_gt, ALU.mult, [k], [k])
        P.tt("dve", r2, r2, tmp, ALU.add, [k], [k])
        P.ts("dve", rr, rr, -math.pi, math.pi, ALU.max, ALU.min, [k], [k])
        P.ts("dve", r2, r2, -math.pi, math.pi, ALU.max, ALU.min, [k], [k])
        P.act(osin, rr, AF.Sin, [k], [kout])
        P.act(ocos, r2, AF.Sin, [k], [kout])


def phaseS(X, l):
    nc, P, S = X.nc, X.P, X.S
    I = X.ins
    NC = S // LCH
    L = LCH
    with ExitStack() as es:
        def T(name, shape, dt=F32):
            return sbt(es, nc, "ps_" + name, shape, dt)
        are, aim, ldt = T("are", [128, 64]), T("aim", [128, 64]), T("ldt", [128, 64])
        lre, dtt, rho, th = T("lre", [128, 64]), T("dtt", [128, 64]), T("rho", [128, 64]), T("th", [128, 64])
        cth, sth, thL, cL, sL = T("cth", [128, 64]), T("sth", [128, 64]), T("thL", [128, 64]), T("cL", [128, 64]), T("sL", [128, 64])
        abr, abi, den, nre, fre, fim, t64 = (T(n_, [128, 64]) for n_ in ("abr", "abi", "den", "nre", "fre", "fim", "t64"))
        bre, bim = T("bre", [64, 64, 16]), T("bim", [64, 64, 16])
        Bbr, Bbi, tb = T("Bbr", [64, 64, 16]), T("Bbi", [64, 64, 16]), T("tb", [64, 64, 16])
        sc64 = SinCos(X, es, 64, "ps_sc64")
        scL = SinCos(X, es, L, "ps_scL")
        jrow = X.cst["jrow"]
        gmask = X.cst["gmask"]
        dcol = T("dcol", [128, 4])
        P.dma("sp", dcol[:], colvec(I["ssm_d"][l, :], 4), [], ["dcol"], slow=True)
        for hf in range(2):
            for d_ in range(2):
                for gq_ in range(2):
                    cs_ = slice(d_ * 32 + gq_ * 16, d_ * 32 + gq_ * 16 + 16)
                    P.dma("sp", are[hf * 64:(hf + 1) * 64, cs_], I["ssm_a_re"][l, d_, gq_ * 16:gq_ * 16 + 16, :].rearrange("g n -> n g"), [], ["are"], slow=True)
                    P.dma("pool", aim[hf * 64:(hf + 1) * 64, cs_], I["ssm_a_im"][l, d_, gq_ * 16:gq_ * 16 + 16, :].rearrange("g n -> n g"), [], ["aim"], slow=True)
        P.dma("sp", ldt[:], I["ssm_log_dt"][l].rearrange("d g -> (d g)").partition_broadcast(128), [], ["ldt"])
        for d_ in range(2):
            for gq_ in range(2):
                cs_ = slice(d_ * 32 + gq_ * 16, d_ * 32 + gq_ * 16 + 16)
                P.dma("sp", bre[:, cs_, :], I["ssm_b_re"][l, d_, gq_ * 16:gq_ * 16 + 16].rearrange("g n p -> n g p"), [], ["bre"], slow=True)
                P.dma("pool", bim[:, cs_, :], I["ssm_b_im"][l, d_, gq_ * 16:gq_ * 16 + 16].rearrange("g n p -> n g p"), [], ["bim"], slow=True)
        pk = "sparam"
        P.ts("dve", lre[:], are[:], -1e-4, None, ALU.min, None, ["are"], [pk])
        P.act(dtt[:], ldt[:], AF.Exp, ["ldt"], ["dtt"])
        P.tt("dve", t64[:], lre[:], dtt[:], ALU.mult, [pk, "dtt"], ["t64"])
        P.act(rho[:], t64[:], AF.Exp, ["t64"], ["rho"])
        P.tt("dve", th[:], aim[:], dtt[:], ALU.mult, ["aim", "dtt"], ["th"])
        sc64.run(P, th[:], "th", sth[:], cth[:], "scth")
        P.ts("dve", thL[:], th[:], float(L), None, ALU.mult, None, ["th"], ["thL"])
        sc64.run(P, thL[:], "thL", sL[:], cL[:], "scL")
        P.ts("dve", sL[64:128, :], sL[64:128, :], -1.0, None, ALU.mult, None, ["scL"], ["scL"])
        P.tt("dve", abr[:], rho[:], cth[:], ALU.mult, ["rho", "scth"], ["abr"])
        P.tt("dve", abi[:], rho[:], sth[:], ALU.mult, ["rho", "scth"], ["abi"])
        P.tt("dve", den[:], lre[:], lre[:], ALU.mult, [pk], ["den"])
        P.tt("dve", t64[:], aim[:], aim[:], ALU.mult, ["aim"], ["t64"])
        P.tt("dve", den[:], den[:], t64[:], ALU.add, ["den", "t64"], ["den"])
        P.recip(den[:], den[:], ["den"], ["den"])
        P.ts("dve", nre[:], abr[:], -1.0, None, ALU.add, None, ["abr"], ["nre"])
        P.tt("dve", fre[:], nre[:], lre[:], ALU.mult, ["nre", pk], ["fre"])
        P.tt("dve", t64[:], abi[:], aim[:], ALU.mult, ["abi", "aim"], ["t64"])
        P.tt("dve", fre[:], fre[:], t64[:], ALU.add, ["fre", "t64"], ["fre"])
        P.tt("dve", fre[:], fre[:], den[:], ALU.mult, ["fre", "den"], ["fre"])
        P.tt("dve", fim[:], abi[:], lre[:], ALU.mult, ["abi", pk], ["fim"])
        P.tt("dve", t64[:], nre[:], aim[:], ALU.mult, ["nre", "aim"], ["t64"])
        P.tt("dve", fim[:], fim[:], t64[:], ALU.subtract, ["fim", "t64"], ["fim"])
        P.tt("dve", fim[:], fim[:], den[:], ALU.mult, ["fim", "den"], ["fim"])
        frb = fre[0:64, :].unsqueeze(2).to_broadcast([64, 64, 16])
        fib = fim[0:64, :].unsqueeze(2).to_broadcast([64, 64, 16])
        P.tt("dve", Bbr[:], bre[:], frb, ALU.mult, ["bre", "fre"], ["Bbr"])
        P.tt("dve", tb[:], bim[:], fib, ALU.mult, ["bim", "fim"], ["tb"])
        P.tt("dve", Bbr[:], Bbr[:], tb[:], ALU.subtract, ["Bbr", "tb"], ["Bbr"])
        P.tt("dve", Bbi[:], bim[:], frb, ALU.mult, ["bim", "fre"], ["Bbi"])
        P.tt("dve", tb[:], bre[:], fib, ALU.mult, ["bre", "fim"], ["tb"])
        P.tt("dve", Bbi[:], Bbi[:], tb[:], ALU.add, ["Bbi", "tb"], ["Bbi"])
        ut = T("ut", [128, S], BF16)
        Yacc = T("Yacc", [128, S])
        TT = T("TT", [128, 128])
        cn1, cn2 = T("cn1", [128, 2, 64]), T("cn2", [128, 2, 64])
        S1, S2 = T("S1", [128, 128]), T("S2", [128, 128])
        Zw = [T(f"Zw{g}", [128, 128], BF16) for g in range(8)]
        Zsw = [T(f"Zsw{g}", [128, 128], BF16) for g in range(8)]
        Wa = [T(f"Wa{g}", [128, 128], BF16) for g in range(8)]
        Wb = [T(f"Wb{g}", [128, 128], BF16) for g in range(8)]
        Rm = [T(f"Rm{g}", [128, 128]) for g in range(8)]
        cj = [T(f"cj{g}", [128, L]) for g in range(8)]
        sj = [T(f"sj{g}", [128, L]) for g in range(8)]
        angt = [T(f"angt{i}", [128, L]) for i in range(2)]
        init = T("init", [128, 8])
        zh = [T(f"zh{i}", [128, L]) for i in range(4)]
        zh2 = [T(f"zh2{i}", [128, L]) for i in range(4)]
        G = [T(f"G{i}", [128, L]) for i in range(4)]
        P1 = [T(f"P1{i}", [128, L], BF16) for i in range(4)]
        P2 = [T(f"P2{i}", [128, L], BF16) for i in range(4)]
        yg = [T(f"yg{i}", [128, L]) for i in range(2)]
        ygb = [T(f"ygb{i}", [128, L], BF16) for i in range(2)]
        pZ = [pst(es, nc, f"ps_pZ{i}", [128, 512]) for i in range(2)]
        pZs = [pst(es, nc, f"ps_pZs{i}", [128, 512]) for i in range(2)]
        pY = pst(es, nc, "ps_pY", [128, 512])
        pC = pst(es, nc, "ps_pC", [128, 512])
        pX = pst(es, nc, "ps_pX", [128, 512])
        identf = X.cst["ident_f"]
        jswap = X.cst["jswap"]
        gcount = 0
        for ct in range(4):
            P.dma("sp", ut[:], X.uT[ct, :, :], [], ["ut"])
            for d in range(2):
                g0 = d * 32 + ct * 8
                for q_, (src, col) in enumerate(((Bbr, 0), (Bbi, 64))):
                    P.tr(pX[:, col:col + 64], src[:, g0:g0 + 8, :].rearrange("n g p -> n (g p)"), identf[0:64, 0:64], ["Bbr", "Bbi", "ident_f"], ["pX"])
                P.cp("act", TT[:], pX[:, 0:128], ["pX"], ["TT"])
                P.dma("sp", cn1[:, 0, :], I["ssm_c_re"][l, d, ct * 8:(ct + 1) * 8].rearrange("g p n -> (g p) n"), [], ["cn1"])
                P.dma("sp", cn1[:, 1, :], I["ssm_c_im"][l, d, ct * 8:(ct + 1) * 8].rearrange("g p n -> (g p) n"), [], ["cn1"])
                P.dma("pool", cn2[:, 0, :], I["ssm_c_im"][l, d, ct * 8:(ct + 1) * 8].rearrange("g p n -> (g p) n"), [], ["cn2"])
                P.dma("pool", cn2[:, 1, :], I["ssm_c_re"][l, d, ct * 8:(ct + 1) * 8].rearrange("g p n -> (g p) n"), [], ["cn2"])
                P.tr(pX[:, 128:256], cn1[:].rearrange("q t n -> q (t n)"), identf[:], ["cn1", "ident_f"], ["pX"])
                P.tr(pX[:, 256:384], cn2[:].rearrange("q t n -> q (t n)"), identf[:], ["cn2", "ident_f"], ["pX"])
                P.cp("act", S1[:], pX[:, 128:256], ["pX"], ["S1"])
                P.cp("act", S2[:], pX[:, 256:384], ["pX"], ["S2"])
                for g in range(8):
                    dg = g0 + g
                    mk = gmask[:, g:g + 1]
                    wk = ("w", g)
                    P.ts("dve", Zw[g][:], TT[:], mk, None, ALU.mult, None, ["TT", "gmask"], [wk])
                    P.ts("dve", Zsw[g][:, 0:64], TT[:, 64:128], mk, None, ALU.mult, None, ["TT", "gmask"], [wk])
                    P.ts("dve", Zsw[g][:, 64:128], TT[:, 0:64], mk, -1.0, ALU.mult, ALU.mult, ["TT", "gmask"], [wk])
                    P.memset("pool", Wa[g][:], 0.0, [wk])
                    P.memset("pool", Wb[g][:], 0.0, [wk])
                    cs_ = slice(g * 16, (g + 1) * 16)
                    P.cp("pool", Wa[g][0:64, cs_], S1[0:64, cs_], ["S1"], [wk])
                    P.ts("pool", Wa[g][64:128, cs_], S1[64:128, cs_], -1.0, None, ALU.mult, None, ["S1"], [wk])
                    P.ts("pool", Wb[g][:, cs_], S2[:, cs_], -1.0, None, ALU.mult, None, ["S2"], [wk])
                    P.ts("dve", Rm[g][:], identf[:], cL[:, dg:dg + 1], None, ALU.mult, None, ["ident_f", "scL"], [wk])
                    P.stt(Rm[g][:], jswap[:], sL[:, dg:dg + 1], Rm[g][:], ALU.mult, ALU.add, ["jswap", "scL", wk], [wk])
                    ab = gcount % 2
                    gcount += 1
                    P.ts("dve", angt[ab][:], jrow[:], th[:, dg:dg + 1], None, ALU.mult, None, ["jrow", "th"], [("angt", ab)])
                    scL.run(P, angt[ab][:], ("angt", ab), sj[g][:], cj[g][:], ("tab", g))
                P.memset("dve", init[:], 0.0, ["init"])
                order = list(range(NC)) if d == 0 else list(range(NC - 1, -1, -1))
                its = [(ci, g) for ci in order for g in range(8)]

                def views(g):
                    if d == 0:
                        return cj[g][:], sj[g][:]
                    return cj[g][:, ::-1], sj[g][:, ::-1]

                def stage1(n):
                    ci, g = its[n]
                    tok = slice(ci * L, (ci + 1) * L)
                    zb, b4 = n % 2, n % 4
                    wk = ("w", g)
                    cjv, sjv = views(g)
                    P.mm(pZ[zb][:], Zw[g][:], ut[:, tok], True, True, [wk, "ut"], [("pZ", zb)])
                    P.mm(pZs[zb][:], Zsw[g][:], ut[:, tok], True, True, [wk, "ut"], [("pZs", zb)])
                    P.tt("dve", zh[b4][:], pZ[zb][:], cjv, ALU.mult, [("pZ", zb), ("tab", g)], [("zh", b4)])
                    P.tt("dve", zh2[b4][:], pZs[zb][:], sjv, ALU.mult, [("pZs", zb), ("tab", g)], [("zh2", b4)])
                    P.tt("pool", zh[b4][:], zh[b4][:], zh2[b4][:], ALU.add, [("zh", b4), ("zh2", b4)], [("zh", b4)])

                def stage2(n):
                    ci, g = its[n]
                    b4 = n % 4
                    dg = g0 + g
                    cjv, sjv = views(g)
                    rb = rho[:, dg:dg + 1].to_broadcast([128, L])
                    if d == 0:
                        go, zi = G[b4][:], zh[b4][:]
                    else:
                        go, zi = G[b4][:, ::-1], zh[b4][:, ::-1]
                    ini = init[:, g:g + 1]
                    P.op("dve", (lambda go=go, zi=zi, rb=rb, ini=ini: (lambda e: e.tensor_tensor_scan(
                        out=go, data0=rb, data1=zi, initial=ini, op0=ALU.mult, op1=ALU.add)))(),
                        [("zh", b4), "rho", ("init", g)], [("G", b4)])
                    P.tt("pool", P1[b4][:], G[b4][:], cjv, ALU.mult, [("G", b4), ("tab", g)], [("P1", b4)])
                    P.tt("dve", P2[b4][:], G[b4][:], sjv, ALU.mult, [("G", b4), ("tab", g)], [("P2", b4)])

                def stage3(n):
                    ci, g = its[n]
                    tok = slice(ci * L, (ci + 1) * L)
                    b4 = n % 4
                    wk = ("w", g)
                    last = G[b4][:, L - 1:L] if d == 0 else G[b4][:, 0:1]
                    P.mm(pY[:], Wa[g][:], P1[b4][:], g == 0, False, [wk, ("P1", b4)], ["pY"])
                    P.mm(pY[:], Wb[g][:], P2[b4][:], False, g == 7, [wk, ("P2", b4)], ["pY"])
                    P.mm(pC[:, g:g + 1], Rm[g][:], last, True, True, [wk, ("G", b4)], [("pC", g)])
                    P.cp("act", init[:, g:g + 1], pC[:, g:g + 1], [("pC", g)], [("init", g)])
                    if g == 7:
                        if d == 0:
                            P.cp("act", Yacc[:, tok], pY[:], ["pY"], [("Yacc", ci)])
                        else:
                            P.tt("dve", Yacc[:, tok], pY[:], Yacc[:, tok], ALU.add, ["pY", ("Yacc", ci)], [("Yacc", ci)])

                NI = len(its)
                for n in range(NI + 2):
                    if n < NI:
                        stage1(n)
                    if 1 <= n <= NI:
                        stage2(n - 1)
                    if n >= 2:
                        stage3(n - 2)
            for ci in range(NC):
                tok = slice(ci * L, (ci + 1) * L)
                yb = ci % 2
                P.stt(yg[yb][:], ut[:, tok], dcol[:, ct:ct + 1], Yacc[:, tok], ALU.mult, ALU.add, ["ut", "dcol", ("Yacc", ci)], [("yg", yb)])
                P.act(ygb[yb][:], yg[yb][:], AF.Gelu, [("yg", yb)], [("ygb", yb)])
                P.dma("pool", X.ygT[ct, :, tok], ygb[yb][:], [("ygb", yb)], [("ygT", ct, ci)])
        P.barrier()
        P.emit()


def phaseS8(X, l):
    nc, P, S = X.nc, X.P, X.S
    I = X.ins
    NBLK = S // 8
    L = min(256, NBLK)
    NC = NBLK // L
    UB = min(512, NBLK)
    NUB = NBLK // UB
    with ExitStack() as es:
        def T(name, shape, dt=F32):
            return sbt(es, nc, "s8_" + name, shape, dt)
        names = ("are", "aim", "ldt", "lre", "dtt", "rho", "rho8", "th", "th8", "cth", "sth", "thL", "cL", "sL",
                 "abr", "abi", "den", "nre", "fre", "fim", "t64", "t64b", "air", "aii")
        pr = {n_: T(n_, [128, 64]) for n_ in names}
        PWr, PWi, PNr, PNi = T("PWr", [128, 64, 9]), T("PWi", [128, 64, 9]), T("PNr", [128, 64, 9]), T("PNi", [128, 64, 9])
        Bbr, Bbi = T("Bbr", [128, 64, 16]), T("Bbi", [128, 64, 16])
        dvec8 = T("dvec8", [128, 32])
        sg = T("sg", [128, 4])
        es2 = ExitStack()
        bre, bim = sbt(es2, nc, "s8_bre", [128, 64, 16], F32), sbt(es2, nc, "s8_bim", [128, 64, 16], F32)
        tB = sbt(es2, nc, "s8_tB", [128, 64, 16], F32)
        sc64 = SinCos(X, es2, 64, "s8_sc64")
        jrow = X.cst["jrow"]
        identf, jswap = X.cst["ident_f"], X.cst["jswap"]
        selbig = X.cst["selbig"]
        nmask = X.cst["nmask"]
        K_ = "prm"

        def V(out, a, b, op):
            P.tt("dve", out, a, b, op, [K_], [K_])

        def cmul(o_r, o_i, a_r, a_i, b_r, b_i, t1, t2):
            V(t1, a_r, b_r, ALU.mult)
            V(t2, a_i, b_i, ALU.mult)
            V(o_r, t1, t2, ALU.subtract)
            V(t1, a_r, b_i, ALU.mult)
            V(t2, a_i, b_r, ALU.mult)
            V(o_i, t1, t2, ALU.add)

        for j in range(8):
            P.dma("sp", dvec8[j * 16:(j + 1) * 16, :], I["ssm_d"][l, :].rearrange("(g p) -> p g", p=16), [], [K_], slow=True)
        P.memset("dve", sg[0:64, 0:1], 1.0, [K_])
        P.memset("dve", sg[64:128, 0:1], -1.0, [K_])
        P.memset("dve", sg[0:64, 1:2], -1.0, [K_])
        P.memset("dve", sg[64:128, 1:2], 1.0, [K_])
        P.memset("dve", sg[0:64, 2:3], 1.0, [K_])
        P.memset("dve", sg[64:128, 2:3], 0.0, [K_])
        P.memset("dve", sg[0:64, 3:4], 0.0, [K_])
        P.memset("dve", sg[64:128, 3:4], 1.0, [K_])
        for hf in range(2):
            hs = slice(hf * 64, (hf + 1) * 64)
            for d_ in range(2):
                for gq_ in range(2):
                    cs_ = slice(d_ * 32 + gq_ * 16, d_ * 32 + gq_ * 16 + 16)
                    gs_ = slice(gq_ * 16, gq_ * 16 + 16)
                    P.dma("sp", pr["are"][hs, cs_], I["ssm_a_re"][l, d_, gs_, :].rearrange("g n -> n g"), [], [K_], slow=True)
                    P.dma("pool", pr["aim"][hs, cs_], I["ssm_a_im"][l, d_, gs_, :].rearrange("g n -> n g"), [], [K_], slow=True)
                    P.dma("sp", bre[hs, cs_, :], I["ssm_b_re"][l, d_, gs_].rearrange("g n p -> n g p"), [], [K_], slow=True)
                    P.dma("pool", bim[hs, cs_, :], I["ssm_b_im"][l, d_, gs_].rearrange("g n p -> n g p"), [], [K_], slow=True)
        P.dma("sp", pr["ldt"][:], I["ssm_log_dt"][l].rearrange("d g -> (d g)").partition_broadcast(128), [], [K_])
        p_ = pr
        P.ts("dve", p_["lre"][:], p_["are"][:], -1e-4, None, ALU.min, None, [K_], [K_])
        P.act(p_["dtt"][:], p_["ldt"][:], AF.Exp, [K_], [K_])
        V(p_["t64"][:], p_["lre"][:], p_["dtt"][:], ALU.mult)
        P.act(p_["rho"][:], p_["t64"][:], AF.Exp, [K_], [K_])
        P.act(p_["rho8"][:], p_["t64"][:], AF.Exp, [K_], [K_], scale=8.0)
        V(p_["th"][:], p_["aim"][:], p_["dtt"][:], ALU.mult)
        sc64.run(P, p_["th"][:], K_, p_["sth"][:], p_["cth"][:], K_)
        P.ts("dve", p_["th8"][:], p_["th"][:], 8.0, None, ALU.mult, None, [K_], [K_])
        P.ts("dve", p_["thL"][:], p_["th8"][:], float(L), None, ALU.mult, None, [K_], [K_])
        sc64.run(P, p_["thL"][:], K_, p_["sL"][:], p_["cL"][:], K_)
        P.ts("dve", p_["sL"][64:128, :], p_["sL"][64:128, :], -1.0, None, ALU.mult, None, [K_], [K_])
        V(p_["abr"][:], p_["rho"][:], p_["cth"][:], ALU.mult)
        V(p_["abi"][:], p_["rho"][:], p_["sth"][:], ALU.mult)
        V(p_["den"][:], p_["lre"][:], p_["lre"][:], ALU.mult)
        V(p_["t64"][:], p_["aim"][:], p_["aim"][:], ALU.mult)
        V(p_["den"][:], p_["den"][:], p_["t64"][:], ALU.add)
        P.recip(p_["den"][:], p_["den"][:], [K_], [K_])
        P.ts("dve", p_["nre"][:], p_["abr"][:], -1.0, None, ALU.add, None, [K_], [K_])
        V(p_["fre"][:], p_["nre"][:], p_["lre"][:], ALU.mult)
        V(p_["t64"][:], p_["abi"][:], p_["aim"][:], ALU.mult)
        V(p_["fre"][:], p_["fre"][:], p_["t64"][:], ALU.add)
        V(p_["fre"][:], p_["fre"][:], p_["den"][:], ALU.mult)
        V(p_["fim"][:], p_["abi"][:], p_["lre"][:], ALU.mult)
        V(p_["t64"][:], p_["nre"][:], p_["aim"][:], ALU.mult)
        V(p_["fim"][:], p_["fim"][:], p_["t64"][:], ALU.subtract)
        V(p_["fim"][:], p_["fim"][:], p_["den"][:], ALU.mult)
        frb = p_["fre"][:].unsqueeze(2).to_broadcast([128, 64, 16])
        fib = p_["fim"][:].unsqueeze(2).to_broadcast([128, 64, 16])
        V(Bbr[:], bre[:], frb, ALU.mult)
        V(tB[:], bim[:], fib, ALU.mult)
        V(Bbr[:], Bbr[:], tB[:], ALU.subtract)
        V(Bbi[:], bim[:], frb, ALU.mult)
        V(tB[:], bre[:], fib, ALU.mult)
        V(Bbi[:], Bbi[:], tB[:], ALU.add)
        P.memset("dve", PWr[:, :, 0], 1.0, [K_])
        P.memset("dve", PWi[:, :, 0], 0.0, [K_])
        P.memset("dve", PNr[:, :, 0], 1.0, [K_])
        P.memset("dve", PNi[:, :, 0], 0.0, [K_])
        P.cp("dve", PWr[:, :, 1], p_["abr"][:], [K_], [K_])
        P.cp("dve", PWi[:, :, 1], p_["abi"][:], [K_], [K_])
        V(p_["den"][:], p_["abr"][:], p_["abr"][:], ALU.mult)
        V(p_["t64"][:], p_["abi"][:], p_["abi"][:], ALU.mult)
        V(p_["den"][:], p_["den"][:], p_["t64"][:], ALU.add)
        P.recip(p_["den"][:], p_["den"][:], [K_], [K_])
        V(p_["air"][:], p_["abr"][:], p_["den"][:], ALU.mult)
        V(p_["aii"][:], p_["abi"][:], p_["den"][:], ALU.mult)
        P.ts("dve", p_["aii"][:], p_["aii"][:], -1.0, None, ALU.mult, None, [K_], [K_])
        P.cp("dve", PNr[:, :, 1], p_["air"][:], [K_], [K_])
        P.cp("dve", PNi[:, :, 1], p_["aii"][:], [K_], [K_])
        for k in range(1, 8):
            cmul(PWr[:, :, k + 1], PWi[:, :, k + 1], PWr[:, :, k], PWi[:, :, k], p_["abr"][:], p_["abi"][:], p_["t64"][:], p_["t64b"][:])
            cmul(PNr[:, :, k + 1], PNi[:, :, k + 1], PNr[:, :, k], PNi[:, :, k], p_["air"][:], p_["aii"][:], p_["t64"][:], p_["t64b"][:])
        P.barrier()
        P.emit()
        es2.close()
        scL = SinCos(X, es, L, "s8_scL", nsets=1)
        ut = T("ut", [128, S], BF16)
        ygU = ut[:].rearrange("p (g c) -> p g c", g=8)
        U = T("U", [128, 8, NBLK], BF16)
        Yflat = T("Yacc", [128, max(8 * NBLK, 7168)])
        Yacc = Yflat[:, 0:8 * NBLK].rearrange("p (g c) -> p g c", g=8)
        alias = True
        if alias:
            Yf = Yflat[:]
            prA, prB, prC, prD = (Yf[:, i * 1024:(i + 1) * 1024].rearrange("p (g j q) -> p g j q", g=8, j=8) for i in range(4))
            XM, WaF, WbF = (Yf[:, i * 1024:(i + 1) * 1024].rearrange("p (g m) -> p g m", g=8) for i in range(4, 7))
        else:
            prA, prB, prC, prD = (T(n_, [128, 8, 8, 16]) for n_ in ("prA", "prB", "prC", "prD"))
            XM, WaF, WbF = T("XM", [128, 8, 128]), T("WaF", [128, 8, 128]), T("WbF", [128, 8, 128])
        cn1, cn2 = T("cn1", [128, 2, 64]), T("cn2", [128, 2, 64])
        S1, S2 = T("S1", [128, 8, 16]), T("S2", [128, 8, 16])
        Zw = [[T(f"Zw{d}_{g}", [128, 128], BF16) for g in range(8)] for d in range(2)]
        Zsw = [[T(f"Zsw{d}_{g}", [128, 128], BF16) for g in range(8)] for d in range(2)]
        Wa = [[T(f"Wa{d}_{g}", [128, 128], BF16) for g in range(8)] for d in range(2)]
        Wb = [[T(f"Wb{d}_{g}", [128, 128], BF16) for g in range(8)] for d in range(2)]
        M1acc = [T(f"M1a{g}", [128, 128]) for g in range(8)]
        M1b = [T(f"M1b{g}", [128, 128], BF16) for g in range(8)]
        tmpM = T("tmpM", [128, 128])
        Rm = [T(f"Rm{g}", [128, 128]) for g in range(8)]
        cj = [T(f"cj{g}", [128, L]) for g in range(8)]
        sj = [T(f"sj{g}", [128, L]) for g in range(8)]
        angt = [T(f"angt{i}", [128, L]) for i in range(2)]
        init = T("init", [128, 8])
        NBUF = 4
        zh = [T(f"zh{i}", [128, L]) for i in range(NBUF)]
        zh2 = [T(f"zh2{i}", [128, L]) for i in range(NBUF)]
        G = [T(f"G{i}", [128, L]) for i in range(NBUF)]
        P1 = [T(f"P1{i}", [128, L], BF16) for i in range(NBUF)]
        P2 = [T(f"P2{i}", [128, L], BF16) for i in range(NBUF)]
        ytile = T("ytile", [128, UB * 8], BF16)
        pZ = [pst(es, nc, f"s8_pZ{i}", [128, 512]) for i in range(2)]
        pZs = [pst(es, nc, f"s8_pZs{i}", [128, 512]) for i in range(2)]
        pY = [pst(es, nc, f"s8_pY{i}", [128, 512]) for i in range(2)]
        pC = pst(es, nc, "s8_pC", [128, 512])
        pX = pst(es, nc, "s8_pX", [128, 512])
        alt = [pX, pY[0]]
        acount = 0
        gcount = 0
        for ct in range(4):
            P.dma("sp", ut[:], X.uT[ct, :, :], [], ["ut"])
            for g in range(8):
                for ub in range(NUB):
                    ps = alt[acount % 2]
                    pk = ("alt", acount % 2)
                    acount += 1
                    for j in range(8):
                        P.mm(ps[:, 0:UB], selbig[:, g, 112 - 16 * j:240 - 16 * j], ut[:, slice(ub * UB * 8 + j, (ub + 1) * UB * 8, 8)],
                             j == 0, j == 7, ["ut", "selbig"], [pk])
                    P.cp("act", U[:, g, ub * UB:(ub + 1) * UB], ps[:, 0:UB], [pk], [("U", g)])
            P.barrier()
            for d in range(2):
                g0 = d * 32 + ct * 8
                gs = slice(g0, g0 + 8)
                kk = slice(7, None, -1) if d == 0 else slice(0, 8)
                P.dma("sp", cn1[:, 0, :], I["ssm_c_re"][l, d, ct * 8:(ct + 1) * 8].rearrange("g p n -> (g p) n"), [], ["cn1"])
                P.dma("sp", cn1[:, 1, :], I["ssm_c_im"][l, d, ct * 8:(ct + 1) * 8].rearrange("g p n -> (g p) n"), [], ["cn1"])
                P.dma("pool", cn2[:, 0, :], I["ssm_c_im"][l, d, ct * 8:(ct + 1) * 8].rearrange("g p n -> (g p) n"), [], ["cn2"])
                P.dma("pool", cn2[:, 1, :], I["ssm_c_re"][l, d, ct * 8:(ct + 1) * 8].rearrange("g p n -> (g p) n"), [], ["cn2"])
                P.tr(pX[:, 0:128], cn1[:].rearrange("q t n -> q (t n)"), identf[:], ["cn1", "ident_f"], ["pXa"])
                P.tr(pX[:, 128:256], cn2[:].rearrange("q t n -> q (t n)"), identf[:], ["cn2", "ident_f"], ["pXb"])
                P.cp("act", S1[:].rearrange("q g p -> q (g p)"), pX[:, 0:128], ["pXa"], [K_])
                P.cp("act", S2[:].rearrange("q g p -> q (g p)"), pX[:, 128:256], ["pXb"], [K_])
                shp = [128, 8, 8, 16]
                Pr2 = PWr[:, gs, kk].unsqueeze(3).to_broadcast(shp)
                Pi2 = PWi[:, gs, kk].unsqueeze(3).to_broadcast(shp)
                Pcr = PNr[:, gs, kk].unsqueeze(3).to_broadcast(shp)
                Pci = PNi[:, gs, kk].unsqueeze(3).to_broadcast(shp)
                Brb = Bbr[:, gs, :].unsqueeze(2).to_broadcast(shp)
                Bib = Bbi[:, gs, :].unsqueeze(2).to_broadcast(shp)
                S1b = S1[:].unsqueeze(2).to_broadcast(shp)
                S2b = S2[:].unsqueeze(2).to_broadcast(shp)
                XMv = XM[:].rearrange("q g (j p) -> q g j p", p=16)
                WaFv = WaF[:].rearrange("q g (j p) -> q g j p", p=16)
                WbFv = WbF[:].rearrange("q g (j p) -> q g j p", p=16)
                V(prA[:], Brb, Pr2, ALU.mult)
                V(prC[:], Bib, Pi2, ALU.mult)
                V(prA[:], prA[:], prC[:], ALU.subtract)
                V(prB[:], Brb, Pi2, ALU.mult)
                V(prC[:], Bib, Pr2, ALU.mult)
                V(prB[:], prB[:], prC[:], ALU.add)
                P.ts("dve", XMv, prA[:], sg[:, 2:3], None, ALU.mult, None, [K_], [K_])
                P.stt(XMv, prB[:], sg[:, 3:4], XMv, ALU.mult, ALU.add, [K_], [K_])
                V(prA[:], S1b, Pcr, ALU.mult)
                V(prB[:], S2b, Pci, ALU.mult)
                P.stt(WaFv, prA[:], sg[:, 0:1], prB[:], ALU.mult, ALU.subtract, [K_], [K_])
                V(prC[:], S1b, Pci, ALU.mult)
                V(prD[:], S2b, Pcr, ALU.mult)
                P.stt(WbFv, prC[:], sg[:, 1:2], prD[:], ALU.mult, ALU.subtract, [K_], [K_])
                for g in range(8):
                    wk = ("w", d, g)
                    gg = ct * 8 + g
                    P.tr(pX[:, 256:384], XM[:, g, :], identf[:], [K_, "ident_f"], ["pXc"])
                    P.cp("act", Zw[d][g][:], pX[:, 256:384], ["pXc"], [wk])
                    P.cp("act", Zsw[d][g][:, 0:64], pX[:, 320:384], ["pXc"], [wk])
                    P.ts("dve", Zsw[d][g][:, 64:128], pX[:, 256:320], -1.0, None, ALU.mult, None, ["pXc"], [wk])
                    P.cp("pool", Wa[d][g][:], WaF[:, g, :], [K_], [wk])
                    P.cp("pool", Wb[d][g][:], WbF[:, g, :], [K_], [wk])
                    P.mm(pX[:, 384:512], XM[:, g, :], WaF[:, g, :], True, True, [K_], ["pXd"])
                    if d == 0:
                        P.ts("dve", M1acc[g][:], identf[:], dvec8[:, gg:gg + 1], None, ALU.mult, None, [K_, "ident_f"], [("M1", g)])
                    P.tt("dve", tmpM[:], pX[:, 384:512], nmask[:, d, :], ALU.mult, ["pXd", "nmask"], ["tmpM"])
                    P.tt("dve", M1acc[g][:], M1acc[g][:], tmpM[:], ALU.add, ["tmpM", ("M1", g)], [("M1", g)])
                    if d == 1:
                        P.cp("dve", M1b[g][:], M1acc[g][:], [("M1", g)], [("M1b", g)])
            P.barrier()
            for d in range(2):
                g0 = d * 32 + ct * 8
                for g in range(8):
                    dg = g0 + g
                    wk = ("r", g)
                    P.ts("dve", Rm[g][:], identf[:], p_["cL"][:, dg:dg + 1], None, ALU.mult, None, ["ident_f", K_], [wk])
                    P.stt(Rm[g][:], jswap[:], p_["sL"][:, dg:dg + 1], Rm[g][:], ALU.mult, ALU.add, ["jswap", K_, wk], [wk])
                    ab = gcount % 2
                    gcount += 1
                    P.ts("dve", angt[ab][:], jrow[:, 0:L], p_["th8"][:, dg:dg + 1], None, ALU.mult, None, ["jrow", K_], [("angt", ab)])
                    scL.run(P, angt[ab][:], ("angt", ab), sj[g][:], cj[g][:], ("tab", g))
                P.memset("dve", init[:], 0.0, ["init"])
                order = list(range(NC)) if d == 0 else list(range(NC - 1, -1, -1))
                its = [(ci, g) for ci in order for g in range(8)]

                def views(g):
                    if d == 0:
                        return cj[g][:], sj[g][:]
                    return cj[g][:, ::-1], sj[g][:, ::-1]

                def stage1(n):
                    ci, g = its[n]
                    blk = slice(ci * L, (ci + 1) * L)
                    zb, b4 = n % 2, n % NBUF
                    wk = ("w", d, g)
                    cjv, sjv = views(g)
                    P.mm(pZ[zb][:, 0:L], Zw[d][g][:], U[:, g, blk], True, True, [wk, ("U", g)], [("pZ", zb)])
                    P.mm(pZs[zb][:, 0:L], Zsw[d][g][:], U[:, g, blk], True, True, [wk, ("U", g)], [("pZs", zb)])
                    P.tt("dve", zh[b4][:], pZ[zb][:, 0:L], cjv, ALU.mult, [("pZ", zb), ("tab", g)], [("zh", b4)])
                    P.tt("dve", zh2[b4][:], pZs[zb][:, 0:L], sjv, ALU.mult, [("pZs", zb), ("tab", g)], [("zh2", b4)])
                    P.tt("pool", zh[b4][:], zh[b4][:], zh2[b4][:], ALU.add, [("zh", b4), ("zh2", b4)], [("zh", b4)])

                def stage2(n):
                    ci, g = its[n]
                    b4 = n % NBUF
                    dg = g0 + g
                    cjv, sjv = views(g)
                    rb = p_["rho8"][:, dg:dg + 1].to_broadcast([128, L])
                    if d == 0:
                        go, zi = G[b4][:], zh[b4][:]
                    else:
                        go, zi = G[b4][:, ::-1], zh[b4][:, ::-1]
                    ini = init[:, g:g + 1]
                    P.op("dve", (lambda go=go, zi=zi, rb=rb, ini=ini: (lambda e: e.tensor_tensor_scan(
                        out=go, data0=rb, data1=zi, initial=ini, op0=ALU.mult, op1=ALU.add)))(),
                        [("zh", b4), K_, ("init", g)], [("G", b4)])
                    P.tt("pool", P1[b4][:], G[b4][:], cjv, ALU.mult, [("G", b4), ("tab", g)], [("P1", b4)])
                    P.tt("dve", P2[b4][:], G[b4][:], sjv, ALU.mult, [("G", b4), ("tab", g)], [("P2", b4)])

                def stage3(n):
                    ci, g = its[n]
                    blk = slice(ci * L, (ci + 1) * L)
                    b4 = n % NBUF
                    yb = n % 2
                    wk = ("w", d, g)
                    last = G[b4][:, L - 1:L] if d == 0 else G[b4][:, 0:1]
                    if d == 0:
                        P.mm(pY[yb][:, 0:L], M1b[g][:], U[:, g, blk], True, False, [("M1b", g), ("U", g)], [("pY", yb)])
                    P.mm(pY[yb][:, 0:L], Wa[d][g][:], P1[b4][:], d == 1, False, [wk, ("P1", b4)], [("pY", yb)])
                    P.mm(pY[yb][:, 0:L], Wb[d][g][:], P2[b4][:], False, True, [wk, ("P2", b4)], [("pY", yb)])
                    P.mm(pC[:, g:g + 1], Rm[g][:], last, True, True, [("r", g), ("G", b4)], [("pC", g)])
                    P.cp("act", init[:, g:g + 1], pC[:, g:g + 1], [("pC", g)], [("init", g)])
                    if d == 0:
                        P.cp("act", Yacc[:, g, blk], pY[yb][:, 0:L], [("pY", yb)], [("Yacc", g, ci)])
                    else:
                        P.tt("dve", Yacc[:, g, blk], pY[yb][:, 0:L], Yacc[:, g, blk], ALU.add, [("pY", yb), ("Yacc", g, ci)], [("Yacc", g, ci)])

                NI = len(its)
                for n in range(NI + 2):
                    if n < NI:
                        stage1(n)
                    if 1 <= n <= NI:
                        stage2(n - 1)
                    if n >= 2:
                        stage3(n - 2)
            CPU_ = UB // L
            for ub in range(NUB):
                cols = slice(ub * UB, (ub + 1) * UB)
                for g in range(8):
                    P.act(ygU[:, g, cols], Yacc[:, g, cols], AF.Gelu, [("Yacc", g, ci_) for ci_ in range(ub * CPU_, (ub + 1) * CPU_)], ["ut"])
                for j in range(8):
                    ps = alt[acount % 2]
                    pk = ("alt", acount % 2)
                    acount += 1
                    for g in range(8):
                        P.mm(ps[:, 0:UB], selbig[:, j, 112 - 16 * g:240 - 16 * g], ygU[:, g, cols], g == 0, g == 7, ["ut", "selbig"], [pk])
                    P.cp("act" if j % 2 == 0 else "dve", ytile[:, slice(j, UB * 8, 8)], ps[:, 0:UB], [pk], ["ytile"])
                P.dma("sp", X.ygT[ct, :, ub * UB * 8:(ub + 1) * UB * 8], ytile[:], ["ytile"], [("ygT", ct, ub)])
        P.barrier()
        P.emit()


def phaseC(X, l, xsrc):
    nc, P, S = X.nc, X.P, X.S
    I = X.ins
    NB = S // 512
    with ExitStack() as es:
        def T(name, shape, dt=F32):
            return sbt(es, nc, "pc_" + name, shape, dt)
        stage = [T(f"stage{i}", [128, 2048]) for i in range(3)]
        gw = load_weight_bf16(X, es, "pc_gw", I["glu_w"][l], 4, 512, "gw", stage)
        wo = load_weight_bf16(X, es, "pc_wo", I["w_out"][l], 8, 1024, "wo", stage)
        gb = T("gb", [128, 4])
        gsn = T("gsn", [128, 4])
        P.dma("sp", gb[:], colvec(I["glu_b"][l, :], 4), [], ["gb"], slow=True)
        P.dma("sp", gsn[:], colvec(I["ssm_norm_g"][l, :], 4), [], ["gsn"], slow=True)
        zt = T("zt", [128, 8, 1], BF16)
        P.memset("dve", zt[:], 0.0, ["zt"])
        P.dma("sp", X.h2T[:, :, 0:1].rearrange("c p t -> p c t"), zt[:], ["zt"], ["h2z0"], slow=True)
        P.dma("sp", X.h2T[:, :, S + 1:S + 2].rearrange("c p t -> p c t"), zt[:], ["zt"], ["h2z1"], slow=True)
        mix = [T(f"mix{i}", [128, 8, 512], BF16) for i in range(2)]
        yg = [T(f"yg{i}", [128, 4, 512], BF16) for i in range(2)]
        sig = [T(f"sig{i}", [128, 512]) for i in range(2)]
        y2 = T("y2", [128, 4, 512])
        sq = [T(f"sq{i}", [128, 512]) for i in range(2)]
        rstd = T("rstd", [128, 512])
        xt = [T(f"xt{i}", [128, 1024]) for i in range(6)]
        tmp = [T(f"tmp{i}", [128, 1024]) for i in range(2)]
        junk = T("junk", [128, 1024], BF16)
        xn = [T(f"xn{i}", [128, 1024], BF16) for i in range(2)]
        h2 = [T(f"h2{i}", [128, 8, 128], BF16) for i in range(2)]
        ss = T("ss", [128, 8])
        rs = T("rs", [128, 8])
        pG = [pst(es, nc, f"pc_pG{i}", [128, 512]) for i in range(2)]
        pSS = pst(es, nc, "pc_pSS", [128, 512])
        pW = [pst(es, nc, f"pc_pW{i}", [128, 512]) for i in range(2)]
        pT = [pst(es, nc, f"pc_pT{i}", [128, 8, 128], BF16) for i in range(2)]
        ident = X.cst["ident_bf"]
        ones = X.cst["ones_f"]
        gm, mT, g1bc = X.gm2[l], X.modT[l], X.g1bc[l]
        st = {"g": 0, "w": 0}

        def stageG(nb):
            b2 = nb % 2
            tok = slice(nb * 512, (nb + 1) * 512)
            P.dma("sp", yg[b2][:], X.ygT[:, :, tok].rearrange("c p t -> p c t"), [], [("yg", b2)])
            P.dma("pool", mix[b2][:, 0:4, :], X.yaT[:, :, tok].rearrange("c p t -> p c t"), [], [("mixa", b2)])
            for m in range(4):
                gbf = st["g"] % 2
                st["g"] += 1
                for k in range(4):
                    P.mm(pG[gbf][:], gw[:, k, m * 128:(m + 1) * 128], yg[b2][:, k, :], k == 0, k == 3, [("gw", k), ("yg", b2)], [("pG", gbf)])
                P.act(sig[gbf][:], pG[gbf][:], AF.Sigmoid, [("pG", gbf), "gb"], [("sig", gbf)], bias=gb[:, m:m + 1])
                P.tt("dve", y2[:, m, :], yg[b2][:, m, :], sig[gbf][:], ALU.mult, [("yg", b2), ("sig", gbf)], [("y2", m)])
                P.act(sq[gbf][:], y2[:, m, :], AF.Square, [("y2", m)], [("sq", gbf)])
                P.mm(pSS[:], ones[:], sq[gbf][:], m == 0, m == 3, [("sq", gbf), "ones_f"], ["pSS"])
            P.act(rstd[:], pSS[:], AF.Sqrt, ["pSS"], ["rstd"], bias=X.epsc[:, 0:1], scale=1.0 / 512)
            P.recip(rstd[:], rstd[:], ["rstd"], ["rstd"])
            for m in range(4):
                P.stt(mix[b2][:, 4 + m, :], y2[:, m, :], gsn[:, m:m + 1], rstd[:], ALU.mult, ALU.mult, [("y2", m), "gsn", "rstd"], [("mixs", b2, m)])

        def stage1(nb, tl):
            b2 = nb % 2
            tn = nb * 4 + tl
            xb = tn % 6
            t0 = nb * 512 + tl * 128
            mk = [("mixa", b2)] + [("mixs", b2, m) for m in range(4)]
            P.dma("sp", xt[xb][:], xsrc[t0:t0 + 128, :], [], [("xt", xb)])
            for hf in range(2):
                wb = st["w"] % 2
                st["w"] += 1
                for k in range(8):
                    P.mm(pW[wb][:], mix[b2][:, k, tl * 128:(tl + 1) * 128], wo[:, k, hf * 512:(hf + 1) * 512], k == 0, k == 7,
                         mk + [("wo", k)], [("pW", wb)])
                P.tt("dve", tmp[tn % 2][:, hf * 512:(hf + 1) * 512], pW[wb][:], g1bc[:, hf * 512:(hf + 1) * 512], ALU.mult,
                     [("pW", wb), ("g1bc", l)], [("tmp", tn % 2, hf)])
                P.tt("pool", xt[xb][:, hf * 512:(hf + 1) * 512], tmp[tn % 2][:, hf * 512:(hf + 1) * 512], xt[xb][:, hf * 512:(hf + 1) * 512], ALU.add,
                     [("tmp", tn % 2, hf), ("xt", xb)], [("xt", xb)])
            P.dma("pool", X.x1[t0:t0 + 128, :], xt[xb][:], [("xt", xb)], [("x1", t0)])

        def stage2(nb, tl):
            tn = nb * 4 + tl
            xb = tn % 6
            x2 = tn % 2
            sc = tn % 8
            t0 = nb * 512 + tl * 128
            P.act(junk[:], xt[xb][:], AF.Square, [("xt", xb)], ["junk", ("ss", sc)], accum=ss[:, sc:sc + 1])
            P.act(rs[:, sc:sc + 1], ss[:, sc:sc + 1], AF.Sqrt, [("ss", sc)], [("rs", sc)], bias=X.epsc[:, 0:1], scale=1.0 / D)
            P.recip(rs[:, sc:sc + 1], rs[:, sc:sc + 1], [("rs", sc)], [("rs", sc)])
            P.ts("dve", xn[x2][:], xt[xb][:], rs[:, sc:sc + 1], None, ALU.mult, None, [("xt", xb), ("rs", sc)], [("xn", x2)])
            for c in range(8):
                P.tr(pT[x2][:, c, :], xn[x2][:, c * 128:(c + 1) * 128], ident[:], [("xn", x2), "ident"], [("pT", x2)])
            for c in range(8):
                if x2 == 0:
                    P.act(h2[x2][:, c, :], pT[x2][:, c, :], AF.Identity, [("pT", x2), ("gm2", l), ("modT", l)], [("h2", x2)],
                          bias=mT[:, 24 + c:25 + c], scale=gm[:, c:c + 1])
                else:
                    P.ts("dve", h2[x2][:, c, :], pT[x2][:, c, :], gm[:, c:c + 1], mT[:, 24 + c:25 + c], ALU.mult, ALU.add,
                         [("pT", x2), ("gm2", l), ("modT", l)], [("h2", x2)])
            P.dma("sp", X.h2T[:, :, 1 + t0:1 + t0 + 128].rearrange("c p t -> p c t"), h2[x2][:], [("h2", x2)], [("h2T", t0)])

        tiles = [(nb, tl) for nb in range(NB) for tl in range(4)]
        SK = 3
        stageG(0)
        for i, (nb, tl) in enumerate(tiles):
            stage1(nb, tl)
            if tl == 1 and nb + 1 < NB:
                stageG(nb + 1)
            if i >= SK:
                stage2(*tiles[i - SK])
        for i in range(max(0, len(tiles) - SK), len(tiles)):
            stage2(*tiles[i])
        P.barrier()
        P.emit()


def phaseF(X, l, xdst):
    nc, P, S = X.nc, X.P, X.S
    I = X.ins
    NB = S // 512
    HP = NFF // 2
    for hp in range(2):
        with ExitStack() as es:
            def T(name, shape, dt=F32):
                return sbt(es, nc, f"pf{hp}_" + name, shape, dt)
            stage = [T(f"stage{i}", [128, 2048]) for i in range(3)]
            wu = T("wu", [128, 8, 2 * HP * 128], BF16)
            for k in range(8):
                for part in range(2):
                    c0 = part * DFF + hp * HP * 128
                    i = X.stage_i
                    X.stage_i += 1
                    st = stage[i % 3]
                    P.dma("sp" if i % 2 == 0 else "pool", st[:, 0:HP * 128], I["w_up"][l, k * 128:(k + 1) * 128, c0:c0 + HP * 128], [], [("stage", i % 3)])
                    P.cp(("dve", "pool", "act")[i % 3], wu[:, k, part * HP * 128:(part + 1) * HP * 128], st[:, 0:HP * 128], [("stage", i % 3)], [("wu", k)])
            wd = load_weight_bf16(X, es, f"pf{hp}_wd", I["w_down"][l, hp * HP * 128:(hp + 1) * HP * 128, :], HP, 1024, "wd", stage)
            cw = T("cw", [128, 3, 44])
            cb = T("cb", [128, 44])
            for c0 in range(0, 44, 11):
                for t in range(3):
                    P.dma("sp", cw[:, t, c0:c0 + 11], colvec(I["conv_w"][l, t, :], 44)[:, c0:c0 + 11], [], ["cw"], slow=True)
                P.dma("sp", cb[:, c0:c0 + 11], colvec(I["conv_b"][l, :], 44)[:, c0:c0 + 11], [], ["cb"], slow=True)
            h2 = [T(f"h2{i}", [128, 8, 512], BF16) for i in range(2)]
            cva = [T(f"cva{i}", [128, 512]) for i in range(2)]
            cvg = [T(f"cvg{i}", [128, 512]) for i in range(2)]
            sg = [T(f"sg{i}", [128, 512]) for i in range(2)]
            hid = [T(f"hid{i}", [128, HP, 512], BF16) for i in range(2)]
            xt = [T(f"xt{i}", [128, 1024]) for i in range(3)]
            tmp = [T(f"tmp{i}", [128, 1024]) for i in range(2)]
            pA = [pst(es, nc, f"pf{hp}_pA{i}", [128, 512]) for i in range(2)]
            pGt = [pst(es, nc, f"pf{hp}_pG{i}", [128, 512]) for i in range(2)]
            pW = [pst(es, nc, f"pf{hp}_pW{i}", [128, 512]) for i in range(2)]
            g2bc = X.g2bc[l]
            xin = X.x1 if hp == 0 else xdst
            icount = 0
            wcount = 0
            tcount = 0
            BT = 510
            blocks = [(t0, min(BT, S - t0)) for t0 in range(0, S, BT)]
            for nb, (t0, nt) in enumerate(blocks):
                b2 = nb % 2
                N = nt + 2
                P.dma("sp", h2[b2][:, :, 0:N], X.h2T[:, :, t0:t0 + N].rearrange("c p t -> p c t"), [], [("h2", b2)])
                for i in range(HP):
                    ib = icount % 2
                    icount += 1
                    for part, (pp, cv) in enumerate(((pA[ib], cva[ib]), (pGt[ib], cvg[ib]))):
                        col = part * 22 + hp * HP + i
                        wc = slice(part * HP * 128 + i * 128, part * HP * 128 + (i + 1) * 128)
                        for k in range(8):
                            P.mm(pp[:, 0:N], wu[:, k, wc], h2[b2][:, k, 0:N], k == 0, k == 7, [("wu", k), ("h2", b2)], [("pp", ib, part)])
                        ck = ("cv", ib, part)
                        P.act(cv[:, 0:nt], pp[:, 1:nt + 1], AF.Identity, [("pp", ib, part), "cw", "cb"], [ck], bias=cb[:, col:col + 1], scale=cw[:, 1, col:col + 1])
                        P.stt(cv[:, 0:nt], pp[:, 0:nt], cw[:, 0, col:col + 1], cv[:, 0:nt], ALU.mult, ALU.add, [("pp", ib, part), "cw", ck], [ck])
                        P.stt(cv[:, 0:nt], pp[:, 2:nt + 2], cw[:, 2, col:col + 1], cv[:, 0:nt], ALU.mult, ALU.add, [("pp", ib, part), "cw", ck], [ck])
                    P.act(sg[ib][:, 0:nt], cvg[ib][:, 0:nt], AF.Silu, [("cv", ib, 1)], [("sg", ib)])
                    P.tt("pool", hid[b2][:, i, 0:nt], sg[ib][:, 0:nt], cva[ib][:, 0:nt], ALU.mult, [("sg", ib), ("cv", ib, 0)], [("hid", b2, i)])
                hk = [("hid", b2, i) for i in range(HP)]
                for tl in range((nt + 127) // 128):
                    m = min(128, nt - tl * 128)
                    r0 = t0 + tl * 128
                    xb = tcount % 3
                    tb_ = tcount % 2
                    tcount += 1
                    P.dma("sp", xt[xb][0:m, :], xin[r0:r0 + m, :], [("xd", r0)], [("xt", xb)])
                    for hf in range(2):
                        wb = wcount % 2
                        wcount += 1
                        for i in range(HP):
                            P.mm(pW[wb][0:m, :], hid[b2][:, i, tl * 128:tl * 128 + m], wd[:, i, hf * 512:(hf + 1) * 512], i == 0, i == HP - 1,
                                 hk + [("wd", i)], [("pW", wb)])
                        P.tt("dve", tmp[tb_][0:m, hf * 512:(hf + 1) * 512], pW[wb][0:m, :], g2bc[0:m, hf * 512:(hf + 1) * 512], ALU.mult,
                             [("pW", wb), ("g2bc", l)], [("tmp", tb_, hf)])
                        P.tt("pool", xt[xb][0:m, hf * 512:(hf + 1) * 512], tmp[tb_][0:m, hf * 512:(hf + 1) * 512], xt[xb][0:m, hf * 512:(hf + 1) * 512], ALU.add,
                             [("tmp", tb_, hf), ("xt", xb)], [("xt", xb)])
                    P.dma("pool", xdst[r0:r0 + m, :], xt[xb][0:m, :], [("xt", xb)], [("xd", r0)])
            P.barrier()
            P.emit()


def build(S, debug=None, nlayers=DEPTH, phases=None, cut=0):
    nc = bass.Bass("TRN2", target_bir_lowering=False)
    X = Ctx()
    X.cut = cut
    import os
    X.evac = os.environ.get('EVAC', 'both')
    X.rowtile = os.environ.get('ROWTILE', '0') == '1'
    X.s8 = os.environ.get('S8', '1') == '1'
    X.nc, X.S = nc, S
    X.stage_i = 0
    dbg = set(debug or ())

    def din(name, shape, dt=F32):
        return nc.dram_tensor(name, list(shape), dt, kind="ExternalInput").ap()

    def dscr(name, shape, dt):
        kind = "ExternalOutput" if name in dbg else "Internal"
        return nc.dram_tensor(name, list(shape), dt, kind=kind).ap()

    X.ins = {"x": din("x", [S, D]), "pos": din("pos", [S], I32)}
    for k, shp in IN_SHAPES.items():
        X.ins[k] = din(k, shp)
    cin = {k: din("cst_" + k, shp, dt) for k, (shp, dt) in CONST_SHAPES.items()}
    X.out = nc.dram_tensor("out", [S, D], F32, kind="ExternalOutput").ap()
    X.modrow = dscr("modrow", [2, 6144], F32)
    X.cosT = dscr("cosT", [128, S], F32)
    X.sinT = dscr("sinT", [128, S], F32)
    X.qT = dscr("qT", [4, 128, S], BF16)
    X.kT = dscr("kT", [4, 128, S], BF16)
    X.vv = dscr("vv", [S, 512], BF16)
    X.uT = dscr("uT", [4, 128, S], BF16)
    X.ygT = dscr("ygT", [4, 128, S], BF16)
    X.yaT = dscr("yaT", [4, 128, S], BF16)
    X.x1 = dscr("x1", [S, D], F32)
    X.h2T = dscr("h2T", [8, 128, S + 2], BF16)
    X.xmid = dscr("xmid", [S, D], F32)
    with ExitStack() as es:
        P = Prog(nc, es)
        X.P = P
        X.cst = {}
        for k, (shp, dt) in CONST_SHAPES.items():
            X.cst[k] = sbt(es, nc, "c_" + k, shp, dt)
            P.dma("sp", X.cst[k][:], cin[k], [], [k])
        X.epsc = sbt(es, nc, "c_eps", [128, 1], F32)
        P.memset("dve", X.epsc[:], EPS, ["epsc"])
        X.modT = [sbt(es, nc, f"modT{l}", [128, 48], F32) for l in range(DEPTH)]
        X.g1bc = [sbt(es, nc, f"g1bc{l}", [128, 1024], F32) for l in range(DEPTH)]
        X.g2bc = [sbt(es, nc, f"g2bc{l}", [128, 1024], F32) for l in range(DEPTH)]
        X.gm1 = [sbt(es, nc, f"gm1{l}", [128, 8], F32) for l in range(DEPTH)]
        X.gm2 = [sbt(es, nc, f"gm2{l}", [128, 8], F32) for l in range(DEPTH)]
        P.barrier()
        phases = phases or ("0", "A", "B", "S", "C", "F")
        if "0" in phases:
            phase0(X)
        for l in range(nlayers):
            xsrc = X.ins["x"] if l == 0 else X.xmid
            xdst = X.xmid if l < DEPTH - 1 else X.out
            if "A" in phases:
                phaseA(X, l, xsrc)
            if "B" in phases:
                phaseB(X, l)
            if "S" in phases:
                if X.s8:
                    phaseS8(X, l)
                else:
                    phaseS(X, l)
            if "C" in phases:
                phaseC(X, l, xsrc)
            if "F" in phases:
                phaseF(X, l, xdst)
        P.barrier()
        P.emit(final=True)
    X.ninstr = P.ninstr
    return nc, X


_CACHE = {}


def kernel(**inputs):
    x = np.asarray(inputs["x"])
    B, S, _ = x.shape
    if S not in _CACHE:
        _CACHE[S] = build(S)[0]
    nc = _CACHE[S]
    consts = make_consts()
    n_cores = 8
    in_maps = []
    for i in range(n_cores):
        b = i % B
        m = {"x": np.ascontiguousarray(x[b]).astype(np.float32),
             "pos": np.ascontiguousarray(np.asarray(inputs["positions"])[b]).astype(np.int32),
             "c": np.ascontiguousarray(np.asarray(inputs["c"])[b]).astype(np.float32)}
        for k in IN_SHAPES:
            if k != "c":
                m[k] = np.ascontiguousarray(np.asarray(inputs[k])).astype(np.float32)
        for k, v in consts.items():
            m["cst_" + k] = v
        in_maps.append(m)
    res = run_bass_kernel_spmd(nc, in_maps, core_ids=list(range(n_cores)))
    out = np.stack([np.asarray(res.results[b]["out"]) for b in range(B)], axis=0)
    return out.astype(np.float32)
```

```python
import math
from contextlib import ExitStack
import numpy as np
import ml_dtypes
import concourse.bass as bass
import concourse.mybir as mybir
from concourse.bass_utils import run_bass_kernel_spmd

F32 = mybir.dt.float32
BF16 = mybir.dt.bfloat16
I32 = mybir.dt.int32
AF = mybir.ActivationFunctionType
ALU = mybir.AluOpType
AX = mybir.AxisListType

D = 1024
DEPTH = 2
DFF = 2816
NFF = DFF // 128
EPS = 1e-6
TWO_PI = 2.0 * math.pi
EPOCH = 30000
LCH = 512


class Prog:
    ENGS = ("pe", "act", "dve", "pool", "sp")

    def __init__(self, nc, es):
        self.nc = nc
        self.es = es
        self.nsem = 0
        self.ops = {e: [] for e in self.ENGS}
        self.cnt = {e: 0 for e in self.ENGS}
        self.sem = {e: self._newsem() for e in self.ENGS}
        self.pesems = {id(self.sem["pe"])}
        self.known = {e: {} for e in self.ENGS}
        self.lastw = {}
        self.readers = {}
        self.pend = {e: [] for e in self.ENGS}
        self.dpool = {e: [self._newsem() for _ in range(6)] for e in ("sp", "pool", "act")}
        self.dval = {e: [0] * 6 for e in self.dpool}
        self.drr = {e: 0 for e in self.dpool}
        self.semobj = {}
        self.ninstr = 0
        self.banklast = {}

    def _newsem(self):
        self.nsem += 1
        return self.es.enter_context(self.nc.semaphore(f"s{self.nsem}"))

    def _banks(self, *aps):
        ks = []
        for a in aps:
            if a is None or isinstance(a, (int, float)):
                continue
            if type(a.tensor).__name__ == "PSumTensorHandle":
                ks.append(a.name)
        return ks

    def op(self, eng, fn, r=(), w=(), dma=False, x=()):
        deps = list(self.pend[eng])
        self.pend[eng] = []
        for k in x:
            t = self.banklast.get(k)
            if t is not None and t[0] != eng:
                deps.append(t[1])
        for k in list(r) + list(w):
            t = self.lastw.get(k)
            if t is not None:
                deps.append(t)
        for k in w:
            deps.extend(self.readers.get(k, ()))
        if dma:
            i = self.drr[eng]
            self.drr[eng] = (i + 1) % len(self.dpool[eng])
            s = self.dpool[eng][i]
            if self.dval[eng][i] > 0:
                deps.append((s, self.dval[eng][i]))
            self.dval[eng][i] += 16
            tok = (s, self.dval[eng][i])
            amt = 16
        else:
            if self.cnt[eng] >= EPOCH:
                self.sem[eng] = self._newsem()
                self.cnt[eng] = 0
                if eng == "pe":
                    self.pesems.add(id(self.sem[eng]))
            self.cnt[eng] += 1
            tok = (self.sem[eng], self.cnt[eng])
            amt = 1
        waits = {}
        kn = self.known[eng]
        for (s, v) in deps:
            if eng == "pe" and id(s) in self.pesems:
                continue
            if v <= kn.get(id(s), 0):
                continue
            if id(s) not in waits or waits[id(s)][1] < v:
                waits[id(s)] = (s, v)
        for i_, (s, v) in waits.items():
            kn[i_] = v
        self.ops[eng].append((list(waits.values()), fn, tok[0], amt))
        self.ninstr += 1
        for k in x:
            self.banklast[k] = (eng, tok)
        for k in w:
            self.lastw[k] = tok
            self.readers[k] = []
        for k in r:
            self.readers.setdefault(k, []).append(tok)
        return tok

    def barrier(self):
        toks = []
        for e in self.ENGS:
            if self.cnt[e] > 0:
                toks.append((self.sem[e], self.cnt[e]))
        for e in self.dpool:
            for s, v in zip(self.dpool[e], self.dval[e]):
                if v > 0:
                    toks.append((s, v))
        for e in self.ENGS:
            self.pend[e].extend(toks)
        self.lastw = {}
        self.readers = {}
        self.banklast = {}

    def emit(self, final=False):
        nc = self.nc
        with nc.Block() as block:
            decos = {"pe": block.tensor, "act": block.scalar, "dve": block.vector,
                     "pool": block.gpsimd, "sp": block.sync}
            for eng in self.ENGS:
                ops = self.ops[eng]
                tail = []
                if final:
                    kn = self.known[eng]
                    for (s, v) in self.pend[eng]:
                        if eng == "pe" and id(s) in self.pesems:
                            continue
                        if v > kn.get(id(s), 0):
                            kn[id(s)] = v
                            tail.append((s, v))
                    self.pend[eng] = []

                def body(e, ops=ops, tail=tail):
                    for waits, fn, s, amt in ops:
                        for (ws, wv) in waits:
                            e.wait_ge(ws, wv)
                        fn(e).then_inc(s, amt)
                    for (ws, wv) in tail:
                        e.wait_ge(ws, wv)
                decos[eng](body)
        self.ops = {e: [] for e in self.ENGS}

    def mm(self, out, lhsT, rhs, start, stop, r, w, sgc=False):
        if sgc:
            return self.op("pe", lambda e: e.matmul(out, lhsT=lhsT, rhs=rhs, start=start, stop=stop, skip_group_check=True), r, w, x=self._banks(out))
        return self.op("pe", lambda e: e.matmul(out, lhsT=lhsT, rhs=rhs, start=start, stop=stop), r, w, x=self._banks(out))

    def tr(self, out, in_, ident, r, w):
        return self.op("pe", lambda e: e.transpose(out, in_, ident), r, w, x=self._banks(out))

    def act(self, out, in_, func, r, w, bias=None, scale=None, accum=None):
        kw = {}
        if bias is not None:
            kw["bias"] = bias
        if scale is not None:
            kw["scale"] = scale
        if accum is not None:
            kw["accum_out"] = accum
        return self.op("act", lambda e: e.activation(out=out, in_=in_, func=func, **kw), r, w, x=self._banks(out, in_, bias, scale))

    def ts(self, eng, out, in0, s1, s2, op0, op1, r, w):
        if op1 is None:
            return self.op(eng, lambda e: e.tensor_scalar(out=out, in0=in0, scalar1=s1, scalar2=None, op0=op0), r, w, x=self._banks(out, in0, s1))
        return self.op(eng, lambda e: e.tensor_scalar(out=out, in0=in0, scalar1=s1, scalar2=s2, op0=op0, op1=op1), r, w, x=self._banks(out, in0, s1, s2))

    def stt(self, out, in0, scalar, in1, op0, op1, r, w):
        return self.op("dve", lambda e: e.scalar_tensor_tensor(out=out, in0=in0, scalar=scalar, in1=in1, op0=op0, op1=op1), r, w, x=self._banks(out, in0, scalar, in1))

    def tt(self, eng, out, in0, in1, op, r, w):
        return self.op(eng, lambda e: e.tensor_tensor(out=out, in0=in0, in1=in1, op=op), r, w, x=self._banks(out, in0, in1))

    def cp(self, eng, out, in_, r, w):
        if eng == "act":
            return self.op("act", lambda e: e.copy(out=out, in_=in_), r, w, x=self._banks(out, in_))
        return self.op(eng, lambda e: e.tensor_copy(out=out, in_=in_), r, w, x=self._banks(out, in_))

    def recip(self, out, in_, r, w):
        return self.op("dve", lambda e: e.reciprocal(out=out, in_=in_), r, w, x=self._banks(out, in_))

    def memset(self, eng, ap, val, w):
        return self.op(eng, lambda e: e.memset(ap, val), (), w)

    def dma(self, q, out, in_, r, w, slow=False):
        if slow:
            return self.op(q, lambda e: e.dma_start(out=out, in_=in_, allow_slow_non_contiguous=True), r, w, dma=True)
        return self.op(q, lambda e: e.dma_start(out=out, in_=in_), r, w, dma=True)


def _rr(lst, i):
    return lst[i % len(lst)]


def make_consts():
    c = {}
    c["ident_bf"] = np.eye(128, dtype=np.float32).astype(ml_dtypes.bfloat16)
    c["ident_f"] = np.eye(128, dtype=np.float32)
    bo = np.zeros((128, 128), np.float32)
    bo[:64, :64] = 1.0
    bo[64:, 64:] = 1.0
    c["blockones"] = bo
    c["ones_f"] = np.ones((128, 128), np.float32)
    js = np.zeros((128, 128), np.float32)
    for k in range(128):
        js[k, (k + 64) % 128] = 1.0
    c["jswap"] = js
    inv = (np.float32(10000.0) ** (-(np.arange(0, 64, 2, dtype=np.float32)) / np.float32(64))).astype(np.float32)
    invf = np.zeros((128, 1), np.float32)
    for p in range(128):
        invf[p, 0] = inv[(p % 64) % 32]
    c["invf"] = invf
    gm = np.zeros((128, 8), np.float32)
    for p in range(128):
        gm[p, p // 16] = 1.0
    c["gmask"] = gm
    c["jrow"] = np.tile(np.arange(LCH, dtype=np.float32)[None, :], (128, 1))
    rm = np.zeros((128, 128), np.float32)
    for m_ in range(128):
        if (m_ % 64) < 32:
            rm[m_ + 32, m_] = -1.0
        else:
            rm[m_ - 32, m_] = 1.0
    c["rotmat"] = rm
    sb = np.zeros((128, 8, 240), np.float32)
    for x_ in range(8):
        for p in range(16):
            sb[16 * x_ + p, x_, 112 + p] = 1.0
    c["selbig"] = sb.astype(ml_dtypes.bfloat16)
    nm = np.zeros((128, 2, 128), np.float32)
    for r_ in range(128):
        for c_ in range(128):
            jp, jj = r_ // 16, c_ // 16
            if jp > jj:
                nm[r_, 0, c_] = -1.0
            if jp < jj:
                nm[r_, 1, c_] = -1.0
    c["nmask"] = nm
    return c


CONST_SHAPES = {"ident_bf": ([128, 128], BF16), "ident_f": ([128, 128], F32), "blockones": ([128, 128], F32),
                "ones_f": ([128, 128], F32), "jswap": ([128, 128], F32), "invf": ([128, 1], F32),
                "gmask": ([128, 8], F32), "jrow": ([128, LCH], F32),
                "selbig": ([128, 8, 240], BF16), "nmask": ([128, 2, 128], F32),
                "rotmat": ([128, 128], F32)}

IN_SHAPES = {
    "c": [1024], "ada_w": [2, 1024, 6144], "ada_b": [2, 6144], "norm1_g": [2, 1024],
    "w_in": [2, 1024, 2048], "q_norm_g": [2, 64], "k_norm_g": [2, 64], "lam_q1": [2, 64], "lam_k1": [2, 64],
    "lam_q2": [2, 64], "lam_k2": [2, 64], "subln_g": [2, 128], "ssm_a_re": [2, 2, 32, 64],
    "ssm_a_im": [2, 2, 32, 64], "ssm_log_dt": [2, 2, 32], "ssm_b_re": [2, 2, 32, 64, 16],
    "ssm_b_im": [2, 2, 32, 64, 16], "ssm_c_re": [2, 2, 32, 16, 64], "ssm_c_im": [2, 2, 32, 16, 64],
    "ssm_d": [2, 512], "glu_w": [2, 512, 512], "glu_b": [2, 512], "ssm_norm_g": [2, 512],
    "w_out": [2, 1024, 1024], "norm2_g": [2, 1024], "w_up": [2, 1024, 5632], "conv_w": [2, 3, 5632],
    "conv_b": [2, 5632], "w_down": [2, 2816, 1024],
}


class Ctx:
    pass


_UNIQ = [0]


def sbt(es, nc, name, shape, dt):
    _UNIQ[0] += 1
    return es.enter_context(nc.sbuf_tensor(f"{name}_{_UNIQ[0]}", list(shape), dt))


def pst(es, nc, name, shape, dt=F32):
    _UNIQ[0] += 1
    return es.enter_context(nc.psum_tensor(f"{name}_{_UNIQ[0]}", list(shape), dt))


def colvec(ap1d, n):
    return ap1d.rearrange("(c p) -> p c", p=128)


def phase0(X):
    nc, P, S = X.nc, X.P, X.S
    I = X.ins
    with ExitStack() as es:
        ct = sbt(es, nc, "p0_ct", [128, 8], F32)
        cond = sbt(es, nc, "p0_cond", [128, 8], F32)
        aw = [sbt(es, nc, f"p0_aw{i}", [128, 8, 512], F32) for i in range(2)]
        abr = sbt(es, nc, "p0_abr", [1, 6144], F32)
        mrow = sbt(es, nc, "p0_mrow", [1, 6144], F32)
        psr = [pst(es, nc, f"p0_ps{i}", [1, 512]) for i in range(2)]
        P.dma("sp", ct[:], colvec(I["c"], 8), [], ["ct"], slow=True)
        P.act(cond[:], ct[:], AF.Silu, ["ct"], ["cond"])
        for l in range(DEPTH):
            P.dma("pool", abr[:], I["ada_b"][l:l + 1, :], [], ["abr"])
            for n in range(12):
                b = n % 2
                P.dma("sp", aw[b][:], I["ada_w"][l, :, n * 512:(n + 1) * 512].rearrange("(k p) n -> p k n", p=128),
                      [], [("aw", b)])
                for k in range(8):
                    P.mm(psr[b][:], cond[:, k:k + 1], aw[b][:, k, :], k == 0, k == 7, ["cond", ("aw", b)], [("psr", b)])
                P.tt("dve", mrow[:, n * 512:(n + 1) * 512], psr[b][:], abr[:, n * 512:(n + 1) * 512], ALU.add,
                     [("psr", b), "abr"], [("mrow", n)])
            P.dma("sp", X.modrow[l:l + 1, :], mrow[:], [("mrow", n) for n in range(12)], [("modrow", l)])
            for c0 in range(0, 48, 16):
                P.dma("sp", X.modT[l][:, c0:c0 + 16], X.modrow[l, :].rearrange("(j p) -> p j", p=128)[:, c0:c0 + 16], [("modrow", l)], [("modT", l)], slow=True)
            P.dma("pool", X.g1bc[l][:], X.modrow[l, 2048:3072].partition_broadcast(128), [("modrow", l)], [("g1bc", l)])
            P.dma("pool", X.g2bc[l][:], X.modrow[l, 5120:6144].partition_broadcast(128), [("modrow", l)], [("g2bc", l)])
            n1 = sbt(es, nc, f"p0_n1_{l}", [128, 8], F32)
            n2 = sbt(es, nc, f"p0_n2_{l}", [128, 8], F32)
            P.dma("sp", n1[:], colvec(I["norm1_g"][l, :], 8), [], [("n1", l)], slow=True)
            P.dma("sp", n2[:], colvec(I["norm2_g"][l, :], 8), [], [("n2", l)], slow=True)
            P.stt(X.gm1[l][:], X.modT[l][:, 8:16], 1.0, n1[:], ALU.add, ALU.mult, [("modT", l), ("n1", l)], [("gm1", l)])
            P.stt(X.gm2[l][:], X.modT[l][:, 32:40], 1.0, n2[:], ALU.add, ALU.mult, [("modT", l), ("n2", l)], [("gm2", l)])
        CW = min(S, 2048)
        posi = sbt(es, nc, "p0_posi", [128, CW], I32)
        posf = sbt(es, nc, "p0_posf", [128, CW], F32)
        ang = sbt(es, nc, "p0_ang", [128, CW], F32)
        kf = sbt(es, nc, "p0_kf", [128, CW], F32)
        ki = sbt(es, nc, "p0_ki", [128, CW], I32)
        rr = sbt(es, nc, "p0_r", [128, CW], F32)
        tmp = sbt(es, nc, "p0_tmp", [128, CW], F32)
        r2 = sbt(es, nc, "p0_r2", [128, CW], F32)
        outs = sbt(es, nc, "p0_outs", [128, CW], F32)
        outc = sbt(es, nc, "p0_outc", [128, CW], F32)
        invf = X.cst["invf"]
        C1 = 6.28125
        C2 = TWO_PI - 6.28125
        for ci in range(S // CW):
            sl = slice(ci * CW, (ci + 1) * CW)
            P.dma("sp", posi[:], I["pos"][sl].partition_broadcast(128), [], ["posi"])
            P.cp("dve", posf[:], posi[:], ["posi"], ["posf"])
            P.ts("dve", ang[:], posf[:], invf[:, 0:1], None, ALU.mult, None, ["posf", "invf"], ["ang"])
            P.ts("dve", kf[:], ang[:], 1.0 / TWO_PI, None, ALU.mult, None, ["ang"], ["kf"])
            P.cp("dve", ki[:], kf[:], ["kf"], ["ki"])
            P.cp("dve", kf[:], ki[:], ["ki"], ["kf"])
            P.stt(rr[:], kf[:], -C1, ang[:], ALU.mult, ALU.add, ["kf", "ang"], ["rr"])
            P.stt(rr[:], kf[:], -C2, rr[:], ALU.mult, ALU.add, ["kf", "rr"], ["rr"])
            P.ts("dve", tmp[:], rr[:], math.pi, -TWO_PI, ALU.is_gt, ALU.mult, ["rr"], ["tmp"])
            P.tt("dve", rr[:], rr[:], tmp[:], ALU.add, ["rr", "tmp"], ["rr"])
            P.ts("dve", r2[:], rr[:], math.pi / 2, None, ALU.add, None, ["rr"], ["r2"])
            P.ts("dve", tmp[:], r2[:], math.pi, -TWO_PI, ALU.is_gt, ALU.mult, ["r2"], ["tmp"])
            P.tt("dve", r2[:], r2[:], tmp[:], ALU.add, ["r2", "tmp"], ["r2"])
            P.ts("dve", rr[:], rr[:], -math.pi, math.pi, ALU.max, ALU.min, ["rr"], ["rr"])
            P.ts("dve", r2[:], r2[:], -math.pi, math.pi, ALU.max, ALU.min, ["r2"], ["r2"])
            P.act(outs[:], rr[:], AF.Sin, ["rr"], ["outs"])
            P.act(outc[:], r2[:], AF.Sin, ["r2"], ["outc"])
            P.dma("sp", X.sinT[:, sl], outs[:], ["outs"], ["sinT"])
            P.dma("sp", X.cosT[:, sl], outc[:], ["outc"], ["cosT"])
        P.barrier()
        P.emit()


def load_weight_bf16(X, es, name, dram_ap, K, N, tag, stage):
    nc, P = X.nc, X.P
    wt = sbt(es, nc, name, [128, K, N], BF16)
    for k in range(K):
        for n0 in range(0, N, 2048):
            n1 = min(N, n0 + 2048)
            i = X.stage_i
            X.stage_i += 1
            st = stage[i % len(stage)]
            P.dma("sp" if i % 2 == 0 else "pool", st[:, 0:n1 - n0], dram_ap[k * 128:(k + 1) * 128, n0:n1], [], [("stage", i % len(stage))])
            eng = ("dve", "pool", "act")[i % 3]
            P.cp(eng, wt[:, k, n0:n1], st[:, 0:n1 - n0], [("stage", i % len(stage))], [(tag, k)])
    return wt


def phaseA(X, l, xsrc):
    nc, P, S = X.nc, X.P, X.S
    I = X.ins
    NB = S // 512
    with ExitStack() as es:
        stage = [sbt(es, nc, f"pa_stage{i}", [128, 2048], F32) for i in range(3)]
        win = load_weight_bf16(X, es, "pa_win", I["w_in"][l], 8, 2048, "win", stage)
        wkeys = [("win", k) for k in range(8)]
        gq = sbt(es, nc, "pa_gq", [128, 4], F32)
        for j, (nm, sc) in enumerate((("q_norm_g", 0.125), ("k_norm_g", 1.0))):
            g = I[nm][l, :]
            for m in range(2):
                P.dma("sp", gq[m * 64:(m + 1) * 64, 2 * j:2 * j + 1], g.rearrange("(d o) -> d o", o=1), [], ["gq"], slow=True)
                P.dma("sp", gq[m * 64:m * 64 + 32, 2 * j + 1:2 * j + 2], g[32:64].rearrange("(d o) -> d o", o=1), [], ["gq"], slow=True)
                P.dma("sp", gq[m * 64 + 32:m * 64 + 64, 2 * j + 1:2 * j + 2], g[0:32].rearrange("(d o) -> d o", o=1), [], ["gq"], slow=True)
        P.ts("dve", gq[:, 0:2], gq[:, 0:2], 0.125, None, ALU.mult, None, ["gq"], ["gq"])
        if getattr(X, "cut", 0) == 1:
            P.barrier(); P.emit(); return
        xt = [sbt(es, nc, f"pa_xt{i}", [128, 1024], F32) for i in range(4)]
        junk = [sbt(es, nc, f"pa_junk{i}", [128, 1024], BF16) for i in range(2)]
        xn = [sbt(es, nc, f"pa_xn{i}", [128, 1024], BF16) for i in range(8)]
        ss = sbt(es, nc, "pa_ss", [128, 8], F32)
        rs = sbt(es, nc, "pa_rs", [128, 8], F32)
        hT = [sbt(es, nc, f"pa_hT{i}", [128, 8, 512], BF16) for i in range(2)]
        cs = [sbt(es, nc, f"pa_cs{i}", [128, 2, 512], F32) for i in range(2)]
        sq = [sbt(es, nc, f"pa_sq{i}", [128, 512], F32) for i in range(2)]
        rawsb = [sbt(es, nc, f"pa_rawsb{i}", [128, 512], F32) for i in range(2)]
        rotmat = X.cst["rotmat"]
        rsb = [sbt(es, nc, f"pa_rsb{i}", [128, 512], F32) for i in range(2)]
        t1 = [sbt(es, nc, f"pa_t1{i}", [128, 512], F32) for i in range(2)]
        t2 = [sbt(es, nc, f"pa_t2{i}", [128, 512], F32) for i in range(2)]
        qo = [sbt(es, nc, f"pa_qo{i}", [128, 512], BF16) for i in range(3)]
        uo = [sbt(es, nc, f"pa_uo{i}", [128, 512], BF16) for i in range(3)]
        pT = [pst(es, nc, f"pa_pT{i}", [128, 8, 128], BF16) for i in range(2)]
        praw = [pst(es, nc, f"pa_raw{i}", [128, 512]) for i in range(2)]
        prot = [pst(es, nc, f"pa_rot{i}", [128, 512]) for i in range(2)]
        pss = pst(es, nc, "pa_pss", [128, 512])
        puv = pst(es, nc, "pa_puv", [128, 512])
        ident = X.cst["ident_bf"]
        bones = X.cst["blockones"]
        gm, mT = X.gm1[l], X.modT[l]
        st = {"m": 0, "u": 0}

        def stageA(nb):
            for tl in range(4):
                tt_ = nb * 4 + tl
                xb, jb, sc = tt_ % 4, tt_ % 2, tt_ % 8
                nbuf = tt_ % 8
                P.dma("sp", xt[xb][:], xsrc[tt_ * 128:(tt_ + 1) * 128, :], [], [("xt", xb)])
                P.act(junk[jb][:], xt[xb][:], AF.Square, [("xt", xb)], [("junk", jb), ("ss", sc)], accum=ss[:, sc:sc + 1])
                P.act(rs[:, sc:sc + 1], ss[:, sc:sc + 1], AF.Sqrt, [("ss", sc)], [("rs", sc)], bias=X.epsc[:, 0:1], scale=1.0 / D)
                P.recip(rs[:, sc:sc + 1], rs[:, sc:sc + 1], [("rs", sc)], [("rs", sc)])
                P.ts("dve", xn[nbuf][:], xt[xb][:], rs[:, sc:sc + 1], None, ALU.mult, None, [("xt", xb), ("rs", sc)], [("xn", nbuf)])

        def stageB(nb):
            hb = nb % 2
            P.dma("pool", cs[hb][:, 0, :], X.cosT[:, nb * 512:(nb + 1) * 512], [], [("cs", hb)])
            P.dma("pool", cs[hb][:, 1, :], X.sinT[:, nb * 512:(nb + 1) * 512], [], [("cs", hb)])
            for tl in range(4):
                tt_ = nb * 4 + tl
                nbuf, pb = tt_ % 8, tt_ % 2
                for c in range(8):
                    P.tr(pT[pb][:, c, :], xn[nbuf][:, c * 128:(c + 1) * 128], ident[:], [("xn", nbuf), "ident"], [("pT", pb)])
                for c in range(8):
                    dst = hT[hb][:, c, tl * 128:(tl + 1) * 128]
                    if tt_ % 2 == 0:
                        P.act(dst, pT[pb][:, c, :], AF.Identity, [("pT", pb), ("gm1", l), ("modT", l)], [("hT", hb, c, tl)],
                              bias=mT[:, c:c + 1], scale=gm[:, c:c + 1])
                    else:
                        P.ts("dve", dst, pT[pb][:, c, :], gm[:, c:c + 1], mT[:, c:c + 1], ALU.mult, ALU.add,
                             [("pT", pb), ("gm1", l), ("modT", l)], [("hT", hb, c, tl)])

        def stageP(nb):
            hb = nb % 2
            hk = lambda k: [("hT", hb, k, tl) for tl in range(4)]
            rbs = {}

            def m1(mi):
                rb = st["m"] % 2
                st["m"] += 1
                rbs[mi] = rb
                for k in range(8):
                    P.mm(praw[rb][:], win[:, k, mi * 128:(mi + 1) * 128], hT[hb][:, k, :], k == 0, k == 7, hk(k) + [("win", k)], [("raw", rb)])
                P.cp("act", rawsb[rb][:], praw[rb][:], [("raw", rb)], [("rawsb", rb)])
                P.act(sq[rb][:], praw[rb][:], AF.Square, [("raw", rb)], [("sq", rb)])

            def m2(mi):
                rb = rbs[mi]
                isq = mi < 4
                P.mm(prot[rb][:], rotmat[:], rawsb[rb][:], True, True, [("rawsb", rb), "rotmat"], [("rot", rb)])
                P.mm(pss[:], bones[:], sq[rb][:], True, True, [("sq", rb), "bones"], ["pss"])
                P.act(rsb[rb][:], pss[:], AF.Sqrt, ["pss"], [("rsb", rb)], bias=X.epsc[:, 0:1], scale=1.0 / 64)
                P.recip(rsb[rb][:], rsb[rb][:], [("rsb", rb)], [("rsb", rb)])
                gc = 0 if isq else 2
                P.stt(t1[rb][:], praw[rb][:], gq[:, gc:gc + 1], cs[hb][:, 0, :], ALU.mult, ALU.mult, [("raw", rb), "gq", ("cs", hb)], [("t1", rb)])
                P.stt(t2[rb][:], prot[rb][:], gq[:, gc + 1:gc + 2], cs[hb][:, 1, :], ALU.mult, ALU.mult, [("rot", rb), "gq", ("cs", hb)], [("t2", rb)])
                P.tt("pool", t1[rb][:], t1[rb][:], t2[rb][:], ALU.add, [("t1", rb), ("t2", rb)], [("t1", rb)])
                ob = st["u"] % 3
                st["u"] += 1
                P.tt("pool", qo[ob][:], t1[rb][:], rsb[rb][:], ALU.mult, [("t1", rb), ("rsb", rb)], [("qo", ob)])
                dst = (X.qT if isq else X.kT)[mi % 4, :, nb * 512:(nb + 1) * 512]
                P.dma("sp", dst, qo[ob][:], [("qo", ob)], [("qkT", mi, nb)])

            for mi in range(8):
                m1(mi)
                m2(mi)
            for ui in range(4):
                for k in range(8):
                    P.mm(puv[:], win[:, k, 1536 + ui * 128:1536 + (ui + 1) * 128], hT[hb][:, k, :], k == 0, k == 7, hk(k) + [("win", k)], ["puv"])
                ob = st["u"] % 3
                st["u"] += 1
                P.cp("act", uo[ob][:], puv[:], ["puv"], [("uo", ob)])
                P.dma("pool", X.uT[ui, :, nb * 512:(nb + 1) * 512], uo[ob][:], [("uo", ob)], [("uT", ui, nb)])
            for tl in range(4):
                for k in range(8):
                    P.mm(puv[:], hT[hb][:, k, tl * 128:(tl + 1) * 128], win[:, k, 1024:1536], k == 0, k == 7, [("hT", hb, k, tl), ("win", k)], ["puv"])
                ob = st["u"] % 3
                st["u"] += 1
                P.cp("act", uo[ob][:], puv[:], ["puv"], [("uo", ob)])
                P.dma("pool", X.vv[(nb * 4 + tl) * 128:(nb * 4 + tl + 1) * 128, :], uo[ob][:], [("uo", ob)], [("vv", nb, tl)])

        stageA(0)
        stageB(0)
        for nb in range(NB):
            if nb + 1 < NB:
                stageA(nb + 1)
            stageP(nb)
            if nb + 1 < NB:
                stageB(nb + 1)
        P.barrier()
        P.emit()


def phaseB(X, l):
    nc, P, S = X.nc, X.P, X.S
    I = X.ins
    NQ = S // 512
    NK = S // 128
    lam_init = 0.8 - 0.6 * math.exp(-0.3 * l)
    with ExitStack() as es:
        L4 = sbt(es, nc, "pb_L4", [128, 4], F32)
        pr = sbt(es, nc, "pb_pr", [128, 2], F32)
        ee = sbt(es, nc, "pb_ee", [128, 2], F32)
        nlam = sbt(es, nc, "pb_nlam", [128, 1], F32)
        gsub = sbt(es, nc, "pb_gsub", [128, 128], F32)
        kt = [[sbt(es, nc, f"pb_kt{i}_{m}", [128, S], BF16) for m in range(2)] for i in range(2)]
        vx = [sbt(es, nc, f"pb_vx{i}", [128, NK, 129], BF16) for i in range(2)]
        qt = [sbt(es, nc, f"pb_qt{i}", [128, 512], BF16) for i in range(2)]
        pb = [[sbt(es, nc, f"pb_p{i}_{m}", [128, 512], BF16) for m in range(2)] for i in range(3)]
        tbuf = [sbt(es, nc, f"pb_t{i}", [128, 128], F32) for i in range(4)]
        obuf = [sbt(es, nc, f"pb_o{i}", [128, 128], F32) for i in range(4)]
        jk = [sbt(es, nc, f"pb_jk{i}", [128, 128], F32) for i in range(4)]
        sm = sbt(es, nc, "pb_sm", [128, 8, 4], F32)
        ybf = [sbt(es, nc, f"pb_ybf{i}", [128, 128], BF16) for i in range(4)]
        yT = [sbt(es, nc, f"pb_yT{i}", [128, 512], BF16) for i in range(2)]
        pS = [[pst(es, nc, f"pb_pS{i}_{m}", [128, 512]) for m in range(2)] for i in range(2)]
        pO = [pst(es, nc, f"pb_pO{i}", [128, 512]) for i in range(3)]
        pTr = pst(es, nc, "pb_pTr", [128, 4, 128], BF16)
        ident = X.cst["ident_bf"]
        P.memset("dve", L4[:], 0.0, ["L4"])
        for j, nm in enumerate(("lam_q1", "lam_k1", "lam_q2", "lam_k2")):
            P.dma("sp", L4[0:64, j:j + 1], I[nm][l, :].rearrange("(d o) -> d o", o=1), [], ["L4"], slow=True)
        P.tt("dve", pr[:, 0:1], L4[:, 0:1], L4[:, 1:2], ALU.mult, ["L4"], ["pr"])
        P.tt("dve", pr[:, 1:2], L4[:, 2:3], L4[:, 3:4], ALU.mult, ["L4"], ["pr"])
        P.mm(pO[0][:, 0:2], X.cst["ones_f"][:], pr[:], True, True, ["pr", "ones_f"], ["pO0"])
        P.act(ee[:], pO[0][:, 0:2], AF.Exp, ["pO0"], ["ee"])
        P.tt("dve", nlam[:], ee[:, 1:2], ee[:, 0:1], ALU.subtract, ["ee"], ["nlam"])
        P.ts("dve", nlam[:], nlam[:], -lam_init, None, ALU.add, None, ["nlam"], ["nlam"])
        P.dma("sp", gsub[:], I["subln_g"][l, :].partition_broadcast(128), [], ["gsub"])
        P.ts("dve", gsub[:], gsub[:], 1.0 - lam_init, None, ALU.mult, None, ["gsub"], ["gsub"])
        for i in range(2):
            P.memset("pool", vx[i][:, :, 128:129], 1.0, [("vx", i)])
            P.memset("pool", kt[i][0][64:128, :], 0.0, [("kt", i)])
            P.memset("pool", kt[i][1][0:64, :], 0.0, [("kt", i)])
        reg = {}
        idx = 0
        for m in range(2):
            for j in range(4):
                reg[(m, j)] = (idx // 3, (idx % 3) * 129)
                idx += 1
        pcount = 0
        qcount = 0
        ecount = 0
        for h in range(4):
            hb = h % 2
            P.dma("sp", kt[hb][0][0:64, :], X.kT[h, 0:64, :], [], [("kt", hb)])
            P.dma("sp", kt[hb][1][64:128, :], X.kT[h, 64:128, :], [], [("kt", hb)])
            vsrc = X.vv[:, h * 128:(h + 1) * 128].rearrange("(kb p) e -> p kb e", p=128)
            KS = max(1, NK // 8)
            for k0 in range(0, NK, KS):
                P.dma("pool", vx[hb][:, k0:k0 + KS, 0:128], vsrc[:, k0:k0 + KS, :], [], [("vx", hb)])
            for qb in range(NQ):
                qi = qcount % 2
                qcount += 1
                P.dma("sp", qt[qi][:], X.qT[h, :, qb * 512:(qb + 1) * 512], [], [("qt", qi)])
                started = set()
                pis = {}
                for kb in range(NK + 1):
                    if kb < NK:
                        sb = kb % 2
                        pi = pcount % 3
                        pcount += 1
                        pis[kb] = pi
                        for m in range(2):
                            if X.rowtile:
                                P.mm(pS[sb][m][:], kt[hb][m][m * 64:(m + 1) * 64, kb * 128:(kb + 1) * 128], qt[qi][m * 64:(m + 1) * 64, :],
                                     True, True, [("kt", hb), ("qt", qi)], [("pS", sb, m)])
                            else:
                                P.mm(pS[sb][m][:], kt[hb][m][:, kb * 128:(kb + 1) * 128], qt[qi][:],
                                     True, True, [("kt", hb), ("qt", qi)], [("pS", sb, m)])
                        for m in range(2):
                            P.act(pb[pi][m][:], pS[sb][m][:], AF.Exp, [("pS", sb, m)], [("p", pi, m)])
                    if kb >= 1:
                        kp = kb - 1
                        pi = pis[kp]
                        for m in range(2):
                            for j in range(4):
                                bk, off = reg[(m, j)]
                                st = bk not in started
                                started.add(bk)
                                P.mm(pO[bk][:, off:off + 129], pb[pi][m][:, j * 128:(j + 1) * 128], vx[hb][:, kp, :],
                                     st, kp == NK - 1, [("p", pi, m), ("vx", hb)], [("O", m, j)], sgc=True)
                yi = qcount % 2
                ej = []
                for j in range(4):
                    e8 = ecount % 8
                    ecount += 1
                    b1, o1 = reg[(0, j)]
                    b2, o2 = reg[(1, j)]
                    ej.append((j, e8, pO[b1][:, o1:o1 + 129], pO[b2][:, o2:o2 + 129], ("sm", e8)))
                for (j, e8, O1, O2, smk) in ej:
                    P.recip(sm[:, e8, 0:1], O1[:, 128:129], [("O", 0, j)], [smk])
                    P.recip(sm[:, e8, 1:2], O2[:, 128:129], [("O", 1, j)], [smk])
                for (j, e8, O1, O2, smk) in ej:
                    P.tt("dve", sm[:, e8, 1:2], sm[:, e8, 1:2], nlam[:], ALU.mult, [smk, "nlam"], [smk])
                for (j, e8, O1, O2, smk) in ej:
                    P.ts("dve", tbuf[j][:], O2[:, 0:128], sm[:, e8, 1:2], None, ALU.mult, None, [("O", 1, j), smk], [("tbuf", j)])
                for (j, e8, O1, O2, smk) in ej:
                    P.stt(obuf[j][:], O1[:, 0:128], sm[:, e8, 0:1], tbuf[j][:], ALU.mult, ALU.add, [("O", 0, j), smk, ("tbuf", j)], [("obuf", j)])
                for (j, e8, O1, O2, smk) in ej:
                    P.act(jk[j][:], obuf[j][:], AF.Square, [("obuf", j)], [("jk", j), smk], accum=sm[:, e8, 2:3])
                for (j, e8, O1, O2, smk) in ej:
                    P.act(sm[:, e8, 3:4], sm[:, e8, 2:3], AF.Sqrt, [smk], [smk], bias=X.epsc[:, 0:1], scale=1.0 / 128)
                for (j, e8, O1, O2, smk) in ej:
                    P.recip(sm[:, e8, 3:4], sm[:, e8, 3:4], [smk], [smk])
                for (j, e8, O1, O2, smk) in ej:
                    P.stt(ybf[j][:], obuf[j][:], sm[:, e8, 3:4], gsub[:], ALU.mult, ALU.mult, [("obuf", j), smk, "gsub"], [("ybf", j)])
                for (j, e8, O1, O2, smk) in ej:
                    P.tr(pTr[:, j, :], ybf[j][:], ident[:], [("ybf", j), "ident"], [("pTr", j)])
                for (j, e8, O1, O2, smk) in ej:
                    P.cp("dve", yT[yi][:, j * 128:(j + 1) * 128], pTr[:, j, :], [("pTr", j)], [("yT", yi, j)])
                if X.cut in (31, 32, 33, 34, 35, 36):
                    continue
                P.dma("sp", X.yaT[h, :, qb * 512:(qb + 1) * 512], yT[yi][:], [("yT", yi, j) for j in range(4)], [("yaT", h, qb)])
        P.barrier()
        P.emit()


class SinCos:
    def __init__(self, X, es, W, nm, nsets=2):
        nc = X.nc
        self.W = W
        self.n = 0
        self.sets = []
        for i in range(nsets):
            self.sets.append(dict(kf=sbt(es, nc, f"{nm}_kf{i}", [128, W], F32), ki=sbt(es, nc, f"{nm}_ki{i}", [128, W], I32),
                                  rr=sbt(es, nc, f"{nm}_rr{i}", [128, W], F32), tmp=sbt(es, nc, f"{nm}_tmp{i}", [128, W], F32),
                                  r2=sbt(es, nc, f"{nm}_r2{i}", [128, W], F32), key=(nm, i)))

    def run(self, P, ang, kang, osin, ocos, kout, w=None):
        sc = self.sets[self.n % len(self.sets)]
        self.n += 1
        w = w or self.W
        kf, ki, rr, tmp, r2, k = sc["kf"][:, 0:w], sc["ki"][:, 0:w], sc["rr"][:, 0:w], sc["tmp"][:, 0:w], sc["r2"][:, 0:w], sc["key"]
        C1 = 6.28125
        C2 = TWO_PI - 6.28125
        P.ts("dve", kf, ang, 1.0 / TWO_PI, None, ALU.mult, None, [kang], [k])
        P.cp("dve", ki, kf, [k], [k])
        P.cp("dve", kf, ki, [k], [k])
        P.stt(rr, kf, -C1, ang, ALU.mult, ALU.add, [k, kang], [k])
        P.stt(rr, kf, -C2, rr, ALU.mult, ALU.add, [k], [k])
        P.ts("dve", tmp, rr, math.pi, -TWO_PI, ALU.is_gt, ALU.mult, [k], [k])
        P.tt("dve", rr, rr, tmp, ALU.add, [k], [k])
        P.ts("dve", tmp, rr, -math.pi, TWO_PI, ALU.is_lt, ALU.mult, [k], [k])
        P.tt("dve", rr, rr, tmp, ALU.add, [k], [k])
        P.ts("dve", r2, rr, math.pi / 2, None, ALU.add, None, [k], [k])
        P.ts("dve", tmp, r2, math.pi, -TWO_PI, ALU.is_gt, ALU.mult, [k], [k])
        P.tt("dve", r2, r2, tmp, ALU.add, [k], [k])
        P.ts("dve", rr, rr, -math.pi, math.pi, ALU.max, ALU.min, [k], [k])
        P.ts("dve", r2, r2, -math.pi, math.pi, ALU.max, ALU.min, [k], [k])
        P.act(osin, rr, AF.Sin, [k], [kout])
        P.act(ocos, r2, AF.Sin, [k], [kout])


def phaseS(X, l):
    nc, P, S = X.nc, X.P, X.S
    I = X.ins
    NC = S // LCH
    L = LCH
    with ExitStack() as es:
        def T(name, shape, dt=F32):
            return sbt(es, nc, "ps_" + name, shape, dt)
        are, aim, ldt = T("are", [128, 64]), T("aim", [128, 64]), T("ldt", [128, 64])
        lre, dtt, rho, th = T("lre", [128, 64]), T("dtt", [128, 64]), T("rho", [128, 64]), T("th", [128, 64])
        cth, sth, thL, cL, sL = T("cth", [128, 64]), T("sth", [128, 64]), T("thL", [128, 64]), T("cL", [128, 64]), T("sL", [128, 64])
        abr, abi, den, nre, fre, fim, t64 = (T(n_, [128, 64]) for n_ in ("abr", "abi", "den", "nre", "fre", "fim", "t64"))
        bre, bim = T("bre", [64, 64, 16]), T("bim", [64, 64, 16])
        Bbr, Bbi, tb = T("Bbr", [64, 64, 16]), T("Bbi", [64, 64, 16]), T("tb", [64, 64, 16])
        sc64 = SinCos(X, es, 64, "ps_sc64")
        scL = SinCos(X, es, L, "ps_scL")
        jrow = X.cst["jrow"]
        gmask = X.cst["gmask"]
        dcol = T("dcol", [128, 4])
        P.dma("sp", dcol[:], colvec(I["ssm_d"][l, :], 4), [], ["dcol"], slow=True)
        for hf in range(2):
            for d_ in range(2):
                for gq_ in range(2):
                    cs_ = slice(d_ * 32 + gq_ * 16, d_ * 32 + gq_ * 16 + 16)
                    P.dma("sp", are[hf * 64:(hf + 1) * 64, cs_], I["ssm_a_re"][l, d_, gq_ * 16:gq_ * 16 + 16, :].rearrange("g n -> n g"), [], ["are"], slow=True)
                    P.dma("pool", aim[hf * 64:(hf + 1) * 64, cs_], I["ssm_a_im"][l, d_, gq_ * 16:gq_ * 16 + 16, :].rearrange("g n -> n g"), [], ["aim"], slow=True)
        P.dma("sp", ldt[:], I["ssm_log_dt"][l].rearrange("d g -> (d g)").partition_broadcast(128), [], ["ldt"])
        for d_ in range(2):
            for gq_ in range(2):
                cs_ = slice(d_ * 32 + gq_ * 16, d_ * 32 + gq_ * 16 + 16)
                P.dma("sp", bre[:, cs_, :], I["ssm_b_re"][l, d_, gq_ * 16:gq_ * 16 + 16].rearrange("g n p -> n g p"), [], ["bre"], slow=True)
                P.dma("pool", bim[:, cs_, :], I["ssm_b_im"][l, d_, gq_ * 16:gq_ * 16 + 16].rearrange("g n p -> n g p"), [], ["bim"], slow=True)
        pk = "sparam"
        P.ts("dve", lre[:], are[:], -1e-4, None, ALU.min, None, ["are"], [pk])
        P.act(dtt[:], ldt[:], AF.Exp, ["ldt"], ["dtt"])
        P.tt("dve", t64[:], lre[:], dtt[:], ALU.mult, [pk, "dtt"], ["t64"])
        P.act(rho[:], t64[:], AF.Exp, ["t64"], ["rho"])
        P.tt("dve", th[:], aim[:], dtt[:], ALU.mult, ["aim", "dtt"], ["th"])
        sc64.run(P, th[:], "th", sth[:], cth[:], "scth")
        P.ts("dve", thL[:], th[:], float(L), None, ALU.mult, None, ["th"], ["thL"])
        sc64.run(P, thL[:], "thL", sL[:], cL[:], "scL")
        P.ts("dve", sL[64:128, :], sL[64:128, :], -1.0, None, ALU.mult, None, ["scL"], ["scL"])
        P.tt("dve", abr[:], rho[:], cth[:], ALU.mult, ["rho", "scth"], ["abr"])
        P.tt("dve", abi[:], rho[:], sth[:], ALU.mult, ["rho", "scth"], ["abi"])
        P.tt("dve", den[:], lre[:], lre[:], ALU.mult, [pk], ["den"])
        P.tt("dve", t64[:], aim[:], aim[:], ALU.mult, ["aim"], ["t64"])
        P.tt("dve", den[:], den[:], t64[:], ALU.add, ["den", "t64"], ["den"])
        P.recip(den[:], den[:], ["den"], ["den"])
        P.ts("dve", nre[:], abr[:], -1.0, None, ALU.add, None, ["abr"], ["nre"])
        P.tt("dve", fre[:], nre[:], lre[:], ALU.mult, ["nre", pk], ["fre"])
        P.tt("dve", t64[:], abi[:], aim[:], ALU.mult, ["abi", "aim"], ["t64"])
        P.tt("dve", fre[:], fre[:], t64[:], ALU.add, ["fre", "t64"], ["fre"])
        P.tt("dve", fre[:], fre[:], den[:], ALU.mult, ["fre", "den"], ["fre"])
        P.tt("dve", fim[:], abi[:], lre[:], ALU.mult, ["abi", pk], ["fim"])
        P.tt("dve", t64[:], nre[:], aim[:], ALU.mult, ["nre", "aim"], ["t64"])
        P.tt("dve", fim[:], fim[:], t64[:], ALU.subtract, ["fim", "t64"], ["fim"])
        P.tt("dve", fim[:], fim[:], den[:], ALU.mult, ["fim", "den"], ["fim"])
        frb = fre[0:64, :].unsqueeze(2).to_broadcast([64, 64, 16])
        fib = fim[0:64, :].unsqueeze(2).to_broadcast([64, 64, 16])
        P.tt("dve", Bbr[:], bre[:], frb, ALU.mult, ["bre", "fre"], ["Bbr"])
        P.tt("dve", tb[:], bim[:], fib, ALU.mult, ["bim", "fim"], ["tb"])
        P.tt("dve", Bbr[:], Bbr[:], tb[:], ALU.subtract, ["Bbr", "tb"], ["Bbr"])
        P.tt("dve", Bbi[:], bim[:], frb, ALU.mult, ["bim", "fre"], ["Bbi"])
        P.tt("dve", tb[:], bre[:], fib, ALU.mult, ["bre", "fim"], ["tb"])
        P.tt("dve", Bbi[:], Bbi[:], tb[:], ALU.add, ["Bbi", "tb"], ["Bbi"])
        ut = T("ut", [128, S], BF16)
        Yacc = T("Yacc", [128, S])
        TT = T("TT", [128, 128])
        cn1, cn2 = T("cn1", [128, 2, 64]), T("cn2", [128, 2, 64])
        S1, S2 = T("S1", [128, 128]), T("S2", [128, 128])
        Zw = [T(f"Zw{g}", [128, 128], BF16) for g in range(8)]
        Zsw = [T(f"Zsw{g}", [128, 128], BF16) for g in range(8)]
        Wa = [T(f"Wa{g}", [128, 128], BF16) for g in range(8)]
        Wb = [T(f"Wb{g}", [128, 128], BF16) for g in range(8)]
        Rm = [T(f"Rm{g}", [128, 128]) for g in range(8)]
        cj = [T(f"cj{g}", [128, L]) for g in range(8)]
        sj = [T(f"sj{g}", [128, L]) for g in range(8)]
        angt = [T(f"angt{i}", [128, L]) for i in range(2)]
        init = T("init", [128, 8])
        zh = [T(f"zh{i}", [128, L]) for i in range(4)]
        zh2 = [T(f"zh2{i}", [128, L]) for i in range(4)]
        G = [T(f"G{i}", [128, L]) for i in range(4)]
        P1 = [T(f"P1{i}", [128, L], BF16) for i in range(4)]
        P2 = [T(f"P2{i}", [128, L], BF16) for i in range(4)]
        yg = [T(f"yg{i}", [128, L]) for i in range(2)]
        ygb = [T(f"ygb{i}", [128, L], BF16) for i in range(2)]
        pZ = [pst(es, nc, f"ps_pZ{i}", [128, 512]) for i in range(2)]
        pZs = [pst(es, nc, f"ps_pZs{i}", [128, 512]) for i in range(2)]
        pY = pst(es, nc, "ps_pY", [128, 512])
        pC = pst(es, nc, "ps_pC", [128, 512])
        pX = pst(es, nc, "ps_pX", [128, 512])
        identf = X.cst["ident_f"]
        jswap = X.cst["jswap"]
        gcount = 0
        for ct in range(4):
            P.dma("sp", ut[:], X.uT[ct, :, :], [], ["ut"])
            for d in range(2):
                g0 = d * 32 + ct * 8
                for q_, (src, col) in enumerate(((Bbr, 0), (Bbi, 64))):
                    P.tr(pX[:, col:col + 64], src[:, g0:g0 + 8, :].rearrange("n g p -> n (g p)"), identf[0:64, 0:64], ["Bbr", "Bbi", "ident_f"], ["pX"])
                P.cp("act", TT[:], pX[:, 0:128], ["pX"], ["TT"])
                P.dma("sp", cn1[:, 0, :], I["ssm_c_re"][l, d, ct * 8:(ct + 1) * 8].rearrange("g p n -> (g p) n"), [], ["cn1"])
                P.dma("sp", cn1[:, 1, :], I["ssm_c_im"][l, d, ct * 8:(ct + 1) * 8].rearrange("g p n -> (g p) n"), [], ["cn1"])
                P.dma("pool", cn2[:, 0, :], I["ssm_c_im"][l, d, ct * 8:(ct + 1) * 8].rearrange("g p n -> (g p) n"), [], ["cn2"])
                P.dma("pool", cn2[:, 1, :], I["ssm_c_re"][l, d, ct * 8:(ct + 1) * 8].rearrange("g p n -> (g p) n"), [], ["cn2"])
                P.tr(pX[:, 128:256], cn1[:].rearrange("q t n -> q (t n)"), identf[:], ["cn1", "ident_f"], ["pX"])
                P.tr(pX[:, 256:384], cn2[:].rearrange("q t n -> q (t n)"), identf[:], ["cn2", "ident_f"], ["pX"])
                P.cp("act", S1[:], pX[:, 128:256], ["pX"], ["S1"])
                P.cp("act", S2[:], pX[:, 256:384], ["pX"], ["S2"])
                for g in range(8):
                    dg = g0 + g
                    mk = gmask[:, g:g + 1]
                    wk = ("w", g)
                    P.ts("dve", Zw[g][:], TT[:], mk, None, ALU.mult, None, ["TT", "gmask"], [wk])
                    P.ts("dve", Zsw[g][:, 0:64], TT[:, 64:128], mk, None, ALU.mult, None, ["TT", "gmask"], [wk])
                    P.ts("dve", Zsw[g][:, 64:128], TT[:, 0:64], mk, -1.0, ALU.mult, ALU.mult, ["TT", "gmask"], [wk])
                    P.memset("pool", Wa[g][:], 0.0, [wk])
                    P.memset("pool", Wb[g][:], 0.0, [wk])
                    cs_ = slice(g * 16, (g + 1) * 16)
                    P.cp("pool", Wa[g][0:64, cs_], S1[0:64, cs_], ["S1"], [wk])
                    P.ts("pool", Wa[g][64:128, cs_], S1[64:128, cs_], -1.0, None, ALU.mult, None, ["S1"], [wk])
                    P.ts("pool", Wb[g][:, cs_], S2[:, cs_], -1.0, None, ALU.mult, None, ["S2"], [wk])
                    P.ts("dve", Rm[g][:], identf[:], cL[:, dg:dg + 1], None, ALU.mult, None, ["ident_f", "scL"], [wk])
                    P.stt(Rm[g][:], jswap[:], sL[:, dg:dg + 1], Rm[g][:], ALU.mult, ALU.add, ["jswap", "scL", wk], [wk])
                    ab = gcount % 2
                    gcount += 1
                    P.ts("dve", angt[ab][:], jrow[:], th[:, dg:dg + 1], None, ALU.mult, None, ["jrow", "th"], [("angt", ab)])
                    scL.run(P, angt[ab][:], ("angt", ab), sj[g][:], cj[g][:], ("tab", g))
                P.memset("dve", init[:], 0.0, ["init"])
                order = list(range(NC)) if d == 0 else list(range(NC - 1, -1, -1))
                its = [(ci, g) for ci in order for g in range(8)]

                def views(g):
                    if d == 0:
                        return cj[g][:], sj[g][:]
                    return cj[g][:, ::-1], sj[g][:, ::-1]

                def stage1(n):
                    ci, g = its[n]
                    tok = slice(ci * L, (ci + 1) * L)
                    zb, b4 = n % 2, n % 4
                    wk = ("w", g)
                    cjv, sjv = views(g)
                    P.mm(pZ[zb][:], Zw[g][:], ut[:, tok], True, True, [wk, "ut"], [("pZ", zb)])
                    P.mm(pZs[zb][:], Zsw[g][:], ut[:, tok], True, True, [wk, "ut"], [("pZs", zb)])
                    P.tt("dve", zh[b4][:], pZ[zb][:], cjv, ALU.mult, [("pZ", zb), ("tab", g)], [("zh", b4)])
                    P.tt("dve", zh2[b4][:], pZs[zb][:], sjv, ALU.mult, [("pZs", zb), ("tab", g)], [("zh2", b4)])
                    P.tt("pool", zh[b4][:], zh[b4][:], zh2[b4][:], ALU.add, [("zh", b4), ("zh2", b4)], [("zh", b4)])

                def stage2(n):
                    ci, g = its[n]
                    b4 = n % 4
                    dg = g0 + g
                    cjv, sjv = views(g)
                    rb = rho[:, dg:dg + 1].to_broadcast([128, L])
                    if d == 0:
                        go, zi = G[b4][:], zh[b4][:]
                    else:
                        go, zi = G[b4][:, ::-1], zh[b4][:, ::-1]
                    ini = init[:, g:g + 1]
                    P.op("dve", (lambda go=go, zi=zi, rb=rb, ini=ini: (lambda e: e.tensor_tensor_scan(
                        out=go, data0=rb, data1=zi, initial=ini, op0=ALU.mult, op1=ALU.add)))(),
                        [("zh", b4), "rho", ("init", g)], [("G", b4)])
                    P.tt("pool", P1[b4][:], G[b4][:], cjv, ALU.mult, [("G", b4), ("tab", g)], [("P1", b4)])
                    P.tt("dve", P2[b4][:], G[b4][:], sjv, ALU.mult, [("G", b4), ("tab", g)], [("P2", b4)])

                def stage3(n):
                    ci, g = its[n]
                    tok = slice(ci * L, (ci + 1) * L)
                    b4 = n % 4
                    wk = ("w", g)
                    last = G[b4][:, L - 1:L] if d == 0 else G[b4][:, 0:1]
                    P.mm(pY[:], Wa[g][:], P1[b4][:], g == 0, False, [wk, ("P1", b4)], ["pY"])
                    P.mm(pY[:], Wb[g][:], P2[b4][:], False, g == 7, [wk, ("P2", b4)], ["pY"])
                    P.mm(pC[:, g:g + 1], Rm[g][:], last, True, True, [wk, ("G", b4)], [("pC", g)])
                    P.cp("act", init[:, g:g + 1], pC[:, g:g + 1], [("pC", g)], [("init", g)])
                    if g == 7:
                        if d == 0:
                            P.cp("act", Yacc[:, tok], pY[:], ["pY"], [("Yacc", ci)])
                        else:
                            P.tt("dve", Yacc[:, tok], pY[:], Yacc[:, tok], ALU.add, ["pY", ("Yacc", ci)], [("Yacc", ci)])

                NI = len(its)
                for n in range(NI + 2):
                    if n < NI:
                        stage1(n)
                    if 1 <= n <= NI:
                        stage2(n - 1)
                    if n >= 2:
                        stage3(n - 2)
            for ci in range(NC):
                tok = slice(ci * L, (ci + 1) * L)
                yb = ci % 2
                P.stt(yg[yb][:], ut[:, tok], dcol[:, ct:ct + 1], Yacc[:, tok], ALU.mult, ALU.add, ["ut", "dcol", ("Yacc", ci)], [("yg", yb)])
                P.act(ygb[yb][:], yg[yb][:], AF.Gelu, [("yg", yb)], [("ygb", yb)])
                P.dma("pool", X.ygT[ct, :, tok], ygb[yb][:], [("ygb", yb)], [("ygT", ct, ci)])
        P.barrier()
        P.emit()


def phaseS8(X, l):
    nc, P, S = X.nc, X.P, X.S
    I = X.ins
    NBLK = S // 8
    L = min(256, NBLK)
    NC = NBLK // L
    UB = min(512, NBLK)
    NUB = NBLK // UB
    with ExitStack() as es:
        def T(name, shape, dt=F32):
            return sbt(es, nc, "s8_" + name, shape, dt)
        names = ("are", "aim", "ldt", "lre", "dtt", "rho", "rho8", "th", "th8", "cth", "sth", "thL", "cL", "sL",
                 "abr", "abi", "den", "nre", "fre", "fim", "t64", "t64b", "air", "aii")
        pr = {n_: T(n_, [128, 64]) for n_ in names}
        PWr, PWi, PNr, PNi = T("PWr", [128, 64, 9]), T("PWi", [128, 64, 9]), T("PNr", [128, 64, 9]), T("PNi", [128, 64, 9])
        Bbr, Bbi = T("Bbr", [128, 64, 16]), T("Bbi", [128, 64, 16])
        dvec8 = T("dvec8", [128, 32])
        sg = T("sg", [128, 4])
        es2 = ExitStack()
        bre, bim = sbt(es2, nc, "s8_bre", [128, 64, 16], F32), sbt(es2, nc, "s8_bim", [128, 64, 16], F32)
        tB = sbt(es2, nc, "s8_tB", [128, 64, 16], F32)
        sc64 = SinCos(X, es2, 64, "s8_sc64")
        jrow = X.cst["jrow"]
        identf, jswap = X.cst["ident_f"], X.cst["jswap"]
        selbig = X.cst["selbig"]
        nmask = X.cst["nmask"]
        K_ = "prm"

        def V(out, a, b, op):
            P.tt("dve", out, a, b, op, [K_], [K_])

        def cmul(o_r, o_i, a_r, a_i, b_r, b_i, t1, t2):
            V(t1, a_r, b_r, ALU.mult)
            V(t2, a_i, b_i, ALU.mult)
            V(o_r, t1, t2, ALU.subtract)
            V(t1, a_r, b_i, ALU.mult)
            V(t2, a_i, b_r, ALU.mult)
            V(o_i, t1, t2, ALU.add)

        for j in range(8):
            P.dma("sp", dvec8[j * 16:(j + 1) * 16, :], I["ssm_d"][l, :].rearrange("(g p) -> p g", p=16), [], [K_], slow=True)
        P.memset("dve", sg[0:64, 0:1], 1.0, [K_])
        P.memset("dve", sg[64:128, 0:1], -1.0, [K_])
        P.memset("dve", sg[0:64, 1:2], -1.0, [K_])
        P.memset("dve", sg[64:128, 1:2], 1.0, [K_])
        P.memset("dve", sg[0:64, 2:3], 1.0, [K_])
        P.memset("dve", sg[64:128, 2:3], 0.0, [K_])
        P.memset("dve", sg[0:64, 3:4], 0.0, [K_])
        P.memset("dve", sg[64:128, 3:4], 1.0, [K_])
        for hf in range(2):
            hs = slice(hf * 64, (hf + 1) * 64)
            for d_ in range(2):
                for gq_ in range(2):
                    cs_ = slice(d_ * 32 + gq_ * 16, d_ * 32 + gq_ * 16 + 16)
                    gs_ = slice(gq_ * 16, gq_ * 16 + 16)
                    P.dma("sp", pr["are"][hs, cs_], I["ssm_a_re"][l, d_, gs_, :].rearrange("g n -> n g"), [], [K_], slow=True)
                    P.dma("pool", pr["aim"][hs, cs_], I["ssm_a_im"][l, d_, gs_, :].rearrange("g n -> n g"), [], [K_], slow=True)
                    P.dma("sp", bre[hs, cs_, :], I["ssm_b_re"][l, d_, gs_].rearrange("g n p -> n g p"), [], [K_], slow=True)
                    P.dma("pool", bim[hs, cs_, :], I["ssm_b_im"][l, d_, gs_].rearrange("g n p -> n g p"), [], [K_], slow=True)
        P.dma("sp", pr["ldt"][:], I["ssm_log_dt"][l].rearrange("d g -> (d g)").partition_broadcast(128), [], [K_])
        p_ = pr
        P.ts("dve", p_["lre"][:], p_["are"][:], -1e-4, None, ALU.min, None, [K_], [K_])
        P.act(p_["dtt"][:], p_["ldt"][:], AF.Exp, [K_], [K_])
        V(p_["t64"][:], p_["lre"][:], p_["dtt"][:], ALU.mult)
        P.act(p_["rho"][:], p_["t64"][:], AF.Exp, [K_], [K_])
        P.act(p_["rho8"][:], p_["t64"][:], AF.Exp, [K_], [K_], scale=8.0)
        V(p_["th"][:], p_["aim"][:], p_["dtt"][:], ALU.mult)
        sc64.run(P, p_["th"][:], K_, p_["sth"][:], p_["cth"][:], K_)
        P.ts("dve", p_["th8"][:], p_["th"][:], 8.0, None, ALU.mult, None, [K_], [K_])
        P.ts("dve", p_["thL"][:], p_["th8"][:], float(L), None, ALU.mult, None, [K_], [K_])
        sc64.run(P, p_["thL"][:], K_, p_["sL"][:], p_["cL"][:], K_)
        P.ts("dve", p_["sL"][64:128, :], p_["sL"][64:128, :], -1.0, None, ALU.mult, None, [K_], [K_])
        V(p_["abr"][:], p_["rho"][:], p_["cth"][:], ALU.mult)
        V(p_["abi"][:], p_["rho"][:], p_["sth"][:], ALU.mult)
        V(p_["den"][:], p_["lre"][:], p_["lre"][:], ALU.mult)
        V(p_["t64"][:], p_["aim"][:], p_["aim"][:], ALU.mult)
        V(p_["den"][:], p_["den"][:], p_["t64"][:], ALU.add)
        P.recip(p_["den"][:], p_["den"][:], [K_], [K_])
        P.ts("dve", p_["nre"][:], p_["abr"][:], -1.0, None, ALU.add, None, [K_], [K_])
        V(p_["fre"][:], p_["nre"][:], p_["lre"][:], ALU.mult)
        V(p_["t64"][:], p_["abi"][:], p_["aim"][:], ALU.mult)
        V(p_["fre"][:], p_["fre"][:], p_["t64"][:], ALU.add)
        V(p_["fre"][:], p_["fre"][:], p_["den"][:], ALU.mult)
        V(p_["fim"][:], p_["abi"][:], p_["lre"][:], ALU.mult)
        V(p_["t64"][:], p_["nre"][:], p_["aim"][:], ALU.mult)
        V(p_["fim"][:], p_["fim"][:], p_["t64"][:], ALU.subtract)
        V(p_["fim"][:], p_["fim"][:], p_["den"][:], ALU.mult)
        frb = p_["fre"][:].unsqueeze(2).to_broadcast([128, 64, 16])
        fib = p_["fim"][:].unsqueeze(2).to_broadcast([128, 64, 16])
        V(Bbr[:], bre[:], frb, ALU.mult)
        V(tB[:], bim[:], fib, ALU.mult)
        V(Bbr[:], Bbr[:], tB[:], ALU.subtract)
        V(Bbi[:], bim[:], frb, ALU.mult)
        V(tB[:], bre[:], fib, ALU.mult)
        V(Bbi[:], Bbi[:], tB[:], ALU.add)
        P.memset("dve", PWr[:, :, 0], 1.0, [K_])
        P.memset("dve", PWi[:, :, 0], 0.0, [K_])
        P.memset("dve", PNr[:, :, 0], 1.0, [K_])
        P.memset("dve", PNi[:, :, 0], 0.0, [K_])
        P.cp("dve", PWr[:, :, 1], p_["abr"][:], [K_], [K_])
        P.cp("dve", PWi[:, :, 1], p_["abi"][:], [K_], [K_])
        V(p_["den"][:], p_["abr"][:], p_["abr"][:], ALU.mult)
        V(p_["t64"][:], p_["abi"][:], p_["abi"][:], ALU.mult)
        V(p_["den"][:], p_["den"][:], p_["t64"][:], ALU.add)
        P.recip(p_["den"][:], p_["den"][:], [K_], [K_])
        V(p_["air"][:], p_["abr"][:], p_["den"][:], ALU.mult)
        V(p_["aii"][:], p_["abi"][:], p_["den"][:], ALU.mult)
        P.ts("dve", p_["aii"][:], p_["aii"][:], -1.0, None, ALU.mult, None, [K_], [K_])
        P.cp("dve", PNr[:, :, 1], p_["air"][:], [K_], [K_])
        P.cp("dve", PNi[:, :, 1], p_["aii"][:], [K_], [K_])
        for k in range(1, 8):
            cmul(PWr[:, :, k + 1], PWi[:, :, k + 1], PWr[:, :, k], PWi[:, :, k], p_["abr"][:], p_["abi"][:], p_["t64"][:], p_["t64b"][:])
            cmul(PNr[:, :, k + 1], PNi[:, :, k + 1], PNr[:, :, k], PNi[:, :, k], p_["air"][:], p_["aii"][:], p_["t64"][:], p_["t64b"][:])
        P.barrier()
        P.emit()
        es2.close()
        scL = SinCos(X, es, L, "s8_scL", nsets=1)
        ut = T("ut", [128, S], BF16)
        ygU = ut[:].rearrange("p (g c) -> p g c", g=8)
        U = T("U", [128, 8, NBLK], BF16)
        Yflat = T("Yacc", [128, max(8 * NBLK, 7168)])
        Yacc = Yflat[:, 0:8 * NBLK].rearrange("p (g c) -> p g c", g=8)
        alias = True
        if alias:
            Yf = Yflat[:]
            prA, prB, prC, prD = (Yf[:, i * 1024:(i + 1) * 1024].rearrange("p (g j q) -> p g j q", g=8, j=8) for i in range(4))
            XM, WaF, WbF = (Yf[:, i * 1024:(i + 1) * 1024].rearrange("p (g m) -> p g m", g=8) for i in range(4, 7))
        else:
            prA, prB, prC, prD = (T(n_, [128, 8, 8, 16]) for n_ in ("prA", "prB", "prC", "prD"))
            XM, WaF, WbF = T("XM", [128, 8, 128]), T("WaF", [128, 8, 128]), T("WbF", [128, 8, 128])
        cn1, cn2 = T("cn1", [128, 2, 64]), T("cn2", [128, 2, 64])
        S1, S2 = T("S1", [128, 8, 16]), T("S2", [128, 8, 16])
        Zw = [[T(f"Zw{d}_{g}", [128, 128], BF16) for g in range(8)] for d in range(2)]
        Zsw = [[T(f"Zsw{d}_{g}", [128, 128], BF16) for g in range(8)] for d in range(2)]
        Wa = [[T(f"Wa{d}_{g}", [128, 128], BF16) for g in range(8)] for d in range(2)]
        Wb = [[T(f"Wb{d}_{g}", [128, 128], BF16) for g in range(8)] for d in range(2)]
        M1acc = [T(f"M1a{g}", [128, 128]) for g in range(8)]
        M1b = [T(f"M1b{g}", [128, 128], BF16) for g in range(8)]
        tmpM = T("tmpM", [128, 128])
        Rm = [T(f"Rm{g}", [128, 128]) for g in range(8)]
        cj = [T(f"cj{g}", [128, L]) for g in range(8)]
        sj = [T(f"sj{g}", [128, L]) for g in range(8)]
        angt = [T(f"angt{i}", [128, L]) for i in range(2)]
        init = T("init", [128, 8])
        NBUF = 4
        zh = [T(f"zh{i}", [128, L]) for i in range(NBUF)]
        zh2 = [T(f"zh2{i}", [128, L]) for i in range(NBUF)]
        G = [T(f"G{i}", [128, L]) for i in range(NBUF)]
        P1 = [T(f"P1{i}", [128, L], BF16) for i in range(NBUF)]
        P2 = [T(f"P2{i}", [128, L], BF16) for i in range(NBUF)]
        ytile = T("ytile", [128, UB * 8], BF16)
        pZ = [pst(es, nc, f"s8_pZ{i}", [128, 512]) for i in range(2)]
        pZs = [pst(es, nc, f"s8_pZs{i}", [128, 512]) for i in range(2)]
        pY = [pst(es, nc, f"s8_pY{i}", [128, 512]) for i in range(2)]
        pC = pst(es, nc, "s8_pC", [128, 512])
        pX = pst(es, nc, "s8_pX", [128, 512])
        alt = [pX, pY[0]]
        acount = 0
        gcount = 0
        for ct in range(4):
            P.dma("sp", ut[:], X.uT[ct, :, :], [], ["ut"])
            for g in range(8):
                for ub in range(NUB):
                    ps = alt[acount % 2]
                    pk = ("alt", acount % 2)
                    acount += 1
                    for j in range(8):
                        P.mm(ps[:, 0:UB], selbig[:, g, 112 - 16 * j:240 - 16 * j], ut[:, slice(ub * UB * 8 + j, (ub + 1) * UB * 8, 8)],
                             j == 0, j == 7, ["ut", "selbig"], [pk])
                    P.cp("act", U[:, g, ub * UB:(ub + 1) * UB], ps[:, 0:UB], [pk], [("U", g)])
            P.barrier()
            for d in range(2):
                g0 = d * 32 + ct * 8
                gs = slice(g0, g0 + 8)
                kk = slice(7, None, -1) if d == 0 else slice(0, 8)
                P.dma("sp", cn1[:, 0, :], I["ssm_c_re"][l, d, ct * 8:(ct + 1) * 8].rearrange("g p n -> (g p) n"), [], ["cn1"])
                P.dma("sp", cn1[:, 1, :], I["ssm_c_im"][l, d, ct * 8:(ct + 1) * 8].rearrange("g p n -> (g p) n"), [], ["cn1"])
                P.dma("pool", cn2[:, 0, :], I["ssm_c_im"][l, d, ct * 8:(ct + 1) * 8].rearrange("g p n -> (g p) n"), [], ["cn2"])
                P.dma("pool", cn2[:, 1, :], I["ssm_c_re"][l, d, ct * 8:(ct + 1) * 8].rearrange("g p n -> (g p) n"), [], ["cn2"])
                P.tr(pX[:, 0:128], cn1[:].rearrange("q t n -> q (t n)"), identf[:], ["cn1", "ident_f"], ["pXa"])
                P.tr(pX[:, 128:256], cn2[:].rearrange("q t n -> q (t n)"), identf[:], ["cn2", "ident_f"], ["pXb"])
                P.cp("act", S1[:].rearrange("q g p -> q (g p)"), pX[:, 0:128], ["pXa"], [K_])
                P.cp("act", S2[:].rearrange("q g p -> q (g p)"), pX[:, 128:256], ["pXb"], [K_])
                shp = [128, 8, 8, 16]
                Pr2 = PWr[:, gs, kk].unsqueeze(3).to_broadcast(shp)
                Pi2 = PWi[:, gs, kk].unsqueeze(3).to_broadcast(shp)
                Pcr = PNr[:, gs, kk].unsqueeze(3).to_broadcast(shp)
                Pci = PNi[:, gs, kk].unsqueeze(3).to_broadcast(shp)
                Brb = Bbr[:, gs, :].unsqueeze(2).to_broadcast(shp)
                Bib = Bbi[:, gs, :].unsqueeze(2).to_broadcast(shp)
                S1b = S1[:].unsqueeze(2).to_broadcast(shp)
                S2b = S2[:].unsqueeze(2).to_broadcast(shp)
                XMv = XM[:].rearrange("q g (j p) -> q g j p", p=16)
                WaFv = WaF[:].rearrange("q g (j p) -> q g j p", p=16)
                WbFv = WbF[:].rearrange("q g (j p) -> q g j p", p=16)
                V(prA[:], Brb, Pr2, ALU.mult)
                V(prC[:], Bib, Pi2, ALU.mult)
                V(prA[:], prA[:], prC[:], ALU.subtract)
                V(prB[:], Brb, Pi2, ALU.mult)
                V(prC[:], Bib, Pr2, ALU.mult)
                V(prB[:], prB[:], prC[:], ALU.add)
                P.ts("dve", XMv, prA[:], sg[:, 2:3], None, ALU.mult, None, [K_], [K_])
                P.stt(XMv, prB[:], sg[:, 3:4], XMv, ALU.mult, ALU.add, [K_], [K_])
                V(prA[:], S1b, Pcr, ALU.mult)
                V(prB[:], S2b, Pci, ALU.mult)
                P.stt(WaFv, prA[:], sg[:, 0:1], prB[:], ALU.mult, ALU.subtract, [K_], [K_])
                V(prC[:], S1b, Pci, ALU.mult)
                V(prD[:], S2b, Pcr, ALU.mult)
                P.stt(WbFv, prC[:], sg[:, 1:2], prD[:], ALU.mult, ALU.subtract, [K_], [K_])
                for g in range(8):
                    wk = ("w", d, g)
                    gg = ct * 8 + g
                    P.tr(pX[:, 256:384], XM[:, g, :], identf[:], [K_, "ident_f"], ["pXc"])
                    P.cp("act", Zw[d][g][:], pX[:, 256:384], ["pXc"], [wk])
                    P.cp("act", Zsw[d][g][:, 0:64], pX[:, 320:384], ["pXc"], [wk])
                    P.ts("dve", Zsw[d][g][:, 64:128], pX[:, 256:320], -1.0, None, ALU.mult, None, ["pXc"], [wk])
                    P.cp("pool", Wa[d][g][:], WaF[:, g, :], [K_], [wk])
                    P.cp("pool", Wb[d][g][:], WbF[:, g, :], [K_], [wk])
                    P.mm(pX[:, 384:512], XM[:, g, :], WaF[:, g, :], True, True, [K_], ["pXd"])
                    if d == 0:
                        P.ts("dve", M1acc[g][:], identf[:], dvec8[:, gg:gg + 1], None, ALU.mult, None, [K_, "ident_f"], [("M1", g)])
                    P.tt("dve", tmpM[:], pX[:, 384:512], nmask[:, d, :], ALU.mult, ["pXd", "nmask"], ["tmpM"])
                    P.tt("dve", M1acc[g][:], M1acc[g][:], tmpM[:], ALU.add, ["tmpM", ("M1", g)], [("M1", g)])
                    if d == 1:
                        P.cp("dve", M1b[g][:], M1acc[g][:], [("M1", g)], [("M1b", g)])
            P.barrier()
            for d in range(2):
                g0 = d * 32 + ct * 8
                for g in range(8):
                    dg = g0 + g
                    wk = ("r", g)
                    P.ts("dve", Rm[g][:], identf[:], p_["cL"][:, dg:dg + 1], None, ALU.mult, None, ["ident_f", K_], [wk])
                    P.stt(Rm[g][:], jswap[:], p_["sL"][:, dg:dg + 1], Rm[g][:], ALU.mult, ALU.add, ["jswap", K_, wk], [wk])
                    ab = gcount % 2
                    gcount += 1
                    P.ts("dve", angt[ab][:], jrow[:, 0:L], p_["th8"][:, dg:dg + 1], None, ALU.mult, None, ["jrow", K_], [("angt", ab)])
                    scL.run(P, angt[ab][:], ("angt", ab), sj[g][:], cj[g][:], ("tab", g))
                P.memset("dve", init[:], 0.0, ["init"])
                order = list(range(NC)) if d == 0 else list(range(NC - 1, -1, -1))
                its = [(ci, g) for ci in order for g in range(8)]

                def views(g):
                    if d == 0:
                        return cj[g][:], sj[g][:]
                    return cj[g][:, ::-1], sj[g][:, ::-1]

                def stage1(n):
                    ci, g = its[n]
                    blk = slice(ci * L, (ci + 1) * L)
                    zb, b4 = n % 2, n % NBUF
                    wk = ("w", d, g)
                    cjv, sjv = views(g)
                    P.mm(pZ[zb][:, 0:L], Zw[d][g][:], U[:, g, blk], True, True, [wk, ("U", g)], [("pZ", zb)])
                    P.mm(pZs[zb][:, 0:L], Zsw[d][g][:], U[:, g, blk], True, True, [wk, ("U", g)], [("pZs", zb)])
                    P.tt("dve", zh[b4][:], pZ[zb][:, 0:L], cjv, ALU.mult, [("pZ", zb), ("tab", g)], [("zh", b4)])
                    P.tt("dve", zh2[b4][:], pZs[zb][:, 0:L], sjv, ALU.mult, [("pZs", zb), ("tab", g)], [("zh2", b4)])
                    P.tt("pool", zh[b4][:], zh[b4][:], zh2[b4][:], ALU.add, [("zh", b4), ("zh2", b4)], [("zh", b4)])

                def stage2(n):
                    ci, g = its[n]
                    b4 = n % NBUF
                    dg = g0 + g
                    cjv, sjv = views(g)
                    rb = p_["rho8"][:, dg:dg + 1].to_broadcast([128, L])
                    if d == 0:
                        go, zi = G[b4][:], zh[b4][:]
                    else:
                        go, zi = G[b4][:, ::-1], zh[b4][:, ::-1]
                    ini = init[:, g:g + 1]
                    P.op("dve", (lambda go=go, zi=zi, rb=rb, ini=ini: (lambda e: e.tensor_tensor_scan(
                        out=go, data0=rb, data1=zi, initial=ini, op0=ALU.mult, op1=ALU.add)))(),
                        [("zh", b4), K_, ("init", g)], [("G", b4)])
                    P.tt("pool", P1[b4][:], G[b4][:], cjv, ALU.mult, [("G", b4), ("tab", g)], [("P1", b4)])
                    P.tt("dve", P2[b4][:], G[b4][:], sjv, ALU.mult, [("G", b4), ("tab", g)], [("P2", b4)])

                def stage3(n):
                    ci, g = its[n]
                    blk = slice(ci * L, (ci + 1) * L)
                    b4 = n % NBUF
                    yb = n % 2
                    wk = ("w", d, g)
                    last = G[b4][:, L - 1:L] if d == 0 else G[b4][:, 0:1]
                    if d == 0:
                        P.mm(pY[yb][:, 0:L], M1b[g][:], U[:, g, blk], True, False, [("M1b", g), ("U", g)], [("pY", yb)])
                    P.mm(pY[yb][:, 0:L], Wa[d][g][:], P1[b4][:], d == 1, False, [wk, ("P1", b4)], [("pY", yb)])
                    P.mm(pY[yb][:, 0:L], Wb[d][g][:], P2[b4][:], False, True, [wk, ("P2", b4)], [("pY", yb)])
                    P.mm(pC[:, g:g + 1], Rm[g][:], last, True, True, [("r", g), ("G", b4)], [("pC", g)])
                    P.cp("act", init[:, g:g + 1], pC[:, g:g + 1], [("pC", g)], [("init", g)])
                    if d == 0:
                        P.cp("act", Yacc[:, g, blk], pY[yb][:, 0:L], [("pY", yb)], [("Yacc", g, ci)])
                    else:
                        P.tt("dve", Yacc[:, g, blk], pY[yb][:, 0:L], Yacc[:, g, blk], ALU.add, [("pY", yb), ("Yacc", g, ci)], [("Yacc", g, ci)])

                NI = len(its)
                for n in range(NI + 2):
                    if n < NI:
                        stage1(n)
                    if 1 <= n <= NI:
                        stage2(n - 1)
                    if n >= 2:
                        stage3(n - 2)
            CPU_ = UB // L
            for ub in range(NUB):
                cols = slice(ub * UB, (ub + 1) * UB)
                for g in range(8):
                    P.act(ygU[:, g, cols], Yacc[:, g, cols], AF.Gelu, [("Yacc", g, ci_) for ci_ in range(ub * CPU_, (ub + 1) * CPU_)], ["ut"])
                for j in range(8):
                    ps = alt[acount % 2]
                    pk = ("alt", acount % 2)
                    acount += 1
                    for g in range(8):
                        P.mm(ps[:, 0:UB], selbig[:, j, 112 - 16 * g:240 - 16 * g], ygU[:, g, cols], g == 0, g == 7, ["ut", "selbig"], [pk])
                    P.cp("act" if j % 2 == 0 else "dve", ytile[:, slice(j, UB * 8, 8)], ps[:, 0:UB], [pk], ["ytile"])
                P.dma("sp", X.ygT[ct, :, ub * UB * 8:(ub + 1) * UB * 8], ytile[:], ["ytile"], [("ygT", ct, ub)])
        P.barrier()
        P.emit()


def phaseC(X, l, xsrc):
    nc, P, S = X.nc, X.P, X.S
    I = X.ins
    NB = S // 512
    with ExitStack() as es:
        def T(name, shape, dt=F32):
            return sbt(es, nc, "pc_" + name, shape, dt)
        stage = [T(f"stage{i}", [128, 2048]) for i in range(3)]
        gw = load_weight_bf16(X, es, "pc_gw", I["glu_w"][l], 4, 512, "gw", stage)
        wo = load_weight_bf16(X, es, "pc_wo", I["w_out"][l], 8, 1024, "wo", stage)
        gb = T("gb", [128, 4])
        gsn = T("gsn", [128, 4])
        P.dma("sp", gb[:], colvec(I["glu_b"][l, :], 4), [], ["gb"], slow=True)
        P.dma("sp", gsn[:], colvec(I["ssm_norm_g"][l, :], 4), [], ["gsn"], slow=True)
        zt = T("zt", [128, 8, 1], BF16)
        P.memset("dve", zt[:], 0.0, ["zt"])
        P.dma("sp", X.h2T[:, :, 0:1].rearrange("c p t -> p c t"), zt[:], ["zt"], ["h2z0"], slow=True)
        P.dma("sp", X.h2T[:, :, S + 1:S + 2].rearrange("c p t -> p c t"), zt[:], ["zt"], ["h2z1"], slow=True)
        mix = [T(f"mix{i}", [128, 8, 512], BF16) for i in range(2)]
        yg = [T(f"yg{i}", [128, 4, 512], BF16) for i in range(2)]
        sig = [T(f"sig{i}", [128, 512]) for i in range(2)]
        y2 = T("y2", [128, 4, 512])
        sq = [T(f"sq{i}", [128, 512]) for i in range(2)]
        rstd = T("rstd", [128, 512])
        xt = [T(f"xt{i}", [128, 1024]) for i in range(6)]
        tmp = [T(f"tmp{i}", [128, 1024]) for i in range(2)]
        junk = T("junk", [128, 1024], BF16)
        xn = [T(f"xn{i}", [128, 1024], BF16) for i in range(2)]
        h2 = [T(f"h2{i}", [128, 8, 128], BF16) for i in range(2)]
        ss = T("ss", [128, 8])
        rs = T("rs", [128, 8])
        pG = [pst(es, nc, f"pc_pG{i}", [128, 512]) for i in range(2)]
        pSS = pst(es, nc, "pc_pSS", [128, 512])
        pW = [pst(es, nc, f"pc_pW{i}", [128, 512]) for i in range(2)]
        pT = [pst(es, nc, f"pc_pT{i}", [128, 8, 128], BF16) for i in range(2)]
        ident = X.cst["ident_bf"]
        ones = X.cst["ones_f"]
        gm, mT, g1bc = X.gm2[l], X.modT[l], X.g1bc[l]
        st = {"g": 0, "w": 0}

        def stageG(nb):
            b2 = nb % 2
            tok = slice(nb * 512, (nb + 1) * 512)
            P.dma("sp", yg[b2][:], X.ygT[:, :, tok].rearrange("c p t -> p c t"), [], [("yg", b2)])
            P.dma("pool", mix[b2][:, 0:4, :], X.yaT[:, :, tok].rearrange("c p t -> p c t"), [], [("mixa", b2)])
            for m in range(4):
                gbf = st["g"] % 2
                st["g"] += 1
                for k in range(4):
                    P.mm(pG[gbf][:], gw[:, k, m * 128:(m + 1) * 128], yg[b2][:, k, :], k == 0, k == 3, [("gw", k), ("yg", b2)], [("pG", gbf)])
                P.act(sig[gbf][:], pG[gbf][:], AF.Sigmoid, [("pG", gbf), "gb"], [("sig", gbf)], bias=gb[:, m:m + 1])
                P.tt("dve", y2[:, m, :], yg[b2][:, m, :], sig[gbf][:], ALU.mult, [("yg", b2), ("sig", gbf)], [("y2", m)])
                P.act(sq[gbf][:], y2[:, m, :], AF.Square, [("y2", m)], [("sq", gbf)])
                P.mm(pSS[:], ones[:], sq[gbf][:], m == 0, m == 3, [("sq", gbf), "ones_f"], ["pSS"])
            P.act(rstd[:], pSS[:], AF.Sqrt, ["pSS"], ["rstd"], bias=X.epsc[:, 0:1], scale=1.0 / 512)
            P.recip(rstd[:], rstd[:], ["rstd"], ["rstd"])
            for m in range(4):
                P.stt(mix[b2][:, 4 + m, :], y2[:, m, :], gsn[:, m:m + 1], rstd[:], ALU.mult, ALU.mult, [("y2", m), "gsn", "rstd"], [("mixs", b2, m)])

        def stage1(nb, tl):
            b2 = nb % 2
            tn = nb * 4 + tl
            xb = tn % 6
            t0 = nb * 512 + tl * 128
            mk = [("mixa", b2)] + [("mixs", b2, m) for m in range(4)]
            P.dma("sp", xt[xb][:], xsrc[t0:t0 + 128, :], [], [("xt", xb)])
            for hf in range(2):
                wb = st["w"] % 2
                st["w"] += 1
                for k in range(8):
                    P.mm(pW[wb][:], mix[b2][:, k, tl * 128:(tl + 1) * 128], wo[:, k, hf * 512:(hf + 1) * 512], k == 0, k == 7,
                         mk + [("wo", k)], [("pW", wb)])
                P.tt("dve", tmp[tn % 2][:, hf * 512:(hf + 1) * 512], pW[wb][:], g1bc[:, hf * 512:(hf + 1) * 512], ALU.mult,
                     [("pW", wb), ("g1bc", l)], [("tmp", tn % 2, hf)])
                P.tt("pool", xt[xb][:, hf * 512:(hf + 1) * 512], tmp[tn % 2][:, hf * 512:(hf + 1) * 512], xt[xb][:, hf * 512:(hf + 1) * 512], ALU.add,
                     [("tmp", tn % 2, hf), ("xt", xb)], [("xt", xb)])
            P.dma("pool", X.x1[t0:t0 + 128, :], xt[xb][:], [("xt", xb)], [("x1", t0)])

        def stage2(nb, tl):
            tn = nb * 4 + tl
            xb = tn % 6
            x2 = tn % 2
            sc = tn % 8
            t0 = nb * 512 + tl * 128
            P.act(junk[:], xt[xb][:], AF.Square, [("xt", xb)], ["junk", ("ss", sc)], accum=ss[:, sc:sc + 1])
            P.act(rs[:, sc:sc + 1], ss[:, sc:sc + 1], AF.Sqrt, [("ss", sc)], [("rs", sc)], bias=X.epsc[:, 0:1], scale=1.0 / D)
            P.recip(rs[:, sc:sc + 1], rs[:, sc:sc + 1], [("rs", sc)], [("rs", sc)])
            P.ts("dve", xn[x2][:], xt[xb][:], rs[:, sc:sc + 1], None, ALU.mult, None, [("xt", xb), ("rs", sc)], [("xn", x2)])
            for c in range(8):
                P.tr(pT[x2][:, c, :], xn[x2][:, c * 128:(c + 1) * 128], ident[:], [("xn", x2), "ident"], [("pT", x2)])
            for c in range(8):
                if x2 == 0:
                    P.act(h2[x2][:, c, :], pT[x2][:, c, :], AF.Identity, [("pT", x2), ("gm2", l), ("modT", l)], [("h2", x2)],
                          bias=mT[:, 24 + c:25 + c], scale=gm[:, c:c + 1])
                else:
                    P.ts("dve", h2[x2][:, c, :], pT[x2][:, c, :], gm[:, c:c + 1], mT[:, 24 + c:25 + c], ALU.mult, ALU.add,
                         [("pT", x2), ("gm2", l), ("modT", l)], [("h2", x2)])
            P.dma("sp", X.h2T[:, :, 1 + t0:1 + t0 + 128].rearrange("c p t -> p c t"), h2[x2][:], [("h2", x2)], [("h2T", t0)])

        tiles = [(nb, tl) for nb in range(NB) for tl in range(4)]
        SK = 3
        stageG(0)
        for i, (nb, tl) in enumerate(tiles):
            stage1(nb, tl)
            if tl == 1 and nb + 1 < NB:
                stageG(nb + 1)
            if i >= SK:
                stage2(*tiles[i - SK])
        for i in range(max(0, len(tiles) - SK), len(tiles)):
            stage2(*tiles[i])
        P.barrier()
        P.emit()


def phaseF(X, l, xdst):
    nc, P, S = X.nc, X.P, X.S
    I = X.ins
    NB = S // 512
    HP = NFF // 2
    for hp in range(2):
        with ExitStack() as es:
            def T(name, shape, dt=F32):
                return sbt(es, nc, f"pf{hp}_" + name, shape, dt)
            stage = [T(f"stage{i}", [128, 2048]) for i in range(3)]
            wu = T("wu", [128, 8, 2 * HP * 128], BF16)
            for k in range(8):
                for part in range(2):
                    c0 = part * DFF + hp * HP * 128
                    i = X.stage_i
                    X.stage_i += 1
                    st = stage[i % 3]
                    P.dma("sp" if i % 2 == 0 else "pool", st[:, 0:HP * 128], I["w_up"][l, k * 128:(k + 1) * 128, c0:c0 + HP * 128], [], [("stage", i % 3)])
                    P.cp(("dve", "pool", "act")[i % 3], wu[:, k, part * HP * 128:(part + 1) * HP * 128], st[:, 0:HP * 128], [("stage", i % 3)], [("wu", k)])
            wd = load_weight_bf16(X, es, f"pf{hp}_wd", I["w_down"][l, hp * HP * 128:(hp + 1) * HP * 128, :], HP, 1024, "wd", stage)
            cw = T("cw", [128, 3, 44])
            cb = T("cb", [128, 44])
            for c0 in range(0, 44, 11):
                for t in range(3):
                    P.dma("sp", cw[:, t, c0:c0 + 11], colvec(I["conv_w"][l, t, :], 44)[:, c0:c0 + 11], [], ["cw"], slow=True)
                P.dma("sp", cb[:, c0:c0 + 11], colvec(I["conv_b"][l, :], 44)[:, c0:c0 + 11], [], ["cb"], slow=True)
            h2 = [T(f"h2{i}", [128, 8, 512], BF16) for i in range(2)]
            cva = [T(f"cva{i}", [128, 512]) for i in range(2)]
            cvg = [T(f"cvg{i}", [128, 512]) for i in range(2)]
            sg = [T(f"sg{i}", [128, 512]) for i in range(2)]
            hid = [T(f"hid{i}", [128, HP, 512], BF16) for i in range(2)]
            xt = [T(f"xt{i}", [128, 1024]) for i in range(3)]
            tmp = [T(f"tmp{i}", [128, 1024]) for i in range(2)]
            pA = [pst(es, nc, f"pf{hp}_pA{i}", [128, 512]) for i in range(2)]
            pGt = [pst(es, nc, f"pf{hp}_pG{i}", [128, 512]) for i in range(2)]
            pW = [pst(es, nc, f"pf{hp}_pW{i}", [128, 512]) for i in range(2)]
            g2bc = X.g2bc[l]
            xin = X.x1 if hp == 0 else xdst
            icount = 0
            wcount = 0
            tcount = 0
            BT = 510
            blocks = [(t0, min(BT, S - t0)) for t0 in range(0, S, BT)]
            def load_h2(nb_):
                t0_, nt_ = blocks[nb_]
                P.dma("sp", h2[nb_ % 2][:, :, 0:nt_ + 2], X.h2T[:, :, t0_:t0_ + nt_ + 2].rearrange("c p t -> p c t"), [], [("h2", nb_ % 2)])

            load_h2(0)
            for nb, (t0, nt) in enumerate(blocks):
                b2 = nb % 2
                N = nt + 2
                for i in range(HP):
                    ib = icount % 2
                    icount += 1
                    for part, (pp, cv) in enumerate(((pA[ib], cva[ib]), (pGt[ib], cvg[ib]))):
                        col = part * 22 + hp * HP + i
                        wc = slice(part * HP * 128 + i * 128, part * HP * 128 + (i + 1) * 128)
                        for k in range(8):
                            P.mm(pp[:, 0:N], wu[:, k, wc], h2[b2][:, k, 0:N], k == 0, k == 7, [("wu", k), ("h2", b2)], [("pp", ib, part)])
                        ck = ("cv", ib, part)
                        P.act(cv[:, 0:nt], pp[:, 1:nt + 1], AF.Identity, [("pp", ib, part), "cw", "cb"], [ck], bias=cb[:, col:col + 1], scale=cw[:, 1, col:col + 1])
                        P.stt(cv[:, 0:nt], pp[:, 0:nt], cw[:, 0, col:col + 1], cv[:, 0:nt], ALU.mult, ALU.add, [("pp", ib, part), "cw", ck], [ck])
                        P.stt(cv[:, 0:nt], pp[:, 2:nt + 2], cw[:, 2, col:col + 1], cv[:, 0:nt], ALU.mult, ALU.add, [("pp", ib, part), "cw", ck], [ck])
                    P.act(sg[ib][:, 0:nt], cvg[ib][:, 0:nt], AF.Silu, [("cv", ib, 1)], [("sg", ib)])
                    P.tt("pool", hid[b2][:, i, 0:nt], sg[ib][:, 0:nt], cva[ib][:, 0:nt], ALU.mult, [("sg", ib), ("cv", ib, 0)], [("hid", b2, i)])
                hk = [("hid", b2, i) for i in range(HP)]
                if nb + 1 < len(blocks):
                    load_h2(nb + 1)
                for tl in range((nt + 127) // 128):
                    m = min(128, nt - tl * 128)
                    r0 = t0 + tl * 128
                    xb = tcount % 3
                    tb_ = tcount % 2
                    tcount += 1
                    P.dma("sp", xt[xb][0:m, :], xin[r0:r0 + m, :], [("xd", r0)], [("xt", xb)])
                    for hf in range(2):
                        wb = wcount % 2
                        wcount += 1
                        for i in range(HP):
                            P.mm(pW[wb][0:m, :], hid[b2][:, i, tl * 128:tl * 128 + m], wd[:, i, hf * 512:(hf + 1) * 512], i == 0, i == HP - 1,
                                 hk + [("wd", i)], [("pW", wb)])
                        P.tt("dve", tmp[tb_][0:m, hf * 512:(hf + 1) * 512], pW[wb][0:m, :], g2bc[0:m, hf * 512:(hf + 1) * 512], ALU.mult,
                             [("pW", wb), ("g2bc", l)], [("tmp", tb_, hf)])
                        P.tt("pool", xt[xb][0:m, hf * 512:(hf + 1) * 512], tmp[tb_][0:m, hf * 512:(hf + 1) * 512], xt[xb][0:m, hf * 512:(hf + 1) * 512], ALU.add,
                             [("tmp", tb_, hf), ("xt", xb)], [("xt", xb)])
                    P.dma("pool", xdst[r0:r0 + m, :], xt[xb][0:m, :], [("xt", xb)], [("xd", r0)])
            P.barrier()
            P.emit()


def build(S, debug=None, nlayers=DEPTH, phases=None, cut=0):
    nc = bass.Bass("TRN2", target_bir_lowering=False)
    X = Ctx()
    X.cut = cut
    import os
    X.evac = os.environ.get('EVAC', 'both')
    X.rowtile = os.environ.get('ROWTILE', '0') == '1'
    X.s8 = os.environ.get('S8', '1') == '1'
    X.nc, X.S = nc, S
    X.stage_i = 0
    dbg = set(debug or ())

    def din(name, shape, dt=F32):
        return nc.dram_tensor(name, list(shape), dt, kind="ExternalInput").ap()

    def dscr(name, shape, dt):
        kind = "ExternalOutput" if name in dbg else "Internal"
        return nc.dram_tensor(name, list(shape), dt, kind=kind).ap()

    X.ins = {"x": din("x", [S, D]), "pos": din("pos", [S], I32)}
    for k, shp in IN_SHAPES.items():
        X.ins[k] = din(k, shp)
    cin = {k: din("cst_" + k, shp, dt) for k, (shp, dt) in CONST_SHAPES.items()}
    X.out = nc.dram_tensor("out", [S, D], F32, kind="ExternalOutput").ap()
    X.modrow = dscr("modrow", [2, 6144], F32)
    X.cosT = dscr("cosT", [128, S], F32)
    X.sinT = dscr("sinT", [128, S], F32)
    X.qT = dscr("qT", [4, 128, S], BF16)
    X.kT = dscr("kT", [4, 128, S], BF16)
    X.vv = dscr("vv", [S, 512], BF16)
    X.uT = dscr("uT", [4, 128, S], BF16)
    X.ygT = dscr("ygT", [4, 128, S], BF16)
    X.yaT = dscr("yaT", [4, 128, S], BF16)
    X.x1 = dscr("x1", [S, D], F32)
    X.h2T = dscr("h2T", [8, 128, S + 2], BF16)
    X.xmid = dscr("xmid", [S, D], F32)
    with ExitStack() as es:
        P = Prog(nc, es)
        X.P = P
        X.cst = {}
        for k, (shp, dt) in CONST_SHAPES.items():
            X.cst[k] = sbt(es, nc, "c_" + k, shp, dt)
            P.dma("sp", X.cst[k][:], cin[k], [], [k])
        X.epsc = sbt(es, nc, "c_eps", [128, 1], F32)
        P.memset("dve", X.epsc[:], EPS, ["epsc"])
        X.modT = [sbt(es, nc, f"modT{l}", [128, 48], F32) for l in range(DEPTH)]
        X.g1bc = [sbt(es, nc, f"g1bc{l}", [128, 1024], F32) for l in range(DEPTH)]
        X.g2bc = [sbt(es, nc, f"g2bc{l}", [128, 1024], F32) for l in range(DEPTH)]
        X.gm1 = [sbt(es, nc, f"gm1{l}", [128, 8], F32) for l in range(DEPTH)]
        X.gm2 = [sbt(es, nc, f"gm2{l}", [128, 8], F32) for l in range(DEPTH)]
        P.barrier()
        phases = phases or ("0", "A", "B", "S", "C", "F")
        if "0" in phases:
            phase0(X)
        for l in range(nlayers):
            xsrc = X.ins["x"] if l == 0 else X.xmid
            xdst = X.xmid if l < DEPTH - 1 else X.out
            if "A" in phases:
                phaseA(X, l, xsrc)
            if "B" in phases:
                phaseB(X, l)
            if "S" in phases:
                if X.s8:
                    phaseS8(X, l)
                else:
                    phaseS(X, l)
            if "C" in phases:
                phaseC(X, l, xsrc)
            if "F" in phases:
                phaseF(X, l, xdst)
        P.barrier()
        P.emit(final=True)
    X.ninstr = P.ninstr
    return nc, X


_CACHE = {}


def kernel(**inputs):
    x = np.asarray(inputs["x"])
    B, S, _ = x.shape
    if S not in _CACHE:
        _CACHE[S] = build(S)[0]
    nc = _CACHE[S]
    consts = make_consts()
    n_cores = 8
    in_maps = []
    for i in range(n_cores):
        b = i % B
        m = {"x": np.ascontiguousarray(x[b]).astype(np.float32),
             "pos": np.ascontiguousarray(np.asarray(inputs["positions"])[b]).astype(np.int32),
             "c": np.ascontiguousarray(np.asarray(inputs["c"])[b]).astype(np.float32)}
        for k in IN_SHAPES:
            if k != "c":
                m[k] = np.ascontiguousarray(np.asarray(inputs[k])).astype(np.float32)
        for k, v in consts.items():
            m["cst_" + k] = v
        in_maps.append(m)
    res = run_bass_kernel_spmd(nc, in_maps, core_ids=list(range(n_cores)))
    out = np.stack([np.asarray(res.results[b]["out"]) for b in range(B)], axis=0)
    return out.astype(np.float32)
```

```python
import math
from contextlib import ExitStack
import numpy as np
import ml_dtypes
import concourse.bass as bass
import concourse.mybir as mybir
from concourse.bass_utils import run_bass_kernel_spmd

F32 = mybir.dt.float32
BF16 = mybir.dt.bfloat16
I32 = mybir.dt.int32
AF = mybir.ActivationFunctionType
ALU = mybir.AluOpType
AX = mybir.AxisListType

D = 1024
DEPTH = 2
DFF = 2816
NFF = DFF // 128
EPS = 1e-6
TWO_PI = 2.0 * math.pi
EPOCH = 30000
LCH = 512


class Prog:
    ENGS = ("pe", "act", "dve", "pool", "sp")

    def __init__(self, nc, es):
        self.nc = nc
        self.es = es
        self.nsem = 0
        self.ops = {e: [] for e in self.ENGS}
        self.cnt = {e: 0 for e in self.ENGS}
        self.sem = {e: self._newsem() for e in self.ENGS}
        self.pesems = {id(self.sem["pe"])}
        self.known = {e: {} for e in self.ENGS}
        self.lastw = {}
        self.readers = {}
        self.pend = {e: [] for e in self.ENGS}
        self.dpool = {e: [self._newsem() for _ in range(6)] for e in ("sp", "pool", "act")}
        self.dval = {e: [0] * 6 for e in self.dpool}
        self.drr = {e: 0 for e in self.dpool}
        self.semobj = {}
        self.ninstr = 0
        self.banklast = {}

    def _newsem(self):
        self.nsem += 1
        return self.es.enter_context(self.nc.semaphore(f"s{self.nsem}"))

    def _banks(self, *aps):
        ks = []
        for a in aps:
            if a is None or isinstance(a, (int, float)):
                continue
            if type(a.tensor).__name__ == "PSumTensorHandle":
                ks.append(a.name)
        return ks

    def op(self, eng, fn, r=(), w=(), dma=False, x=()):
        deps = list(self.pend[eng])
        self.pend[eng] = []
        for k in x:
            t = self.banklast.get(k)
            if t is not None and t[0] != eng:
                deps.append(t[1])
        for k in list(r) + list(w):
            t = self.lastw.get(k)
            if t is not None:
                deps.append(t)
        for k in w:
            deps.extend(self.readers.get(k, ()))
        if dma:
            i = self.drr[eng]
            self.drr[eng] = (i + 1) % len(self.dpool[eng])
            s = self.dpool[eng][i]
            if self.dval[eng][i] > 0:
                deps.append((s, self.dval[eng][i]))
            self.dval[eng][i] += 16
            tok = (s, self.dval[eng][i])
            amt = 16
        else:
            if self.cnt[eng] >= EPOCH:
                self.sem[eng] = self._newsem()
                self.cnt[eng] = 0
                if eng == "pe":
                    self.pesems.add(id(self.sem[eng]))
            self.cnt[eng] += 1
            tok = (self.sem[eng], self.cnt[eng])
            amt = 1
        waits = {}
        kn = self.known[eng]
        for (s, v) in deps:
            if eng == "pe" and id(s) in self.pesems:
                continue
            if v <= kn.get(id(s), 0):
                continue
            if id(s) not in waits or waits[id(s)][1] < v:
                waits[id(s)] = (s, v)
        for i_, (s, v) in waits.items():
            kn[i_] = v
        self.ops[eng].append((list(waits.values()), fn, tok[0], amt))
        self.ninstr += 1
        for k in x:
            self.banklast[k] = (eng, tok)
        for k in w:
            self.lastw[k] = tok
            self.readers[k] = []
        for k in r:
            self.readers.setdefault(k, []).append(tok)
        return tok

    def barrier(self):
        toks = []
        for e in self.ENGS:
            if self.cnt[e] > 0:
                toks.append((self.sem[e], self.cnt[e]))
        for e in self.dpool:
            for s, v in zip(self.dpool[e], self.dval[e]):
                if v > 0:
                    toks.append((s, v))
        for e in self.ENGS:
            self.pend[e].extend(toks)
        self.lastw = {}
        self.readers = {}
        self.banklast = {}

    def emit(self, final=False):
        nc = self.nc
        with nc.Block() as block:
            decos = {"pe": block.tensor, "act": block.scalar, "dve": block.vector,
                     "pool": block.gpsimd, "sp": block.sync}
            for eng in self.ENGS:
                ops = self.ops[eng]
                tail = []
                if final:
                    kn = self.known[eng]
                    for (s, v) in self.pend[eng]:
                        if eng == "pe" and id(s) in self.pesems:
                            continue
                        if v > kn.get(id(s), 0):
                            kn[id(s)] = v
                            tail.append((s, v))
                    self.pend[eng] = []

                def body(e, ops=ops, tail=tail):
                    for waits, fn, s, amt in ops:
                        for (ws, wv) in waits:
                            e.wait_ge(ws, wv)
                        fn(e).then_inc(s, amt)
                    for (ws, wv) in tail:
                        e.wait_ge(ws, wv)
                decos[eng](body)
        self.ops = {e: [] for e in self.ENGS}

    def mm(self, out, lhsT, rhs, start, stop, r, w, sgc=False):
        if sgc:
            return self.op("pe", lambda e: e.matmul(out, lhsT=lhsT, rhs=rhs, start=start, stop=stop, skip_group_check=True), r, w, x=self._banks(out))
        return self.op("pe", lambda e: e.matmul(out, lhsT=lhsT, rhs=rhs, start=start, stop=stop), r, w, x=self._banks(out))

    def tr(self, out, in_, ident, r, w):
        return self.op("pe", lambda e: e.transpose(out, in_, ident), r, w, x=self._banks(out))

    def act(self, out, in_, func, r, w, bias=None, scale=None, accum=None):
        kw = {}
        if bias is not None:
            kw["bias"] = bias
        if scale is not None:
            kw["scale"] = scale
        if accum is not None:
            kw["accum_out"] = accum
        return self.op("act", lambda e: e.activation(out=out, in_=in_, func=func, **kw), r, w, x=self._banks(out, in_, bias, scale))

    def ts(self, eng, out, in0, s1, s2, op0, op1, r, w):
        if op1 is None:
            return self.op(eng, lambda e: e.tensor_scalar(out=out, in0=in0, scalar1=s1, scalar2=None, op0=op0), r, w, x=self._banks(out, in0, s1))
        return self.op(eng, lambda e: e.tensor_scalar(out=out, in0=in0, scalar1=s1, scalar2=s2, op0=op0, op1=op1), r, w, x=self._banks(out, in0, s1, s2))

    def stt(self, out, in0, scalar, in1, op0, op1, r, w):
        return self.op("dve", lambda e: e.scalar_tensor_tensor(out=out, in0=in0, scalar=scalar, in1=in1, op0=op0, op1=op1), r, w, x=self._banks(out, in0, scalar, in1))

    def tt(self, eng, out, in0, in1, op, r, w):
        return self.op(eng, lambda e: e.tensor_tensor(out=out, in0=in0, in1=in1, op=op), r, w, x=self._banks(out, in0, in1))

    def cp(self, eng, out, in_, r, w):
        if eng == "act":
            return self.op("act", lambda e: e.copy(out=out, in_=in_), r, w, x=self._banks(out, in_))
        return self.op(eng, lambda e: e.tensor_copy(out=out, in_=in_), r, w, x=self._banks(out, in_))

    def recip(self, out, in_, r, w):
        return self.op("dve", lambda e: e.reciprocal(out=out, in_=in_), r, w, x=self._banks(out, in_))

    def memset(self, eng, ap, val, w):
        return self.op(eng, lambda e: e.memset(ap, val), (), w)

    def dma(self, q, out, in_, r, w, slow=False):
        if slow:
            return self.op(q, lambda e: e.dma_start(out=out, in_=in_, allow_slow_non_contiguous=True), r, w, dma=True)
        return self.op(q, lambda e: e.dma_start(out=out, in_=in_), r, w, dma=True)


def _rr(lst, i):
    return lst[i % len(lst)]


def make_consts():
    c = {}
    c["ident_bf"] = np.eye(128, dtype=np.float32).astype(ml_dtypes.bfloat16)
    c["ident_f"] = np.eye(128, dtype=np.float32)
    bo = np.zeros((128, 128), np.float32)
    bo[:64, :64] = 1.0
    bo[64:, 64:] = 1.0
    c["blockones"] = bo
    c["ones_f"] = np.ones((128, 128), np.float32)
    js = np.zeros((128, 128), np.float32)
    for k in range(128):
        js[k, (k + 64) % 128] = 1.0
    c["jswap"] = js
    inv = (np.float32(10000.0) ** (-(np.arange(0, 64, 2, dtype=np.float32)) / np.float32(64))).astype(np.float32)
    invf = np.zeros((128, 1), np.float32)
    for p in range(128):
        invf[p, 0] = inv[(p % 64) % 32]
    c["invf"] = invf
    gm = np.zeros((128, 8), np.float32)
    for p in range(128):
        gm[p, p // 16] = 1.0
    c["gmask"] = gm
    c["jrow"] = np.tile(np.arange(LCH, dtype=np.float32)[None, :], (128, 1))
    rm = np.zeros((128, 128), np.float32)
    for m_ in range(128):
        if (m_ % 64) < 32:
            rm[m_ + 32, m_] = -1.0
        else:
            rm[m_ - 32, m_] = 1.0
    c["rotmat"] = rm
    sb = np.zeros((128, 8, 240), np.float32)
    for x_ in range(8):
        for p in range(16):
            sb[16 * x_ + p, x_, 112 + p] = 1.0
    c["selbig"] = sb.astype(ml_dtypes.bfloat16)
    nm = np.zeros((128, 2, 128), np.float32)
    for r_ in range(128):
        for c_ in range(128):
            jp, jj = r_ // 16, c_ // 16
            if jp > jj:
                nm[r_, 0, c_] = -1.0
            if jp < jj:
                nm[r_, 1, c_] = -1.0
    c["nmask"] = nm
    return c


CONST_SHAPES = {"ident_bf": ([128, 128], BF16), "ident_f": ([128, 128], F32), "blockones": ([128, 128], F32),
                "ones_f": ([128, 128], F32), "jswap": ([128, 128], F32), "invf": ([128, 1], F32),
                "gmask": ([128, 8], F32), "jrow": ([128, LCH], F32),
                "selbig": ([128, 8, 240], BF16), "nmask": ([128, 2, 128], F32),
                "rotmat": ([128, 128], F32)}

IN_SHAPES = {
    "c": [1024], "ada_w": [2, 1024, 6144], "ada_b": [2, 6144], "norm1_g": [2, 1024],
    "w_in": [2, 1024, 2048], "q_norm_g": [2, 64], "k_norm_g": [2, 64], "lam_q1": [2, 64], "lam_k1": [2, 64],
    "lam_q2": [2, 64], "lam_k2": [2, 64], "subln_g": [2, 128], "ssm_a_re": [2, 2, 32, 64],
    "ssm_a_im": [2, 2, 32, 64], "ssm_log_dt": [2, 2, 32], "ssm_b_re": [2, 2, 32, 64, 16],
    "ssm_b_im": [2, 2, 32, 64, 16], "ssm_c_re": [2, 2, 32, 16, 64], "ssm_c_im": [2, 2, 32, 16, 64],
    "ssm_d": [2, 512], "glu_w": [2, 512, 512], "glu_b": [2, 512], "ssm_norm_g": [2, 512],
    "w_out": [2, 1024, 1024], "norm2_g": [2, 1024], "w_up": [2, 1024, 5632], "conv_w": [2, 3, 5632],
    "conv_b": [2, 5632], "w_down": [2, 2816, 1024],
}


class Ctx:
    pass


_UNIQ = [0]


def sbt(es, nc, name, shape, dt):
    _UNIQ[0] += 1
    return es.enter_context(nc.sbuf_tensor(f"{name}_{_UNIQ[0]}", list(shape), dt))


def pst(es, nc, name, shape, dt=F32):
    _UNIQ[0] += 1
    return es.enter_context(nc.psum_tensor(f"{name}_{_UNIQ[0]}", list(shape), dt))


def colvec(ap1d, n):
    return ap1d.rearrange("(c p) -> p c", p=128)


def phase0(X):
    nc, P, S = X.nc, X.P, X.S
    I = X.ins
    with ExitStack() as es:
        ct = sbt(es, nc, "p0_ct", [128, 8], F32)
        cond = sbt(es, nc, "p0_cond", [128, 8], F32)
        aw = [sbt(es, nc, f"p0_aw{i}", [128, 8, 512], F32) for i in range(2)]
        abr = sbt(es, nc, "p0_abr", [1, 6144], F32)
        mrow = sbt(es, nc, "p0_mrow", [1, 6144], F32)
        psr = [pst(es, nc, f"p0_ps{i}", [1, 512]) for i in range(2)]
        P.dma("sp", ct[:], colvec(I["c"], 8), [], ["ct"], slow=True)
        P.act(cond[:], ct[:], AF.Silu, ["ct"], ["cond"])
        for l in range(DEPTH):
            P.dma("pool", abr[:], I["ada_b"][l:l + 1, :], [], ["abr"])
            for n in range(12):
                b = n % 2
                P.dma("sp", aw[b][:], I["ada_w"][l, :, n * 512:(n + 1) * 512].rearrange("(k p) n -> p k n", p=128),
                      [], [("aw", b)])
                for k in range(8):
                    P.mm(psr[b][:], cond[:, k:k + 1], aw[b][:, k, :], k == 0, k == 7, ["cond", ("aw", b)], [("psr", b)])
                P.tt("dve", mrow[:, n * 512:(n + 1) * 512], psr[b][:], abr[:, n * 512:(n + 1) * 512], ALU.add,
                     [("psr", b), "abr"], [("mrow", n)])
            P.dma("sp", X.modrow[l:l + 1, :], mrow[:], [("mrow", n) for n in range(12)], [("modrow", l)])
            for c0 in range(0, 48, 16):
                P.dma("sp", X.modT[l][:, c0:c0 + 16], X.modrow[l, :].rearrange("(j p) -> p j", p=128)[:, c0:c0 + 16], [("modrow", l)], [("modT", l)], slow=True)
            P.dma("pool", X.g1bc[l][:], X.modrow[l, 2048:3072].partition_broadcast(128), [("modrow", l)], [("g1bc", l)])
            P.dma("pool", X.g2bc[l][:], X.modrow[l, 5120:6144].partition_broadcast(128), [("modrow", l)], [("g2bc", l)])
            n1 = sbt(es, nc, f"p0_n1_{l}", [128, 8], F32)
            n2 = sbt(es, nc, f"p0_n2_{l}", [128, 8], F32)
            P.dma("sp", n1[:], colvec(I["norm1_g"][l, :], 8), [], [("n1", l)], slow=True)
            P.dma("sp", n2[:], colvec(I["norm2_g"][l, :], 8), [], [("n2", l)], slow=True)
            P.stt(X.gm1[l][:], X.modT[l][:, 8:16], 1.0, n1[:], ALU.add, ALU.mult, [("modT", l), ("n1", l)], [("gm1", l)])
            P.stt(X.gm2[l][:], X.modT[l][:, 32:40], 1.0, n2[:], ALU.add, ALU.mult, [("modT", l), ("n2", l)], [("gm2", l)])
        CW = min(S, 2048)
        posi = sbt(es, nc, "p0_posi", [128, CW], I32)
        posf = sbt(es, nc, "p0_posf", [128, CW], F32)
        ang = sbt(es, nc, "p0_ang", [128, CW], F32)
        kf = sbt(es, nc, "p0_kf", [128, CW], F32)
        ki = sbt(es, nc, "p0_ki", [128, CW], I32)
        rr = sbt(es, nc, "p0_r", [128, CW], F32)
        tmp = sbt(es, nc, "p0_tmp", [128, CW], F32)
        r2 = sbt(es, nc, "p0_r2", [128, CW], F32)
        outs = sbt(es, nc, "p0_outs", [128, CW], F32)
        outc = sbt(es, nc, "p0_outc", [128, CW], F32)
        invf = X.cst["invf"]
        C1 = 6.28125
        C2 = TWO_PI - 6.28125
        for ci in range(S // CW):
            sl = slice(ci * CW, (ci + 1) * CW)
            P.dma("sp", posi[:], I["pos"][sl].partition_broadcast(128), [], ["posi"])
            P.cp("dve", posf[:], posi[:], ["posi"], ["posf"])
            P.ts("dve", ang[:], posf[:], invf[:, 0:1], None, ALU.mult, None, ["posf", "invf"], ["ang"])
            P.ts("dve", kf[:], ang[:], 1.0 / TWO_PI, None, ALU.mult, None, ["ang"], ["kf"])
            P.cp("dve", ki[:], kf[:], ["kf"], ["ki"])
            P.cp("dve", kf[:], ki[:], ["ki"], ["kf"])
            P.stt(rr[:], kf[:], -C1, ang[:], ALU.mult, ALU.add, ["kf", "ang"], ["rr"])
            P.stt(rr[:], kf[:], -C2, rr[:], ALU.mult, ALU.add, ["kf", "rr"], ["rr"])
            P.ts("dve", tmp[:], rr[:], math.pi, -TWO_PI, ALU.is_gt, ALU.mult, ["rr"], ["tmp"])
            P.tt("dve", rr[:], rr[:], tmp[:], ALU.add, ["rr", "tmp"], ["rr"])
            P.ts("dve", r2[:], rr[:], math.pi / 2, None, ALU.add, None, ["rr"], ["r2"])
            P.ts("dve", tmp[:], r2[:], math.pi, -TWO_PI, ALU.is_gt, ALU.mult, ["r2"], ["tmp"])
            P.tt("dve", r2[:], r2[:], tmp[:], ALU.add, ["r2", "tmp"], ["r2"])
            P.ts("dve", rr[:], rr[:], -math.pi, math.pi, ALU.max, ALU.min, ["rr"], ["rr"])
            P.ts("dve", r2[:], r2[:], -math.pi, math.pi, ALU.max, ALU.min, ["r2"], ["r2"])
            P.act(outs[:], rr[:], AF.Sin, ["rr"], ["outs"])
            P.act(outc[:], r2[:], AF.Sin, ["r2"], ["outc"])
            P.dma("sp", X.sinT[:, sl], outs[:], ["outs"], ["sinT"])
            P.dma("sp", X.cosT[:, sl], outc[:], ["outc"], ["cosT"])
        P.barrier()
        P.emit()


def load_weight_bf16(X, es, name, dram_ap, K, N, tag, stage):
    nc, P = X.nc, X.P
    wt = sbt(es, nc, name, [128, K, N], BF16)
    for k in range(K):
        for n0 in range(0, N, 2048):
            n1 = min(N, n0 + 2048)
            i = X.stage_i
            X.stage_i += 1
            st = stage[i % len(stage)]
            P.dma("sp" if i % 2 == 0 else "pool", st[:, 0:n1 - n0], dram_ap[k * 128:(k + 1) * 128, n0:n1], [], [("stage", i % len(stage))])
            eng = ("dve", "pool", "act")[i % 3]
            P.cp(eng, wt[:, k, n0:n1], st[:, 0:n1 - n0], [("stage", i % len(stage))], [(tag, k)])
    return wt


def phaseA(X, l, xsrc):
    nc, P, S = X.nc, X.P, X.S
    I = X.ins
    NB = S // 512
    with ExitStack() as es:
        stage = [sbt(es, nc, f"pa_stage{i}", [128, 2048], F32) for i in range(3)]
        win = load_weight_bf16(X, es, "pa_win", I["w_in"][l], 8, 2048, "win", stage)
        wkeys = [("win", k) for k in range(8)]
        gq = sbt(es, nc, "pa_gq", [128, 4], F32)
        for j, (nm, sc) in enumerate((("q_norm_g", 0.125), ("k_norm_g", 1.0))):
            g = I[nm][l, :]
            for m in range(2):
                P.dma("sp", gq[m * 64:(m + 1) * 64, 2 * j:2 * j + 1], g.rearrange("(d o) -> d o", o=1), [], ["gq"], slow=True)
                P.dma("sp", gq[m * 64:m * 64 + 32, 2 * j + 1:2 * j + 2], g[32:64].rearrange("(d o) -> d o", o=1), [], ["gq"], slow=True)
                P.dma("sp", gq[m * 64 + 32:m * 64 + 64, 2 * j + 1:2 * j + 2], g[0:32].rearrange("(d o) -> d o", o=1), [], ["gq"], slow=True)
        P.ts("dve", gq[:, 0:2], gq[:, 0:2], 0.125, None, ALU.mult, None, ["gq"], ["gq"])
        if getattr(X, "cut", 0) == 1:
            P.barrier(); P.emit(); return
        xt = [sbt(es, nc, f"pa_xt{i}", [128, 1024], F32) for i in range(4)]
        junk = [sbt(es, nc, f"pa_junk{i}", [128, 1024], BF16) for i in range(2)]
        xn = [sbt(es, nc, f"pa_xn{i}", [128, 1024], BF16) for i in range(8)]
        ss = sbt(es, nc, "pa_ss", [128, 8], F32)
        rs = sbt(es, nc, "pa_rs", [128, 8], F32)
        hT = [sbt(es, nc, f"pa_hT{i}", [128, 8, 512], BF16) for i in range(2)]
        cs = [sbt(es, nc, f"pa_cs{i}", [128, 2, 512], F32) for i in range(2)]
        sq = [sbt(es, nc, f"pa_sq{i}", [128, 512], F32) for i in range(2)]
        rawsb = [sbt(es, nc, f"pa_rawsb{i}", [128, 512], F32) for i in range(2)]
        rotmat = X.cst["rotmat"]
        rsb = [sbt(es, nc, f"pa_rsb{i}", [128, 512], F32) for i in range(2)]
        t1 = [sbt(es, nc, f"pa_t1{i}", [128, 512], F32) for i in range(2)]
        t2 = [sbt(es, nc, f"pa_t2{i}", [128, 512], F32) for i in range(2)]
        qo = [sbt(es, nc, f"pa_qo{i}", [128, 512], BF16) for i in range(3)]
        uo = [sbt(es, nc, f"pa_uo{i}", [128, 512], BF16) for i in range(3)]
        pT = [pst(es, nc, f"pa_pT{i}", [128, 8, 128], BF16) for i in range(2)]
        praw = [pst(es, nc, f"pa_raw{i}", [128, 512]) for i in range(2)]
        prot = [pst(es, nc, f"pa_rot{i}", [128, 512]) for i in range(2)]
        pss = pst(es, nc, "pa_pss", [128, 512])
        puv = pst(es, nc, "pa_puv", [128, 512])
        ident = X.cst["ident_bf"]
        bones = X.cst["blockones"]
        gm, mT = X.gm1[l], X.modT[l]
        st = {"m": 0, "u": 0}

        def stageA(nb):
            for tl in range(4):
                tt_ = nb * 4 + tl
                xb, jb, sc = tt_ % 4, tt_ % 2, tt_ % 8
                nbuf = tt_ % 8
                P.dma("sp", xt[xb][:], xsrc[tt_ * 128:(tt_ + 1) * 128, :], [], [("xt", xb)])
                P.act(junk[jb][:], xt[xb][:], AF.Square, [("xt", xb)], [("junk", jb), ("ss", sc)], accum=ss[:, sc:sc + 1])
                P.act(rs[:, sc:sc + 1], ss[:, sc:sc + 1], AF.Sqrt, [("ss", sc)], [("rs", sc)], bias=X.epsc[:, 0:1], scale=1.0 / D)
                P.recip(rs[:, sc:sc + 1], rs[:, sc:sc + 1], [("rs", sc)], [("rs", sc)])
                P.ts("dve", xn[nbuf][:], xt[xb][:], rs[:, sc:sc + 1], None, ALU.mult, None, [("xt", xb), ("rs", sc)], [("xn", nbuf)])

        def stageB(nb):
            hb = nb % 2
            P.dma("pool", cs[hb][:, 0, :], X.cosT[:, nb * 512:(nb + 1) * 512], [], [("cs", hb)])
            P.dma("pool", cs[hb][:, 1, :], X.sinT[:, nb * 512:(nb + 1) * 512], [], [("cs", hb)])
            for tl in range(4):
                tt_ = nb * 4 + tl
                nbuf, pb = tt_ % 8, tt_ % 2
                for c in range(8):
                    P.tr(pT[pb][:, c, :], xn[nbuf][:, c * 128:(c + 1) * 128], ident[:], [("xn", nbuf), "ident"], [("pT", pb)])
                for c in range(8):
                    dst = hT[hb][:, c, tl * 128:(tl + 1) * 128]
                    if tt_ % 2 == 0:
                        P.act(dst, pT[pb][:, c, :], AF.Identity, [("pT", pb), ("gm1", l), ("modT", l)], [("hT", hb, c, tl)],
                              bias=mT[:, c:c + 1], scale=gm[:, c:c + 1])
                    else:
                        P.ts("dve", dst, pT[pb][:, c, :], gm[:, c:c + 1], mT[:, c:c + 1], ALU.mult, ALU.add,
                             [("pT", pb), ("gm1", l), ("modT", l)], [("hT", hb, c, tl)])

        def stageP(nb):
            hb = nb % 2
            hk = lambda k: [("hT", hb, k, tl) for tl in range(4)]
            rbs = {}

            def m1(mi):
                rb = st["m"] % 2
                st["m"] += 1
                rbs[mi] = rb
                for k in range(8):
                    P.mm(praw[rb][:], win[:, k, mi * 128:(mi + 1) * 128], hT[hb][:, k, :], k == 0, k == 7, hk(k) + [("win", k)], [("raw", rb)])
                P.cp("act", rawsb[rb][:], praw[rb][:], [("raw", rb)], [("rawsb", rb)])
                P.act(sq[rb][:], praw[rb][:], AF.Square, [("raw", rb)], [("sq", rb)])

            def m2(mi):
                rb = rbs[mi]
                isq = mi < 4
                P.mm(prot[rb][:], rotmat[:], rawsb[rb][:], True, True, [("rawsb", rb), "rotmat"], [("rot", rb)])
                P.mm(pss[:], bones[:], sq[rb][:], True, True, [("sq", rb), "bones"], ["pss"])
                P.act(rsb[rb][:], pss[:], AF.Sqrt, ["pss"], [("rsb", rb)], bias=X.epsc[:, 0:1], scale=1.0 / 64)
                P.recip(rsb[rb][:], rsb[rb][:], [("rsb", rb)], [("rsb", rb)])
                gc = 0 if isq else 2
                P.stt(t1[rb][:], praw[rb][:], gq[:, gc:gc + 1], cs[hb][:, 0, :], ALU.mult, ALU.mult, [("raw", rb), "gq", ("cs", hb)], [("t1", rb)])
                P.stt(t2[rb][:], prot[rb][:], gq[:, gc + 1:gc + 2], cs[hb][:, 1, :], ALU.mult, ALU.mult, [("rot", rb), "gq", ("cs", hb)], [("t2", rb)])
                P.tt("pool", t1[rb][:], t1[rb][:], t2[rb][:], ALU.add, [("t1", rb), ("t2", rb)], [("t1", rb)])
                ob = st["u"] % 3
                st["u"] += 1
                P.tt("pool", qo[ob][:], t1[rb][:], rsb[rb][:], ALU.mult, [("t1", rb), ("rsb", rb)], [("qo", ob)])
                dst = (X.qT if isq else X.kT)[mi % 4, :, nb * 512:(nb + 1) * 512]
                P.dma("sp", dst, qo[ob][:], [("qo", ob)], [("qkT", mi, nb)])

            for mi in range(8):
                m1(mi)
                m2(mi)
            for ui in range(4):
                for k in range(8):
                    P.mm(puv[:], win[:, k, 1536 + ui * 128:1536 + (ui + 1) * 128], hT[hb][:, k, :], k == 0, k == 7, hk(k) + [("win", k)], ["puv"])
                ob = st["u"] % 3
                st["u"] += 1
                P.cp("act", uo[ob][:], puv[:], ["puv"], [("uo", ob)])
                P.dma("pool", X.uT[ui, :, nb * 512:(nb + 1) * 512], uo[ob][:], [("uo", ob)], [("uT", ui, nb)])
            for tl in range(4):
                for k in range(8):
                    P.mm(puv[:], hT[hb][:, k, tl * 128:(tl + 1) * 128], win[:, k, 1024:1536], k == 0, k == 7, [("hT", hb, k, tl), ("win", k)], ["puv"])
                ob = st["u"] % 3
                st["u"] += 1
                P.cp("act", uo[ob][:], puv[:], ["puv"], [("uo", ob)])
                P.dma("pool", X.vv[(nb * 4 + tl) * 128:(nb * 4 + tl + 1) * 128, :], uo[ob][:], [("uo", ob)], [("vv", nb, tl)])

        stageA(0)
        stageB(0)
        for nb in range(NB):
            if nb + 1 < NB:
                stageA(nb + 1)
            stageP(nb)
            if nb + 1 < NB:
                stageB(nb + 1)
        P.barrier()
        P.emit()


def phaseB(X, l):
    nc, P, S = X.nc, X.P, X.S
    I = X.ins
    NQ = S // 512
    NK = S // 128
    lam_init = 0.8 - 0.6 * math.exp(-0.3 * l)
    with ExitStack() as es:
        L4 = sbt(es, nc, "pb_L4", [128, 4], F32)
        pr = sbt(es, nc, "pb_pr", [128, 2], F32)
        ee = sbt(es, nc, "pb_ee", [128, 2], F32)
        nlam = sbt(es, nc, "pb_nlam", [128, 1], F32)
        gsub = sbt(es, nc, "pb_gsub", [128, 128], F32)
        kt = [[sbt(es, nc, f"pb_kt{i}_{m}", [128, S], BF16) for m in range(2)] for i in range(2)]
        vx = [sbt(es, nc, f"pb_vx{i}", [128, NK, 129], BF16) for i in range(2)]
        qt = [sbt(es, nc, f"pb_qt{i}", [128, 512], BF16) for i in range(2)]
        pb = [[sbt(es, nc, f"pb_p{i}_{m}", [128, 512], BF16) for m in range(2)] for i in range(3)]
        tbuf = [sbt(es, nc, f"pb_t{i}", [128, 128], F32) for i in range(4)]
        obuf = [sbt(es, nc, f"pb_o{i}", [128, 128], F32) for i in range(4)]
        jk = [sbt(es, nc, f"pb_jk{i}", [128, 128], F32) for i in range(4)]
        sm = sbt(es, nc, "pb_sm", [128, 8, 4], F32)
        ybf = [sbt(es, nc, f"pb_ybf{i}", [128, 128], BF16) for i in range(4)]
        yT = [sbt(es, nc, f"pb_yT{i}", [128, 512], BF16) for i in range(2)]
        pS = [[pst(es, nc, f"pb_pS{i}_{m}", [128, 512]) for m in range(2)] for i in range(2)]
        pO = [pst(es, nc, f"pb_pO{i}", [128, 512]) for i in range(3)]
        pTr = pst(es, nc, "pb_pTr", [128, 4, 128], BF16)
        ident = X.cst["ident_bf"]
        P.memset("dve", L4[:], 0.0, ["L4"])
        for j, nm in enumerate(("lam_q1", "lam_k1", "lam_q2", "lam_k2")):
            P.dma("sp", L4[0:64, j:j + 1], I[nm][l, :].rearrange("(d o) -> d o", o=1), [], ["L4"], slow=True)
        P.tt("dve", pr[:, 0:1], L4[:, 0:1], L4[:, 1:2], ALU.mult, ["L4"], ["pr"])
        P.tt("dve", pr[:, 1:2], L4[:, 2:3], L4[:, 3:4], ALU.mult, ["L4"], ["pr"])
        P.mm(pO[0][:, 0:2], X.cst["ones_f"][:], pr[:], True, True, ["pr", "ones_f"], ["pO0"])
        P.act(ee[:], pO[0][:, 0:2], AF.Exp, ["pO0"], ["ee"])
        P.tt("dve", nlam[:], ee[:, 1:2], ee[:, 0:1], ALU.subtract, ["ee"], ["nlam"])
        P.ts("dve", nlam[:], nlam[:], -lam_init, None, ALU.add, None, ["nlam"], ["nlam"])
        P.dma("sp", gsub[:], I["subln_g"][l, :].partition_broadcast(128), [], ["gsub"])
        P.ts("dve", gsub[:], gsub[:], 1.0 - lam_init, None, ALU.mult, None, ["gsub"], ["gsub"])
        for i in range(2):
            P.memset("pool", vx[i][:, :, 128:129], 1.0, [("vx", i)])
            P.memset("pool", kt[i][0][64:128, :], 0.0, [("kt", i)])
            P.memset("pool", kt[i][1][0:64, :], 0.0, [("kt", i)])
        reg = {}
        idx = 0
        for m in range(2):
            for j in range(4):
                reg[(m, j)] = (idx // 3, (idx % 3) * 129)
                idx += 1
        stt_ = {"p": 0, "e": 0}
        rounds = [(h, qb) for h in range(4) for qb in range(NQ)]
        pis = {}
        started = {}

        def load_head(h):
            hb = h % 2
            P.dma("sp", kt[hb][0][0:64, :], X.kT[h, 0:64, :], [], [("kt", hb)])
            P.dma("sp", kt[hb][1][64:128, :], X.kT[h, 64:128, :], [], [("kt", hb)])
            vsrc = X.vv[:, h * 128:(h + 1) * 128].rearrange("(kb p) e -> p kb e", p=128)
            KS = max(1, NK // 8)
            for k0 in range(0, NK, KS):
                P.dma("pool", vx[hb][:, k0:k0 + KS, 0:128], vsrc[:, k0:k0 + KS, :], [], [("vx", hb)])

        def load_q(r):
            h, qb = rounds[r]
            P.dma("sp", qt[r % 2][:], X.qT[h, :, qb * 512:(qb + 1) * 512], [], [("qt", r % 2)])

        def qk_exp(r, kb):
            h, qb = rounds[r]
            hb, qi, sb = h % 2, r % 2, kb % 2
            pi = stt_["p"] % 3
            stt_["p"] += 1
            pis[(r, kb)] = pi
            for m in range(2):
                P.mm(pS[sb][m][:], kt[hb][m][:, kb * 128:(kb + 1) * 128], qt[qi][:],
                     True, True, [("kt", hb), ("qt", qi)], [("pS", sb, m)])
            for m in range(2):
                P.act(pb[pi][m][:], pS[sb][m][:], AF.Exp, [("pS", sb, m)], [("p", pi, m)])

        def pv(r, kp):
            h, qb = rounds[r]
            hb = h % 2
            pi = pis.pop((r, kp))
            stset = started.setdefault(r, set())
            for m in range(2):
                for j in range(4):
                    bk, off = reg[(m, j)]
                    st = bk not in stset
                    stset.add(bk)
                    P.mm(pO[bk][:, off:off + 129], pb[pi][m][:, j * 128:(j + 1) * 128], vx[hb][:, kp, :],
                         st, kp == NK - 1, [("p", pi, m), ("vx", hb)], [("O", m, j)], sgc=True)

        def epilogue(r):
            h, qb = rounds[r]
            yi = r % 2
            ej = []
            for j in range(4):
                e8 = stt_["e"] % 8
                stt_["e"] += 1
                b1, o1 = reg[(0, j)]
                b2, o2 = reg[(1, j)]
                ej.append((j, e8, pO[b1][:, o1:o1 + 129], pO[b2][:, o2:o2 + 129], ("sm", e8)))
            for (j, e8, O1, O2, smk) in ej:
                P.recip(sm[:, e8, 0:1], O1[:, 128:129], [("O", 0, j)], [smk])
                P.recip(sm[:, e8, 1:2], O2[:, 128:129], [("O", 1, j)], [smk])
            for (j, e8, O1, O2, smk) in ej:
                P.tt("dve", sm[:, e8, 1:2], sm[:, e8, 1:2], nlam[:], ALU.mult, [smk, "nlam"], [smk])
            for (j, e8, O1, O2, smk) in ej:
                P.ts("dve", tbuf[j][:], O2[:, 0:128], sm[:, e8, 1:2], None, ALU.mult, None, [("O", 1, j), smk], [("tbuf", j)])
            for (j, e8, O1, O2, smk) in ej:
                P.stt(obuf[j][:], O1[:, 0:128], sm[:, e8, 0:1], tbuf[j][:], ALU.mult, ALU.add, [("O", 0, j), smk, ("tbuf", j)], [("obuf", j)])
            for (j, e8, O1, O2, smk) in ej:
                P.act(jk[j][:], obuf[j][:], AF.Square, [("obuf", j)], [("jk", j), smk], accum=sm[:, e8, 2:3])
            for (j, e8, O1, O2, smk) in ej:
                P.act(sm[:, e8, 3:4], sm[:, e8, 2:3], AF.Sqrt, [smk], [smk], bias=X.epsc[:, 0:1], scale=1.0 / 128)
            for (j, e8, O1, O2, smk) in ej:
                P.recip(sm[:, e8, 3:4], sm[:, e8, 3:4], [smk], [smk])
            for (j, e8, O1, O2, smk) in ej:
                P.stt(ybf[j][:], obuf[j][:], sm[:, e8, 3:4], gsub[:], ALU.mult, ALU.mult, [("obuf", j), smk, "gsub"], [("ybf", j)])
            for (j, e8, O1, O2, smk) in ej:
                P.tr(pTr[:, j, :], ybf[j][:], ident[:], [("ybf", j), "ident"], [("pTr", j)])
            for (j, e8, O1, O2, smk) in ej:
                P.cp("dve", yT[yi][:, j * 128:(j + 1) * 128], pTr[:, j, :], [("pTr", j)], [("yT", yi, j)])
            P.dma("sp", X.yaT[h, :, qb * 512:(qb + 1) * 512], yT[yi][:], [("yT", yi, j) for j in range(4)], [("yaT", h, qb)])

        load_head(0)
        load_q(0)
        qk_exp(0, 0)
        for r, (h, qb) in enumerate(rounds):
            if qb == 0 and h + 1 < 4:
                load_head(h + 1)
            if r + 1 < len(rounds):
                load_q(r + 1)
            for kb in range(1, NK + 1):
                if kb < NK:
                    qk_exp(r, kb)
                pv(r, kb - 1)
            if r + 1 < len(rounds):
                qk_exp(r + 1, 0)
            epilogue(r)
        P.barrier()
        P.emit()


class SinCos:
    def __init__(self, X, es, W, nm, nsets=2):
        nc = X.nc
        self.W = W
        self.n = 0
        self.sets = []
        for i in range(nsets):
            self.sets.append(dict(kf=sbt(es, nc, f"{nm}_kf{i}", [128, W], F32), ki=sbt(es, nc, f"{nm}_ki{i}", [128, W], I32),
                                  rr=sbt(es, nc, f"{nm}_rr{i}", [128, W], F32), tmp=sbt(es, nc, f"{nm}_tmp{i}", [128, W], F32),
                                  r2=sbt(es, nc, f"{nm}_r2{i}", [128, W], F32), key=(nm, i)))

    def run(self, P, ang, kang, osin, ocos, kout, w=None):
        sc = self.sets[self.n % len(self.sets)]
        self.n += 1
        w = w or self.W
        kf, ki, rr, tmp, r2, k = sc["kf"][:, 0:w], sc["ki"][:, 0:w], sc["rr"][:, 0:w], sc["tmp"][:, 0:w], sc["r2"][:, 0:w], sc["key"]
        C1 = 6.28125
        C2 = TWO_PI - 6.28125
        P.ts("dve", kf, ang, 1.0 / TWO_PI, None, ALU.mult, None, [kang], [k])
        P.cp("dve", ki, kf, [k], [k])
        P.cp("dve", kf, ki, [k], [k])
        P.stt(rr, kf, -C1, ang, ALU.mult, ALU.add, [k, kang], [k])
        P.stt(rr, kf, -C2, rr, ALU.mult, ALU.add, [k], [k])
        P.ts("dve", tmp, rr, math.pi, -TWO_PI, ALU.is_gt, ALU.mult, [k], [k])
        P.tt("dve", rr, rr, tmp, ALU.add, [k], [k])
        P.ts("dve", tmp, rr, -math.pi, TWO_PI, ALU.is_lt, ALU.mult, [k], [k])
        P.tt("dve", rr, rr, tmp, ALU.add, [k], [k])
        P.ts("dve", r2, rr, math.pi / 2, None, ALU.add, None, [k], [k])
        P.ts("dve", tmp, r2, math.pi, -TWO_PI, ALU.is_gt, ALU.mult, [k], [k])
        P.tt("dve", r2, r2, tmp, ALU.add, [k], [k])
        P.ts("dve", rr, rr, -math.pi, math.pi, ALU.max, ALU.min, [k], [k])
        P.ts("dve", r2, r2, -math.pi, math.pi, ALU.max, ALU.min, [k], [k])
        P.act(osin, rr, AF.Sin, [k], [kout])
        P.act(ocos, r2, AF.Sin, [k], [kout])


def phaseS(X, l):
    nc, P, S = X.nc, X.P, X.S
    I = X.ins
    NC = S // LCH
    L = LCH
    with ExitStack() as es:
        def T(name, shape, dt=F32):
            return sbt(es, nc, "ps_" + name, shape, dt)
        are, aim, ldt = T("are", [128, 64]), T("aim", [128, 64]), T("ldt", [128, 64])
        lre, dtt, rho, th = T("lre", [128, 64]), T("dtt", [128, 64]), T("rho", [128, 64]), T("th", [128, 64])
        cth, sth, thL, cL, sL = T("cth", [128, 64]), T("sth", [128, 64]), T("thL", [128, 64]), T("cL", [128, 64]), T("sL", [128, 64])
        abr, abi, den, nre, fre, fim, t64 = (T(n_, [128, 64]) for n_ in ("abr", "abi", "den", "nre", "fre", "fim", "t64"))
        bre, bim = T("bre", [64, 64, 16]), T("bim", [64, 64, 16])
        Bbr, Bbi, tb = T("Bbr", [64, 64, 16]), T("Bbi", [64, 64, 16]), T("tb", [64, 64, 16])
        sc64 = SinCos(X, es, 64, "ps_sc64")
        scL = SinCos(X, es, L, "ps_scL")
        jrow = X.cst["jrow"]
        gmask = X.cst["gmask"]
        dcol = T("dcol", [128, 4])
        P.dma("sp", dcol[:], colvec(I["ssm_d"][l, :], 4), [], ["dcol"], slow=True)
        for hf in range(2):
            for d_ in range(2):
                for gq_ in range(2):
                    cs_ = slice(d_ * 32 + gq_ * 16, d_ * 32 + gq_ * 16 + 16)
                    P.dma("sp", are[hf * 64:(hf + 1) * 64, cs_], I["ssm_a_re"][l, d_, gq_ * 16:gq_ * 16 + 16, :].rearrange("g n -> n g"), [], ["are"], slow=True)
                    P.dma("pool", aim[hf * 64:(hf + 1) * 64, cs_], I["ssm_a_im"][l, d_, gq_ * 16:gq_ * 16 + 16, :].rearrange("g n -> n g"), [], ["aim"], slow=True)
        P.dma("sp", ldt[:], I["ssm_log_dt"][l].rearrange("d g -> (d g)").partition_broadcast(128), [], ["ldt"])
        for d_ in range(2):
            for gq_ in range(2):
                cs_ = slice(d_ * 32 + gq_ * 16, d_ * 32 + gq_ * 16 + 16)
                P.dma("sp", bre[:, cs_, :], I["ssm_b_re"][l, d_, gq_ * 16:gq_ * 16 + 16].rearrange("g n p -> n g p"), [], ["bre"], slow=True)
                P.dma("pool", bim[:, cs_, :], I["ssm_b_im"][l, d_, gq_ * 16:gq_ * 16 + 16].rearrange("g n p -> n g p"), [], ["bim"], slow=True)
        pk = "sparam"
        P.ts("dve", lre[:], are[:], -1e-4, None, ALU.min, None, ["are"], [pk])
        P.act(dtt[:], ldt[:], AF.Exp, ["ldt"], ["dtt"])
        P.tt("dve", t64[:], lre[:], dtt[:], ALU.mult, [pk, "dtt"], ["t64"])
        P.act(rho[:], t64[:], AF.Exp, ["t64"], ["rho"])
        P.tt("dve", th[:], aim[:], dtt[:], ALU.mult, ["aim", "dtt"], ["th"])
        sc64.run(P, th[:], "th", sth[:], cth[:], "scth")
        P.ts("dve", thL[:], th[:], float(L), None, ALU.mult, None, ["th"], ["thL"])
        sc64.run(P, thL[:], "thL", sL[:], cL[:], "scL")
        P.ts("dve", sL[64:128, :], sL[64:128, :], -1.0, None, ALU.mult, None, ["scL"], ["scL"])
        P.tt("dve", abr[:], rho[:], cth[:], ALU.mult, ["rho", "scth"], ["abr"])
        P.tt("dve", abi[:], rho[:], sth[:], ALU.mult, ["rho", "scth"], ["abi"])
        P.tt("dve", den[:], lre[:], lre[:], ALU.mult, [pk], ["den"])
        P.tt("dve", t64[:], aim[:], aim[:], ALU.mult, ["aim"], ["t64"])
        P.tt("dve", den[:], den[:], t64[:], ALU.add, ["den", "t64"], ["den"])
        P.recip(den[:], den[:], ["den"], ["den"])
        P.ts("dve", nre[:], abr[:], -1.0, None, ALU.add, None, ["abr"], ["nre"])
        P.tt("dve", fre[:], nre[:], lre[:], ALU.mult, ["nre", pk], ["fre"])
        P.tt("dve", t64[:], abi[:], aim[:], ALU.mult, ["abi", "aim"], ["t64"])
        P.tt("dve", fre[:], fre[:], t64[:], ALU.add, ["fre", "t64"], ["fre"])
        P.tt("dve", fre[:], fre[:], den[:], ALU.mult, ["fre", "den"], ["fre"])
        P.tt("dve", fim[:], abi[:], lre[:], ALU.mult, ["abi", pk], ["fim"])
        P.tt("dve", t64[:], nre[:], aim[:], ALU.mult, ["nre", "aim"], ["t64"])
        P.tt("dve", fim[:], fim[:], t64[:], ALU.subtract, ["fim", "t64"], ["fim"])
        P.tt("dve", fim[:], fim[:], den[:], ALU.mult, ["fim", "den"], ["fim"])
        frb = fre[0:64, :].unsqueeze(2).to_broadcast([64, 64, 16])
        fib = fim[0:64, :].unsqueeze(2).to_broadcast([64, 64, 16])
        P.tt("dve", Bbr[:], bre[:], frb, ALU.mult, ["bre", "fre"], ["Bbr"])
        P.tt("dve", tb[:], bim[:], fib, ALU.mult, ["bim", "fim"], ["tb"])
        P.tt("dve", Bbr[:], Bbr[:], tb[:], ALU.subtract, ["Bbr", "tb"], ["Bbr"])
        P.tt("dve", Bbi[:], bim[:], frb, ALU.mult, ["bim", "fre"], ["Bbi"])
        P.tt("dve", tb[:], bre[:], fib, ALU.mult, ["bre", "fim"], ["tb"])
        P.tt("dve", Bbi[:], Bbi[:], tb[:], ALU.add, ["Bbi", "tb"], ["Bbi"])
        ut = T("ut", [128, S], BF16)
        Yacc = T("Yacc", [128, S])
        TT = T("TT", [128, 128])
        cn1, cn2 = T("cn1", [128, 2, 64]), T("cn2", [128, 2, 64])
        S1, S2 = T("S1", [128, 128]), T("S2", [128, 128])
        Zw = [T(f"Zw{g}", [128, 128], BF16) for g in range(8)]
        Zsw = [T(f"Zsw{g}", [128, 128], BF16) for g in range(8)]
        Wa = [T(f"Wa{g}", [128, 128], BF16) for g in range(8)]
        Wb = [T(f"Wb{g}", [128, 128], BF16) for g in range(8)]
        Rm = [T(f"Rm{g}", [128, 128]) for g in range(8)]
        cj = [T(f"cj{g}", [128, L]) for g in range(8)]
        sj = [T(f"sj{g}", [128, L]) for g in range(8)]
        angt = [T(f"angt{i}", [128, L]) for i in range(2)]
        init = T("init", [128, 8])
        zh = [T(f"zh{i}", [128, L]) for i in range(4)]
        zh2 = [T(f"zh2{i}", [128, L]) for i in range(4)]
        G = [T(f"G{i}", [128, L]) for i in range(4)]
        P1 = [T(f"P1{i}", [128, L], BF16) for i in range(4)]
        P2 = [T(f"P2{i}", [128, L], BF16) for i in range(4)]
        yg = [T(f"yg{i}", [128, L]) for i in range(2)]
        ygb = [T(f"ygb{i}", [128, L], BF16) for i in range(2)]
        pZ = [pst(es, nc, f"ps_pZ{i}", [128, 512]) for i in range(2)]
        pZs = [pst(es, nc, f"ps_pZs{i}", [128, 512]) for i in range(2)]
        pY = pst(es, nc, "ps_pY", [128, 512])
        pC = pst(es, nc, "ps_pC", [128, 512])
        pX = pst(es, nc, "ps_pX", [128, 512])
        identf = X.cst["ident_f"]
        jswap = X.cst["jswap"]
        gcount = 0
        for ct in range(4):
            P.dma("sp", ut[:], X.uT[ct, :, :], [], ["ut"])
            for d in range(2):
                g0 = d * 32 + ct * 8
                for q_, (src, col) in enumerate(((Bbr, 0), (Bbi, 64))):
                    P.tr(pX[:, col:col + 64], src[:, g0:g0 + 8, :].rearrange("n g p -> n (g p)"), identf[0:64, 0:64], ["Bbr", "Bbi", "ident_f"], ["pX"])
                P.cp("act", TT[:], pX[:, 0:128], ["pX"], ["TT"])
                P.dma("sp", cn1[:, 0, :], I["ssm_c_re"][l, d, ct * 8:(ct + 1) * 8].rearrange("g p n -> (g p) n"), [], ["cn1"])
                P.dma("sp", cn1[:, 1, :], I["ssm_c_im"][l, d, ct * 8:(ct + 1) * 8].rearrange("g p n -> (g p) n"), [], ["cn1"])
                P.dma("pool", cn2[:, 0, :], I["ssm_c_im"][l, d, ct * 8:(ct + 1) * 8].rearrange("g p n -> (g p) n"), [], ["cn2"])
                P.dma("pool", cn2[:, 1, :], I["ssm_c_re"][l, d, ct * 8:(ct + 1) * 8].rearrange("g p n -> (g p) n"), [], ["cn2"])
                P.tr(pX[:, 128:256], cn1[:].rearrange("q t n -> q (t n)"), identf[:], ["cn1", "ident_f"], ["pX"])
                P.tr(pX[:, 256:384], cn2[:].rearrange("q t n -> q (t n)"), identf[:], ["cn2", "ident_f"], ["pX"])
                P.cp("act", S1[:], pX[:, 128:256], ["pX"], ["S1"])
                P.cp("act", S2[:], pX[:, 256:384], ["pX"], ["S2"])
                for g in range(8):
                    dg = g0 + g
                    mk = gmask[:, g:g + 1]
                    wk = ("w", g)
                    P.ts("dve", Zw[g][:], TT[:], mk, None, ALU.mult, None, ["TT", "gmask"], [wk])
                    P.ts("dve", Zsw[g][:, 0:64], TT[:, 64:128], mk, None, ALU.mult, None, ["TT", "gmask"], [wk])
                    P.ts("dve", Zsw[g][:, 64:128], TT[:, 0:64], mk, -1.0, ALU.mult, ALU.mult, ["TT", "gmask"], [wk])
                    P.memset("pool", Wa[g][:], 0.0, [wk])
                    P.memset("pool", Wb[g][:], 0.0, [wk])
                    cs_ = slice(g * 16, (g + 1) * 16)
                    P.cp("pool", Wa[g][0:64, cs_], S1[0:64, cs_], ["S1"], [wk])
                    P.ts("pool", Wa[g][64:128, cs_], S1[64:128, cs_], -1.0, None, ALU.mult, None, ["S1"], [wk])
                    P.ts("pool", Wb[g][:, cs_], S2[:, cs_], -1.0, None, ALU.mult, None, ["S2"], [wk])
                    P.ts("dve", Rm[g][:], identf[:], cL[:, dg:dg + 1], None, ALU.mult, None, ["ident_f", "scL"], [wk])
                    P.stt(Rm[g][:], jswap[:], sL[:, dg:dg + 1], Rm[g][:], ALU.mult, ALU.add, ["jswap", "scL", wk], [wk])
                    ab = gcount % 2
                    gcount += 1
                    P.ts("dve", angt[ab][:], jrow[:], th[:, dg:dg + 1], None, ALU.mult, None, ["jrow", "th"], [("angt", ab)])
                    scL.run(P, angt[ab][:], ("angt", ab), sj[g][:], cj[g][:], ("tab", g))
                P.memset("dve", init[:], 0.0, ["init"])
                order = list(range(NC)) if d == 0 else list(range(NC - 1, -1, -1))
                its = [(ci, g) for ci in order for g in range(8)]

                def views(g):
                    if d == 0:
                        return cj[g][:], sj[g][:]
                    return cj[g][:, ::-1], sj[g][:, ::-1]

                def stage1(n):
                    ci, g = its[n]
                    tok = slice(ci * L, (ci + 1) * L)
                    zb, b4 = n % 2, n % 4
                    wk = ("w", g)
                    cjv, sjv = views(g)
                    P.mm(pZ[zb][:], Zw[g][:], ut[:, tok], True, True, [wk, "ut"], [("pZ", zb)])
                    P.mm(pZs[zb][:], Zsw[g][:], ut[:, tok], True, True, [wk, "ut"], [("pZs", zb)])
                    P.tt("dve", zh[b4][:], pZ[zb][:], cjv, ALU.mult, [("pZ", zb), ("tab", g)], [("zh", b4)])
                    P.tt("dve", zh2[b4][:], pZs[zb][:], sjv, ALU.mult, [("pZs", zb), ("tab", g)], [("zh2", b4)])
                    P.tt("pool", zh[b4][:], zh[b4][:], zh2[b4][:], ALU.add, [("zh", b4), ("zh2", b4)], [("zh", b4)])

                def stage2(n):
                    ci, g = its[n]
                    b4 = n % 4
                    dg = g0 + g
                    cjv, sjv = views(g)
                    rb = rho[:, dg:dg + 1].to_broadcast([128, L])
                    if d == 0:
                        go, zi = G[b4][:], zh[b4][:]
                    else:
                        go, zi = G[b4][:, ::-1], zh[b4][:, ::-1]
                    ini = init[:, g:g + 1]
                    P.op("dve", (lambda go=go, zi=zi, rb=rb, ini=ini: (lambda e: e.tensor_tensor_scan(
                        out=go, data0=rb, data1=zi, initial=ini, op0=ALU.mult, op1=ALU.add)))(),
                        [("zh", b4), "rho", ("init", g)], [("G", b4)])
                    P.tt("pool", P1[b4][:], G[b4][:], cjv, ALU.mult, [("G", b4), ("tab", g)], [("P1", b4)])
                    P.tt("dve", P2[b4][:], G[b4][:], sjv, ALU.mult, [("G", b4), ("tab", g)], [("P2", b4)])

                def stage3(n):
                    ci, g = its[n]
                    tok = slice(ci * L, (ci + 1) * L)
                    b4 = n % 4
                    wk = ("w", g)
                    last = G[b4][:, L - 1:L] if d == 0 else G[b4][:, 0:1]
                    P.mm(pY[:], Wa[g][:], P1[b4][:], g == 0, False, [wk, ("P1", b4)], ["pY"])
                    P.mm(pY[:], Wb[g][:], P2[b4][:], False, g == 7, [wk, ("P2", b4)], ["pY"])
                    P.mm(pC[:, g:g + 1], Rm[g][:], last, True, True, [wk, ("G", b4)], [("pC", g)])
                    P.cp("act", init[:, g:g + 1], pC[:, g:g + 1], [("pC", g)], [("init", g)])
                    if g == 7:
                        if d == 0:
                            P.cp("act", Yacc[:, tok], pY[:], ["pY"], [("Yacc", ci)])
                        else:
                            P.tt("dve", Yacc[:, tok], pY[:], Yacc[:, tok], ALU.add, ["pY", ("Yacc", ci)], [("Yacc", ci)])

                NI = len(its)
                for n in range(NI + 2):
                    if n < NI:
                        stage1(n)
                    if 1 <= n <= NI:
                        stage2(n - 1)
                    if n >= 2:
                        stage3(n - 2)
            for ci in range(NC):
                tok = slice(ci * L, (ci + 1) * L)
                yb = ci % 2
                P.stt(yg[yb][:], ut[:, tok], dcol[:, ct:ct + 1], Yacc[:, tok], ALU.mult, ALU.add, ["ut", "dcol", ("Yacc", ci)], [("yg", yb)])
                P.act(ygb[yb][:], yg[yb][:], AF.Gelu, [("yg", yb)], [("ygb", yb)])
                P.dma("pool", X.ygT[ct, :, tok], ygb[yb][:], [("ygb", yb)], [("ygT", ct, ci)])
        P.barrier()
        P.emit()


def phaseS8(X, l):
    nc, P, S = X.nc, X.P, X.S
    I = X.ins
    NBLK = S // 8
    L = min(256, NBLK)
    NC = NBLK // L
    UB = min(512, NBLK)
    NUB = NBLK // UB
    with ExitStack() as es:
        def T(name, shape, dt=F32):
            return sbt(es, nc, "s8_" + name, shape, dt)
        names = ("are", "aim", "ldt", "lre", "dtt", "rho", "rho8", "th", "th8", "cth", "sth", "thL", "cL", "sL",
                 "abr", "abi", "den", "nre", "fre", "fim", "t64", "t64b", "air", "aii")
        pr = {n_: T(n_, [128, 64]) for n_ in names}
        PWr, PWi, PNr, PNi = T("PWr", [128, 64, 9]), T("PWi", [128, 64, 9]), T("PNr", [128, 64, 9]), T("PNi", [128, 64, 9])
        Bbr, Bbi = T("Bbr", [128, 64, 16]), T("Bbi", [128, 64, 16])
        dvec8 = T("dvec8", [128, 32])
        sg = T("sg", [128, 4])
        es2 = ExitStack()
        bre, bim = sbt(es2, nc, "s8_bre", [128, 64, 16], F32), sbt(es2, nc, "s8_bim", [128, 64, 16], F32)
        tB = sbt(es2, nc, "s8_tB", [128, 64, 16], F32)
        sc64 = SinCos(X, es2, 64, "s8_sc64")
        jrow = X.cst["jrow"]
        identf, jswap = X.cst["ident_f"], X.cst["jswap"]
        selbig = X.cst["selbig"]
        nmask = X.cst["nmask"]
        K_ = "prm"

        def V(out, a, b, op):
            P.tt("dve", out, a, b, op, [K_], [K_])

        def cmul(o_r, o_i, a_r, a_i, b_r, b_i, t1, t2):
            V(t1, a_r, b_r, ALU.mult)
            V(t2, a_i, b_i, ALU.mult)
            V(o_r, t1, t2, ALU.subtract)
            V(t1, a_r, b_i, ALU.mult)
            V(t2, a_i, b_r, ALU.mult)
            V(o_i, t1, t2, ALU.add)

        for j in range(8):
            P.dma("sp", dvec8[j * 16:(j + 1) * 16, :], I["ssm_d"][l, :].rearrange("(g p) -> p g", p=16), [], [K_], slow=True)
        P.memset("dve", sg[0:64, 0:1], 1.0, [K_])
        P.memset("dve", sg[64:128, 0:1], -1.0, [K_])
        P.memset("dve", sg[0:64, 1:2], -1.0, [K_])
        P.memset("dve", sg[64:128, 1:2], 1.0, [K_])
        P.memset("dve", sg[0:64, 2:3], 1.0, [K_])
        P.memset("dve", sg[64:128, 2:3], 0.0, [K_])
        P.memset("dve", sg[0:64, 3:4], 0.0, [K_])
        P.memset("dve", sg[64:128, 3:4], 1.0, [K_])
        for hf in range(2):
            hs = slice(hf * 64, (hf + 1) * 64)
            for d_ in range(2):
                for gq_ in range(2):
                    cs_ = slice(d_ * 32 + gq_ * 16, d_ * 32 + gq_ * 16 + 16)
                    gs_ = slice(gq_ * 16, gq_ * 16 + 16)
                    P.dma("sp", pr["are"][hs, cs_], I["ssm_a_re"][l, d_, gs_, :].rearrange("g n -> n g"), [], [K_], slow=True)
                    P.dma("pool", pr["aim"][hs, cs_], I["ssm_a_im"][l, d_, gs_, :].rearrange("g n -> n g"), [], [K_], slow=True)
                    P.dma("sp", bre[hs, cs_, :], I["ssm_b_re"][l, d_, gs_].rearrange("g n p -> n g p"), [], [K_], slow=True)
                    P.dma("pool", bim[hs, cs_, :], I["ssm_b_im"][l, d_, gs_].rearrange("g n p -> n g p"), [], [K_], slow=True)
        P.dma("sp", pr["ldt"][:], I["ssm_log_dt"][l].rearrange("d g -> (d g)").partition_broadcast(128), [], [K_])
        p_ = pr
        P.ts("dve", p_["lre"][:], p_["are"][:], -1e-4, None, ALU.min, None, [K_], [K_])
        P.act(p_["dtt"][:], p_["ldt"][:], AF.Exp, [K_], [K_])
        V(p_["t64"][:], p_["lre"][:], p_["dtt"][:], ALU.mult)
        P.act(p_["rho"][:], p_["t64"][:], AF.Exp, [K_], [K_])
        P.act(p_["rho8"][:], p_["t64"][:], AF.Exp, [K_], [K_], scale=8.0)
        V(p_["th"][:], p_["aim"][:], p_["dtt"][:], ALU.mult)
        sc64.run(P, p_["th"][:], K_, p_["sth"][:], p_["cth"][:], K_)
        P.ts("dve", p_["th8"][:], p_["th"][:], 8.0, None, ALU.mult, None, [K_], [K_])
        P.ts("dve", p_["thL"][:], p_["th8"][:], float(L), None, ALU.mult, None, [K_], [K_])
        sc64.run(P, p_["thL"][:], K_, p_["sL"][:], p_["cL"][:], K_)
        P.ts("dve", p_["sL"][64:128, :], p_["sL"][64:128, :], -1.0, None, ALU.mult, None, [K_], [K_])
        V(p_["abr"][:], p_["rho"][:], p_["cth"][:], ALU.mult)
        V(p_["abi"][:], p_["rho"][:], p_["sth"][:], ALU.mult)
        V(p_["den"][:], p_["lre"][:], p_["lre"][:], ALU.mult)
        V(p_["t64"][:], p_["aim"][:], p_["aim"][:], ALU.mult)
        V(p_["den"][:], p_["den"][:], p_["t64"][:], ALU.add)
        P.recip(p_["den"][:], p_["den"][:], [K_], [K_])
        P.ts("dve", p_["nre"][:], p_["abr"][:], -1.0, None, ALU.add, None, [K_], [K_])
        V(p_["fre"][:], p_["nre"][:], p_["lre"][:], ALU.mult)
        V(p_["t64"][:], p_["abi"][:], p_["aim"][:], ALU.mult)
        V(p_["fre"][:], p_["fre"][:], p_["t64"][:], ALU.add)
        V(p_["fre"][:], p_["fre"][:], p_["den"][:], ALU.mult)
        V(p_["fim"][:], p_["abi"][:], p_["lre"][:], ALU.mult)
        V(p_["t64"][:], p_["nre"][:], p_["aim"][:], ALU.mult)
        V(p_["fim"][:], p_["fim"][:], p_["t64"][:], ALU.subtract)
        V(p_["fim"][:], p_["fim"][:], p_["den"][:], ALU.mult)
        frb = p_["fre"][:].unsqueeze(2).to_broadcast([128, 64, 16])
        fib = p_["fim"][:].unsqueeze(2).to_broadcast([128, 64, 16])
        V(Bbr[:], bre[:], frb, ALU.mult)
        V(tB[:], bim[:], fib, ALU.mult)
        V(Bbr[:], Bbr[:], tB[:], ALU.subtract)
        V(Bbi[:], bim[:], frb, ALU.mult)
        V(tB[:], bre[:], fib, ALU.mult)
        V(Bbi[:], Bbi[:], tB[:], ALU.add)
        P.memset("dve", PWr[:, :, 0], 1.0, [K_])
        P.memset("dve", PWi[:, :, 0], 0.0, [K_])
        P.memset("dve", PNr[:, :, 0], 1.0, [K_])
        P.memset("dve", PNi[:, :, 0], 0.0, [K_])
        P.cp("dve", PWr[:, :, 1], p_["abr"][:], [K_], [K_])
        P.cp("dve", PWi[:, :, 1], p_["abi"][:], [K_], [K_])
        V(p_["den"][:], p_["abr"][:], p_["abr"][:], ALU.mult)
        V(p_["t64"][:], p_["abi"][:], p_["abi"][:], ALU.mult)
        V(p_["den"][:], p_["den"][:], p_["t64"][:], ALU.add)
        P.recip(p_["den"][:], p_["den"][:], [K_], [K_])
        V(p_["air"][:], p_["abr"][:], p_["den"][:], ALU.mult)
        V(p_["aii"][:], p_["abi"][:], p_["den"][:], ALU.mult)
        P.ts("dve", p_["aii"][:], p_["aii"][:], -1.0, None, ALU.mult, None, [K_], [K_])
        P.cp("dve", PNr[:, :, 1], p_["air"][:], [K_], [K_])
        P.cp("dve", PNi[:, :, 1], p_["aii"][:], [K_], [K_])
        for k in range(1, 8):
            cmul(PWr[:, :, k + 1], PWi[:, :, k + 1], PWr[:, :, k], PWi[:, :, k], p_["abr"][:], p_["abi"][:], p_["t64"][:], p_["t64b"][:])
            cmul(PNr[:, :, k + 1], PNi[:, :, k + 1], PNr[:, :, k], PNi[:, :, k], p_["air"][:], p_["aii"][:], p_["t64"][:], p_["t64b"][:])
        P.barrier()
        P.emit()
        es2.close()
        scL = SinCos(X, es, L, "s8_scL", nsets=1)
        ut = T("ut", [128, S], BF16)
        ygU = ut[:].rearrange("p (g c) -> p g c", g=8)
        U = T("U", [128, 8, NBLK], BF16)
        Yflat = T("Yacc", [128, max(8 * NBLK, 7168)])
        Yacc = Yflat[:, 0:8 * NBLK].rearrange("p (g c) -> p g c", g=8)
        alias = True
        if alias:
            Yf = Yflat[:]
            prA, prB, prC, prD = (Yf[:, i * 1024:(i + 1) * 1024].rearrange("p (g j q) -> p g j q", g=8, j=8) for i in range(4))
            XM, WaF, WbF = (Yf[:, i * 1024:(i + 1) * 1024].rearrange("p (g m) -> p g m", g=8) for i in range(4, 7))
        else:
            prA, prB, prC, prD = (T(n_, [128, 8, 8, 16]) for n_ in ("prA", "prB", "prC", "prD"))
            XM, WaF, WbF = T("XM", [128, 8, 128]), T("WaF", [128, 8, 128]), T("WbF", [128, 8, 128])
        cn1, cn2 = T("cn1", [128, 2, 64]), T("cn2", [128, 2, 64])
        S1, S2 = T("S1", [128, 8, 16]), T("S2", [128, 8, 16])
        Zw = [[T(f"Zw{d}_{g}", [128, 128], BF16) for g in range(8)] for d in range(2)]
        Zsw = [[T(f"Zsw{d}_{g}", [128, 128], BF16) for g in range(8)] for d in range(2)]
        Wa = [[T(f"Wa{d}_{g}", [128, 128], BF16) for g in range(8)] for d in range(2)]
        Wb = [[T(f"Wb{d}_{g}", [128, 128], BF16) for g in range(8)] for d in range(2)]
        M1acc = [T(f"M1a{g}", [128, 128]) for g in range(8)]
        M1b = [T(f"M1b{g}", [128, 128], BF16) for g in range(8)]
        tmpM = T("tmpM", [128, 128])
        Rm = [T(f"Rm{g}", [128, 128]) for g in range(8)]
        cj = [T(f"cj{g}", [128, L]) for g in range(8)]
        sj = [T(f"sj{g}", [128, L]) for g in range(8)]
        angt = [T(f"angt{i}", [128, L]) for i in range(2)]
        init = T("init", [128, 8])
        NBUF = 4
        zh = [T(f"zh{i}", [128, L]) for i in range(NBUF)]
        zh2 = [T(f"zh2{i}", [128, L]) for i in range(NBUF)]
        G = [T(f"G{i}", [128, L]) for i in range(NBUF)]
        P1 = [T(f"P1{i}", [128, L], BF16) for i in range(NBUF)]
        P2 = [T(f"P2{i}", [128, L], BF16) for i in range(NBUF)]
        ytile = T("ytile", [128, UB * 8], BF16)
        pZ = [pst(es, nc, f"s8_pZ{i}", [128, 512]) for i in range(2)]
        pZs = [pst(es, nc, f"s8_pZs{i}", [128, 512]) for i in range(2)]
        pY = [pst(es, nc, f"s8_pY{i}", [128, 512]) for i in range(2)]
        pC = pst(es, nc, "s8_pC", [128, 512])
        pX = pst(es, nc, "s8_pX", [128, 512])
        alt = [pX, pY[0]]
        acount = 0
        gcount = 0
        for ct in range(4):
            P.dma("sp", ut[:], X.uT[ct, :, :], [], ["ut"])
            for g in range(8):
                for ub in range(NUB):
                    ps = alt[acount % 2]
                    pk = ("alt", acount % 2)
                    acount += 1
                    for j in range(8):
                        P.mm(ps[:, 0:UB], selbig[:, g, 112 - 16 * j:240 - 16 * j], ut[:, slice(ub * UB * 8 + j, (ub + 1) * UB * 8, 8)],
                             j == 0, j == 7, ["ut", "selbig"], [pk])
                    P.cp("act", U[:, g, ub * UB:(ub + 1) * UB], ps[:, 0:UB], [pk], [("U", g)])
            P.barrier()
            for d in range(2):
                g0 = d * 32 + ct * 8
                gs = slice(g0, g0 + 8)
                kk = slice(7, None, -1) if d == 0 else slice(0, 8)
                P.dma("sp", cn1[:, 0, :], I["ssm_c_re"][l, d, ct * 8:(ct + 1) * 8].rearrange("g p n -> (g p) n"), [], ["cn1"])
                P.dma("sp", cn1[:, 1, :], I["ssm_c_im"][l, d, ct * 8:(ct + 1) * 8].rearrange("g p n -> (g p) n"), [], ["cn1"])
                P.dma("pool", cn2[:, 0, :], I["ssm_c_im"][l, d, ct * 8:(ct + 1) * 8].rearrange("g p n -> (g p) n"), [], ["cn2"])
                P.dma("pool", cn2[:, 1, :], I["ssm_c_re"][l, d, ct * 8:(ct + 1) * 8].rearrange("g p n -> (g p) n"), [], ["cn2"])
                P.tr(pX[:, 0:128], cn1[:].rearrange("q t n -> q (t n)"), identf[:], ["cn1", "ident_f"], ["pXa"])
                P.tr(pX[:, 128:256], cn2[:].rearrange("q t n -> q (t n)"), identf[:], ["cn2", "ident_f"], ["pXb"])
                P.cp("act", S1[:].rearrange("q g p -> q (g p)"), pX[:, 0:128], ["pXa"], [K_])
                P.cp("act", S2[:].rearrange("q g p -> q (g p)"), pX[:, 128:256], ["pXb"], [K_])
                shp = [128, 8, 8, 16]
                Pr2 = PWr[:, gs, kk].unsqueeze(3).to_broadcast(shp)
                Pi2 = PWi[:, gs, kk].unsqueeze(3).to_broadcast(shp)
                Pcr = PNr[:, gs, kk].unsqueeze(3).to_broadcast(shp)
                Pci = PNi[:, gs, kk].unsqueeze(3).to_broadcast(shp)
                Brb = Bbr[:, gs, :].unsqueeze(2).to_broadcast(shp)
                Bib = Bbi[:, gs, :].unsqueeze(2).to_broadcast(shp)
                S1b = S1[:].unsqueeze(2).to_broadcast(shp)
                S2b = S2[:].unsqueeze(2).to_broadcast(shp)
                XMv = XM[:].rearrange("q g (j p) -> q g j p", p=16)
                WaFv = WaF[:].rearrange("q g (j p) -> q g j p", p=16)
                WbFv = WbF[:].rearrange("q g (j p) -> q g j p", p=16)
                V(prA[:], Brb, Pr2, ALU.mult)
                V(prC[:], Bib, Pi2, ALU.mult)
                V(prA[:], prA[:], prC[:], ALU.subtract)
                V(prB[:], Brb, Pi2, ALU.mult)
                V(prC[:], Bib, Pr2, ALU.mult)
                V(prB[:], prB[:], prC[:], ALU.add)
                P.ts("dve", XMv, prA[:], sg[:, 2:3], None, ALU.mult, None, [K_], [K_])
                P.stt(XMv, prB[:], sg[:, 3:4], XMv, ALU.mult, ALU.add, [K_], [K_])
                V(prA[:], S1b, Pcr, ALU.mult)
                V(prB[:], S2b, Pci, ALU.mult)
                P.stt(WaFv, prA[:], sg[:, 0:1], prB[:], ALU.mult, ALU.subtract, [K_], [K_])
                V(prC[:], S1b, Pci, ALU.mult)
                V(prD[:], S2b, Pcr, ALU.mult)
                P.stt(WbFv, prC[:], sg[:, 1:2], prD[:], ALU.mult, ALU.subtract, [K_], [K_])
                for g in range(8):
                    wk = ("w", d, g)
                    gg = ct * 8 + g
                    P.tr(pX[:, 256:384], XM[:, g, :], identf[:], [K_, "ident_f"], ["pXc"])
                    P.cp("act", Zw[d][g][:], pX[:, 256:384], ["pXc"], [wk])
                    P.cp("act", Zsw[d][g][:, 0:64], pX[:, 320:384], ["pXc"], [wk])
                    P.ts("dve", Zsw[d][g][:, 64:128], pX[:, 256:320], -1.0, None, ALU.mult, None, ["pXc"], [wk])
                    P.cp("pool", Wa[d][g][:], WaF[:, g, :], [K_], [wk])
                    P.cp("pool", Wb[d][g][:], WbF[:, g, :], [K_], [wk])
                    P.mm(pX[:, 384:512], XM[:, g, :], WaF[:, g, :], True, True, [K_], ["pXd"])
                    if d == 0:
                        P.ts("dve", M1acc[g][:], identf[:], dvec8[:, gg:gg + 1], None, ALU.mult, None, [K_, "ident_f"], [("M1", g)])
                    P.tt("dve", tmpM[:], pX[:, 384:512], nmask[:, d, :], ALU.mult, ["pXd", "nmask"], ["tmpM"])
                    P.tt("dve", M1acc[g][:], M1acc[g][:], tmpM[:], ALU.add, ["tmpM", ("M1", g)], [("M1", g)])
                    if d == 1:
                        P.cp("dve", M1b[g][:], M1acc[g][:], [("M1", g)], [("M1b", g)])
            P.barrier()
            for d in range(2):
                g0 = d * 32 + ct * 8
                for g in range(8):
                    dg = g0 + g
                    wk = ("r", g)
                    P.ts("dve", Rm[g][:], identf[:], p_["cL"][:, dg:dg + 1], None, ALU.mult, None, ["ident_f", K_], [wk])
                    P.stt(Rm[g][:], jswap[:], p_["sL"][:, dg:dg + 1], Rm[g][:], ALU.mult, ALU.add, ["jswap", K_, wk], [wk])
                    ab = gcount % 2
                    gcount += 1
                    P.ts("dve", angt[ab][:], jrow[:, 0:L], p_["th8"][:, dg:dg + 1], None, ALU.mult, None, ["jrow", K_], [("angt", ab)])
                    scL.run(P, angt[ab][:], ("angt", ab), sj[g][:], cj[g][:], ("tab", g))
                P.memset("dve", init[:], 0.0, ["init"])
                order = list(range(NC)) if d == 0 else list(range(NC - 1, -1, -1))
                its = [(ci, g) for ci in order for g in range(8)]

                def views(g):
                    if d == 0:
                        return cj[g][:], sj[g][:]
                    return cj[g][:, ::-1], sj[g][:, ::-1]

                def stage1(n):
                    ci, g = its[n]
                    blk = slice(ci * L, (ci + 1) * L)
                    zb, b4 = n % 2, n % NBUF
                    wk = ("w", d, g)
                    cjv, sjv = views(g)
                    P.mm(pZ[zb][:, 0:L], Zw[d][g][:], U[:, g, blk], True, True, [wk, ("U", g)], [("pZ", zb)])
                    P.mm(pZs[zb][:, 0:L], Zsw[d][g][:], U[:, g, blk], True, True, [wk, ("U", g)], [("pZs", zb)])
                    P.tt("dve", zh[b4][:], pZ[zb][:, 0:L], cjv, ALU.mult, [("pZ", zb), ("tab", g)], [("zh", b4)])
                    P.tt("dve", zh2[b4][:], pZs[zb][:, 0:L], sjv, ALU.mult, [("pZs", zb), ("tab", g)], [("zh2", b4)])
                    P.tt("pool", zh[b4][:], zh[b4][:], zh2[b4][:], ALU.add, [("zh", b4), ("zh2", b4)], [("zh", b4)])

                def stage2(n):
                    ci, g = its[n]
                    b4 = n % NBUF
                    dg = g0 + g
                    cjv, sjv = views(g)
                    rb = p_["rho8"][:, dg:dg + 1].to_broadcast([128, L])
                    if d == 0:
                        go, zi = G[b4][:], zh[b4][:]
                    else:
                        go, zi = G[b4][:, ::-1], zh[b4][:, ::-1]
                    ini = init[:, g:g + 1]
                    P.op("dve", (lambda go=go, zi=zi, rb=rb, ini=ini: (lambda e: e.tensor_tensor_scan(
                        out=go, data0=rb, data1=zi, initial=ini, op0=ALU.mult, op1=ALU.add)))(),
                        [("zh", b4), K_, ("init", g)], [("G", b4)])
                    P.tt("pool", P1[b4][:], G[b4][:], cjv, ALU.mult, [("G", b4), ("tab", g)], [("P1", b4)])
                    P.tt("dve", P2[b4][:], G[b4][:], sjv, ALU.mult, [("G", b4), ("tab", g)], [("P2", b4)])

                def stage3(n):
                    ci, g = its[n]
                    blk = slice(ci * L, (ci + 1) * L)
                    b4 = n % NBUF
                    yb = n % 2
                    wk = ("w", d, g)
                    last = G[b4][:, L - 1:L] if d == 0 else G[b4][:, 0:1]
                    if d == 0:
                        P.mm(pY[yb][:, 0:L], M1b[g][:], U[:, g, blk], True, False, [("M1b", g), ("U", g)], [("pY", yb)])
                    P.mm(pY[yb][:, 0:L], Wa[d][g][:], P1[b4][:], d == 1, False, [wk, ("P1", b4)], [("pY", yb)])
                    P.mm(pY[yb][:, 0:L], Wb[d][g][:], P2[b4][:], False, True, [wk, ("P2", b4)], [("pY", yb)])
                    P.mm(pC[:, g:g + 1], Rm[g][:], last, True, True, [("r", g), ("G", b4)], [("pC", g)])
                    P.cp("act", init[:, g:g + 1], pC[:, g:g + 1], [("pC", g)], [("init", g)])
                    if d == 0:
                        P.cp("act", Yacc[:, g, blk], pY[yb][:, 0:L], [("pY", yb)], [("Yacc", g, ci)])
                    else:
                        P.tt("dve", Yacc[:, g, blk], pY[yb][:, 0:L], Yacc[:, g, blk], ALU.add, [("pY", yb), ("Yacc", g, ci)], [("Yacc", g, ci)])

                NI = len(its)
                for n in range(NI + 2):
                    if n < NI:
                        stage1(n)
                    if 1 <= n <= NI:
                        stage2(n - 1)
                    if n >= 2:
                        stage3(n - 2)
            CPU_ = UB // L
            for ub in range(NUB):
                cols = slice(ub * UB, (ub + 1) * UB)
                for g in range(8):
                    P.act(ygU[:, g, cols], Yacc[:, g, cols], AF.Gelu, [("Yacc", g, ci_) for ci_ in range(ub * CPU_, (ub + 1) * CPU_)], ["ut"])
                for j in range(8):
                    ps = alt[acount % 2]
                    pk = ("alt", acount % 2)
                    acount += 1
                    for g in range(8):
                        P.mm(ps[:, 0:UB], selbig[:, j, 112 - 16 * g:240 - 16 * g], ygU[:, g, cols], g == 0, g == 7, ["ut", "selbig"], [pk])
                    P.cp("act" if j % 2 == 0 else "dve", ytile[:, slice(j, UB * 8, 8)], ps[:, 0:UB], [pk], ["ytile"])
                P.dma("sp", X.ygT[ct, :, ub * UB * 8:(ub + 1) * UB * 8], ytile[:], ["ytile"], [("ygT", ct, ub)])
        P.barrier()
        P.emit()


def phaseC(X, l, xsrc):
    nc, P, S = X.nc, X.P, X.S
    I = X.ins
    NB = S // 512
    with ExitStack() as es:
        def T(name, shape, dt=F32):
            return sbt(es, nc, "pc_" + name, shape, dt)
        stage = [T(f"stage{i}", [128, 2048]) for i in range(3)]
        gw = load_weight_bf16(X, es, "pc_gw", I["glu_w"][l], 4, 512, "gw", stage)
        wo = load_weight_bf16(X, es, "pc_wo", I["w_out"][l], 8, 1024, "wo", stage)
        gb = T("gb", [128, 4])
        gsn = T("gsn", [128, 4])
        P.dma("sp", gb[:], colvec(I["glu_b"][l, :], 4), [], ["gb"], slow=True)
        P.dma("sp", gsn[:], colvec(I["ssm_norm_g"][l, :], 4), [], ["gsn"], slow=True)
        zt = T("zt", [128, 8, 1], BF16)
        P.memset("dve", zt[:], 0.0, ["zt"])
        P.dma("sp", X.h2T[:, :, 0:1].rearrange("c p t -> p c t"), zt[:], ["zt"], ["h2z0"], slow=True)
        P.dma("sp", X.h2T[:, :, S + 1:S + 2].rearrange("c p t -> p c t"), zt[:], ["zt"], ["h2z1"], slow=True)
        mix = [T(f"mix{i}", [128, 8, 512], BF16) for i in range(2)]
        yg = [T(f"yg{i}", [128, 4, 512], BF16) for i in range(2)]
        sig = [T(f"sig{i}", [128, 512]) for i in range(2)]
        y2 = T("y2", [128, 4, 512])
        sq = [T(f"sq{i}", [128, 512]) for i in range(2)]
        rstd = T("rstd", [128, 512])
        xt = [T(f"xt{i}", [128, 1024]) for i in range(6)]
        tmp = [T(f"tmp{i}", [128, 1024]) for i in range(2)]
        junk = T("junk", [128, 1024], BF16)
        xn = [T(f"xn{i}", [128, 1024], BF16) for i in range(2)]
        h2 = [T(f"h2{i}", [128, 8, 128], BF16) for i in range(2)]
        ss = T("ss", [128, 8])
        rs = T("rs", [128, 8])
        pG = [pst(es, nc, f"pc_pG{i}", [128, 512]) for i in range(2)]
        pSS = pst(es, nc, "pc_pSS", [128, 512])
        pW = [pst(es, nc, f"pc_pW{i}", [128, 512]) for i in range(2)]
        pT = [pst(es, nc, f"pc_pT{i}", [128, 8, 128], BF16) for i in range(2)]
        ident = X.cst["ident_bf"]
        ones = X.cst["ones_f"]
        gm, mT, g1bc = X.gm2[l], X.modT[l], X.g1bc[l]
        st = {"g": 0, "w": 0}

        def stageG(nb):
            b2 = nb % 2
            tok = slice(nb * 512, (nb + 1) * 512)
            P.dma("sp", yg[b2][:], X.ygT[:, :, tok].rearrange("c p t -> p c t"), [], [("yg", b2)])
            P.dma("pool", mix[b2][:, 0:4, :], X.yaT[:, :, tok].rearrange("c p t -> p c t"), [], [("mixa", b2)])
            for m in range(4):
                gbf = st["g"] % 2
                st["g"] += 1
                for k in range(4):
                    P.mm(pG[gbf][:], gw[:, k, m * 128:(m + 1) * 128], yg[b2][:, k, :], k == 0, k == 3, [("gw", k), ("yg", b2)], [("pG", gbf)])
                P.act(sig[gbf][:], pG[gbf][:], AF.Sigmoid, [("pG", gbf), "gb"], [("sig", gbf)], bias=gb[:, m:m + 1])
                P.tt("dve", y2[:, m, :], yg[b2][:, m, :], sig[gbf][:], ALU.mult, [("yg", b2), ("sig", gbf)], [("y2", m)])
                P.act(sq[gbf][:], y2[:, m, :], AF.Square, [("y2", m)], [("sq", gbf)])
                P.mm(pSS[:], ones[:], sq[gbf][:], m == 0, m == 3, [("sq", gbf), "ones_f"], ["pSS"])
            P.act(rstd[:], pSS[:], AF.Sqrt, ["pSS"], ["rstd"], bias=X.epsc[:, 0:1], scale=1.0 / 512)
            P.recip(rstd[:], rstd[:], ["rstd"], ["rstd"])
            for m in range(4):
                P.stt(mix[b2][:, 4 + m, :], y2[:, m, :], gsn[:, m:m + 1], rstd[:], ALU.mult, ALU.mult, [("y2", m), "gsn", "rstd"], [("mixs", b2, m)])

        def stage1(nb, tl):
            b2 = nb % 2
            tn = nb * 4 + tl
            xb = tn % 6
            t0 = nb * 512 + tl * 128
            mk = [("mixa", b2)] + [("mixs", b2, m) for m in range(4)]
            P.dma("sp", xt[xb][:], xsrc[t0:t0 + 128, :], [], [("xt", xb)])
            for hf in range(2):
                wb = st["w"] % 2
                st["w"] += 1
                for k in range(8):
                    P.mm(pW[wb][:], mix[b2][:, k, tl * 128:(tl + 1) * 128], wo[:, k, hf * 512:(hf + 1) * 512], k == 0, k == 7,
                         mk + [("wo", k)], [("pW", wb)])
                P.tt("dve", tmp[tn % 2][:, hf * 512:(hf + 1) * 512], pW[wb][:], g1bc[:, hf * 512:(hf + 1) * 512], ALU.mult,
                     [("pW", wb), ("g1bc", l)], [("tmp", tn % 2, hf)])
                P.tt("pool", xt[xb][:, hf * 512:(hf + 1) * 512], tmp[tn % 2][:, hf * 512:(hf + 1) * 512], xt[xb][:, hf * 512:(hf + 1) * 512], ALU.add,
                     [("tmp", tn % 2, hf), ("xt", xb)], [("xt", xb)])
            P.dma("pool", X.x1[t0:t0 + 128, :], xt[xb][:], [("xt", xb)], [("x1", t0)])

        def stage2(nb, tl):
            tn = nb * 4 + tl
            xb = tn % 6
            x2 = tn % 2
            sc = tn % 8
            t0 = nb * 512 + tl * 128
            P.act(junk[:], xt[xb][:], AF.Square, [("xt", xb)], ["junk", ("ss", sc)], accum=ss[:, sc:sc + 1])
            P.act(rs[:, sc:sc + 1], ss[:, sc:sc + 1], AF.Sqrt, [("ss", sc)], [("rs", sc)], bias=X.epsc[:, 0:1], scale=1.0 / D)
            P.recip(rs[:, sc:sc + 1], rs[:, sc:sc + 1], [("rs", sc)], [("rs", sc)])
            P.ts("dve", xn[x2][:], xt[xb][:], rs[:, sc:sc + 1], None, ALU.mult, None, [("xt", xb), ("rs", sc)], [("xn", x2)])
            for c in range(8):
                P.tr(pT[x2][:, c, :], xn[x2][:, c * 128:(c + 1) * 128], ident[:], [("xn", x2), "ident"], [("pT", x2)])
            for c in range(8):
                if x2 == 0:
                    P.act(h2[x2][:, c, :], pT[x2][:, c, :], AF.Identity, [("pT", x2), ("gm2", l), ("modT", l)], [("h2", x2)],
                          bias=mT[:, 24 + c:25 + c], scale=gm[:, c:c + 1])
                else:
                    P.ts("dve", h2[x2][:, c, :], pT[x2][:, c, :], gm[:, c:c + 1], mT[:, 24 + c:25 + c], ALU.mult, ALU.add,
                         [("pT", x2), ("gm2", l), ("modT", l)], [("h2", x2)])
            P.dma("sp", X.h2T[:, :, 1 + t0:1 + t0 + 128].rearrange("c p t -> p c t"), h2[x2][:], [("h2", x2)], [("h2T", t0)])

        tiles = [(nb, tl) for nb in range(NB) for tl in range(4)]
        SK = 3
        stageG(0)
        for i, (nb, tl) in enumerate(tiles):
            stage1(nb, tl)
            if tl == 1 and nb + 1 < NB:
                stageG(nb + 1)
            if i >= SK:
                stage2(*tiles[i - SK])
        for i in range(max(0, len(tiles) - SK), len(tiles)):
            stage2(*tiles[i])
        P.barrier()
        P.emit()


def phaseF(X, l, xdst):
    nc, P, S = X.nc, X.P, X.S
    I = X.ins
    NB = S // 512
    HP = NFF // 2
    for hp in range(2):
        with ExitStack() as es:
            def T(name, shape, dt=F32):
                return sbt(es, nc, f"pf{hp}_" + name, shape, dt)
            stage = [T(f"stage{i}", [128, 2048]) for i in range(3)]
            wu = T("wu", [128, 8, 2 * HP * 128], BF16)
            for k in range(8):
                for part in range(2):
                    c0 = part * DFF + hp * HP * 128
                    i = X.stage_i
                    X.stage_i += 1
                    st = stage[i % 3]
                    P.dma("sp" if i % 2 == 0 else "pool", st[:, 0:HP * 128], I["w_up"][l, k * 128:(k + 1) * 128, c0:c0 + HP * 128], [], [("stage", i % 3)])
                    P.cp(("dve", "pool", "act")[i % 3], wu[:, k, part * HP * 128:(part + 1) * HP * 128], st[:, 0:HP * 128], [("stage", i % 3)], [("wu", k)])
            wd = load_weight_bf16(X, es, f"pf{hp}_wd", I["w_down"][l, hp * HP * 128:(hp + 1) * HP * 128, :], HP, 1024, "wd", stage)
            cw = T("cw", [128, 3, 44])
            cb = T("cb", [128, 44])
            for c0 in range(0, 44, 11):
                for t in range(3):
                    P.dma("sp", cw[:, t, c0:c0 + 11], colvec(I["conv_w"][l, t, :], 44)[:, c0:c0 + 11], [], ["cw"], slow=True)
                P.dma("sp", cb[:, c0:c0 + 11], colvec(I["conv_b"][l, :], 44)[:, c0:c0 + 11], [], ["cb"], slow=True)
            h2 = [T(f"h2{i}", [128, 8, 512], BF16) for i in range(2)]
            cva = [T(f"cva{i}", [128, 512]) for i in range(2)]
            cvg = [T(f"cvg{i}", [128, 512]) for i in range(2)]
            sg = [T(f"sg{i}", [128, 512]) for i in range(2)]
            hid = [T(f"hid{i}", [128, HP, 512], BF16) for i in range(2)]
            xt = [T(f"xt{i}", [128, 1024]) for i in range(3)]
            tmp = [T(f"tmp{i}", [128, 1024]) for i in range(2)]
            pA = [pst(es, nc, f"pf{hp}_pA{i}", [128, 512]) for i in range(2)]
            pGt = [pst(es, nc, f"pf{hp}_pG{i}", [128, 512]) for i in range(2)]
            pW = [pst(es, nc, f"pf{hp}_pW{i}", [128, 512]) for i in range(2)]
            g2bc = X.g2bc[l]
            xin = X.x1 if hp == 0 else xdst
            icount = 0
            wcount = 0
            tcount = 0
            BT = 510
            blocks = [(t0, min(BT, S - t0)) for t0 in range(0, S, BT)]
            def load_h2(nb_):
                t0_, nt_ = blocks[nb_]
                P.dma("sp", h2[nb_ % 2][:, :, 0:nt_ + 2], X.h2T[:, :, t0_:t0_ + nt_ + 2].rearrange("c p t -> p c t"), [], [("h2", nb_ % 2)])

            load_h2(0)
            for nb, (t0, nt) in enumerate(blocks):
                b2 = nb % 2
                N = nt + 2
                for i in range(HP):
                    ib = icount % 2
                    icount += 1
                    for part, (pp, cv) in enumerate(((pA[ib], cva[ib]), (pGt[ib], cvg[ib]))):
                        col = part * 22 + hp * HP + i
                        wc = slice(part * HP * 128 + i * 128, part * HP * 128 + (i + 1) * 128)
                        for k in range(8):
                            P.mm(pp[:, 0:N], wu[:, k, wc], h2[b2][:, k, 0:N], k == 0, k == 7, [("wu", k), ("h2", b2)], [("pp", ib, part)])
                        ck = ("cv", ib, part)
                        P.act(cv[:, 0:nt], pp[:, 1:nt + 1], AF.Identity, [("pp", ib, part), "cw", "cb"], [ck], bias=cb[:, col:col + 1], scale=cw[:, 1, col:col + 1])
                        P.stt(cv[:, 0:nt], pp[:, 0:nt], cw[:, 0, col:col + 1], cv[:, 0:nt], ALU.mult, ALU.add, [("pp", ib, part), "cw", ck], [ck])
                        P.stt(cv[:, 0:nt], pp[:, 2:nt + 2], cw[:, 2, col:col + 1], cv[:, 0:nt], ALU.mult, ALU.add, [("pp", ib, part), "cw", ck], [ck])
                    P.act(sg[ib][:, 0:nt], cvg[ib][:, 0:nt], AF.Silu, [("cv", ib, 1)], [("sg", ib)])
                    P.tt("pool", hid[b2][:, i, 0:nt], sg[ib][:, 0:nt], cva[ib][:, 0:nt], ALU.mult, [("sg", ib), ("cv", ib, 0)], [("hid", b2, i)])
                hk = [("hid", b2, i) for i in range(HP)]
                if nb + 1 < len(blocks):
                    load_h2(nb + 1)
                for tl in range((nt + 127) // 128):
                    m = min(128, nt - tl * 128)
                    r0 = t0 + tl * 128
                    xb = tcount % 3
                    tb_ = tcount % 2
                    tcount += 1
                    P.dma("sp", xt[xb][0:m, :], xin[r0:r0 + m, :], [("xd", r0)], [("xt", xb)])
                    for hf in range(2):
                        wb = wcount % 2
                        wcount += 1
                        for i in range(HP):
                            P.mm(pW[wb][0:m, :], hid[b2][:, i, tl * 128:tl * 128 + m], wd[:, i, hf * 512:(hf + 1) * 512], i == 0, i == HP - 1,
                                 hk + [("wd", i)], [("pW", wb)])
                        P.tt("dve", tmp[tb_][0:m, hf * 512:(hf + 1) * 512], pW[wb][0:m, :], g2bc[0:m, hf * 512:(hf + 1) * 512], ALU.mult,
                             [("pW", wb), ("g2bc", l)], [("tmp", tb_, hf)])
                        P.tt("pool", xt[xb][0:m, hf * 512:(hf + 1) * 512], tmp[tb_][0:m, hf * 512:(hf + 1) * 512], xt[xb][0:m, hf * 512:(hf + 1) * 512], ALU.add,
                             [("tmp", tb_, hf), ("xt", xb)], [("xt", xb)])
                    P.dma("pool", xdst[r0:r0 + m, :], xt[xb][0:m, :], [("xt", xb)], [("xd", r0)])
            P.barrier()
            P.emit()


def build(S, debug=None, nlayers=DEPTH, phases=None, cut=0):
    nc = bass.Bass("TRN2", target_bir_lowering=False)
    X = Ctx()
    X.cut = cut
    import os
    X.evac = os.environ.get('EVAC', 'both')
    X.rowtile = os.environ.get('ROWTILE', '0') == '1'
    X.s8 = os.environ.get('S8', '1') == '1'
    X.nc, X.S = nc, S
    X.stage_i = 0
    dbg = set(debug or ())

    def din(name, shape, dt=F32):
        return nc.dram_tensor(name, list(shape), dt, kind="ExternalInput").ap()

    def dscr(name, shape, dt):
        kind = "ExternalOutput" if name in dbg else "Internal"
        return nc.dram_tensor(name, list(shape), dt, kind=kind).ap()

    X.ins = {"x": din("x", [S, D]), "pos": din("pos", [S], I32)}
    for k, shp in IN_SHAPES.items():
        X.ins[k] = din(k, shp)
    cin = {k: din("cst_" + k, shp, dt) for k, (shp, dt) in CONST_SHAPES.items()}
    X.out = nc.dram_tensor("out", [S, D], F32, kind="ExternalOutput").ap()
    X.modrow = dscr("modrow", [2, 6144], F32)
    X.cosT = dscr("cosT", [128, S], F32)
    X.sinT = dscr("sinT", [128, S], F32)
    X.qT = dscr("qT", [4, 128, S], BF16)
    X.kT = dscr("kT", [4, 128, S], BF16)
    X.vv = dscr("vv", [S, 512], BF16)
    X.uT = dscr("uT", [4, 128, S], BF16)
    X.ygT = dscr("ygT", [4, 128, S], BF16)
    X.yaT = dscr("yaT", [4, 128, S], BF16)
    X.x1 = dscr("x1", [S, D], F32)
    X.h2T = dscr("h2T", [8, 128, S + 2], BF16)
    X.xmid = dscr("xmid", [S, D], F32)
    with ExitStack() as es:
        P = Prog(nc, es)
        X.P = P
        X.cst = {}
        for k, (shp, dt) in CONST_SHAPES.items():
            X.cst[k] = sbt(es, nc, "c_" + k, shp, dt)
            P.dma("sp", X.cst[k][:], cin[k], [], [k])
        X.epsc = sbt(es, nc, "c_eps", [128, 1], F32)
        P.memset("dve", X.epsc[:], EPS, ["epsc"])
        X.modT = [sbt(es, nc, f"modT{l}", [128, 48], F32) for l in range(DEPTH)]
        X.g1bc = [sbt(es, nc, f"g1bc{l}", [128, 1024], F32) for l in range(DEPTH)]
        X.g2bc = [sbt(es, nc, f"g2bc{l}", [128, 1024], F32) for l in range(DEPTH)]
        X.gm1 = [sbt(es, nc, f"gm1{l}", [128, 8], F32) for l in range(DEPTH)]
        X.gm2 = [sbt(es, nc, f"gm2{l}", [128, 8], F32) for l in range(DEPTH)]
        P.barrier()
        phases = phases or ("0", "A", "B", "S", "C", "F")
        if "0" in phases:
            phase0(X)
        for l in range(nlayers):
            xsrc = X.ins["x"] if l == 0 else X.xmid
            xdst = X.xmid if l < DEPTH - 1 else X.out
            if "A" in phases:
                phaseA(X, l, xsrc)
            if "B" in phases:
                phaseB(X, l)
            if "S" in phases:
                if X.s8:
                    phaseS8(X, l)
                else:
                    phaseS(X, l)
            if "C" in phases:
                phaseC(X, l, xsrc)
            if "F" in phases:
                phaseF(X, l, xdst)
        P.barrier()
        P.emit(final=True)
    X.ninstr = P.ninstr
    return nc, X


_CACHE = {}


def kernel(**inputs):
    x = np.asarray(inputs["x"])
    B, S, _ = x.shape
    if S not in _CACHE:
        _CACHE[S] = build(S)[0]
    nc = _CACHE[S]
    consts = make_consts()
    n_cores = 8
    in_maps = []
    for i in range(n_cores):
        b = i % B
        m = {"x": np.ascontiguousarray(x[b]).astype(np.float32),
             "pos": np.ascontiguousarray(np.asarray(inputs["positions"])[b]).astype(np.int32),
             "c": np.ascontiguousarray(np.asarray(inputs["c"])[b]).astype(np.float32)}
        for k in IN_SHAPES:
            if k != "c":
                m[k] = np.ascontiguousarray(np.asarray(inputs[k])).astype(np.float32)
        for k, v in consts.items():
            m["cst_" + k] = v
        in_maps.append(m)
    res = run_bass_kernel_spmd(nc, in_maps, core_ids=list(range(n_cores)))
    out = np.stack([np.asarray(res.results[b]["out"]) for b in range(B)], axis=0)
    return out.astype(np.float32)
```
